# Optimizing a Trainium2 kernel written in Bass

```python
import math
import jax, jax.numpy as jnp
from jax import lax
import numpy as np

D_MODEL = 1024
BATCH = 8
SEQ = 4096
DEPTH = 1

N_META = 16
BLOCK_Q = 128
EPS = 1e-6
NEG_INF = -1e30

SSM_WIDTH = D_MODEL // 2
SSM_GROUP = 16
SSM_GROUPS = SSM_WIDTH // SSM_GROUP
SSM_STATE = 64
DT_MIN = 1e-3
DT_MAX = 1e-1

ATT_HEADS = 4
QK_DIM = 64
V_DIM = 2 * QK_DIM
ATT_WIDTH = ATT_HEADS * V_DIM
QK_COLS = ATT_HEADS * 2 * QK_DIM

IN_COLS = SSM_WIDTH + 2 * QK_COLS + ATT_WIDTH + 2 * D_MODEL
SPLITS = (SSM_WIDTH,
          SSM_WIDTH + QK_COLS,
          SSM_WIDTH + 2 * QK_COLS,
          SSM_WIDTH + 2 * QK_COLS + ATT_WIDTH,
          SSM_WIDTH + 2 * QK_COLS + ATT_WIDTH + D_MODEL)

D_FF = ((8 * D_MODEL // 3 + 127) // 128) * 128
CONV_W = 3

kernel_name = 'hybrid_s5_diffattn_convffn_block'


def rmsnorm(x, g):
    xf = x.astype(jnp.float32)
    r = lax.rsqrt(jnp.mean(xf * xf, axis=-1, keepdims=True) + EPS)
    return (xf * r).astype(x.dtype) * g


def s5_mixer(u, a_re, a_im, log_dt, b_re, b_im, c_re, c_im, d, glu_w, glu_b):
    bsz, seq_len, _ = u.shape
    ug = u.astype(jnp.float32).reshape(bsz, seq_len, SSM_GROUPS, SSM_GROUP)
    a_re = a_re.astype(jnp.float32); a_im = a_im.astype(jnp.float32)
    dt = jnp.exp(log_dt.astype(jnp.float32))[:, None]
    mag = jnp.exp(a_re * dt)
    lb_re = mag * jnp.cos(a_im * dt)
    lb_im = mag * jnp.sin(a_im * dt)
    den = a_re * a_re + a_im * a_im
    n_re = lb_re - 1.0
    f_re = (n_re * a_re + lb_im * a_im) / den
    f_im = (lb_im * a_re - n_re * a_im) / den
    b_re = b_re.astype(jnp.float32); b_im = b_im.astype(jnp.float32)
    bb_re = f_re[..., None] * b_re - f_im[..., None] * b_im
    bb_im = f_re[..., None] * b_im + f_im[..., None] * b_re
    x_re = jnp.einsum('blgc,gpc->blgp', ug, bb_re)
    x_im = jnp.einsum('blgc,gpc->blgp', ug, bb_im)
    shp = (1, seq_len, SSM_GROUPS, SSM_STATE)
    a_re_t = jnp.broadcast_to(lb_re, shp)
    a_im_t = jnp.broadcast_to(lb_im, shp)

    def combine(left, right):
        ar_l, ai_l, br_l, bi_l = left
        ar_r, ai_r, br_r, bi_r = right
        return (ar_r * ar_l - ai_r * ai_l,
                ar_r * ai_l + ai_r * ar_l,
                ar_r * br_l - ai_r * bi_l + br_r,
                ar_r * bi_l + ai_r * br_l + bi_r)

    _, _, s_re, s_im = lax.associative_scan(combine, (a_re_t, a_im_t, x_re, x_im), axis=1)
    y = (jnp.einsum('blgp,gcp->blgc', s_re, c_re.astype(jnp.float32))
         - jnp.einsum('blgp,gcp->blgc', s_im, c_im.astype(jnp.float32))
         + d.astype(jnp.float32) * ug)
    y = jax.nn.gelu(y.reshape(bsz, seq_len, SSM_WIDTH)).astype(u.dtype)
    return y * jax.nn.sigmoid(y @ glu_w + glu_b)


def diff_attention(q, k, v, lam, slopes):
    bsz, seq_len = q.shape[:2]
    nb = seq_len // BLOCK_Q
    qb = jnp.moveaxis(q.reshape(bsz, nb, BLOCK_Q, ATT_HEADS, 2, QK_DIM), 1, 0)
    starts = jnp.arange(nb, dtype=jnp.int32) * BLOCK_Q
    kpos = jnp.arange(seq_len, dtype=jnp.int32)

    def one_block(args):
        q_blk, start = args
        s = jnp.einsum('bqhjd,bkhjd->bhjqk', q_blk, k,
                       preferred_element_type=jnp.float32)
        dist = (start + jnp.arange(BLOCK_Q, dtype=jnp.int32))[:, None] - kpos[None, :]
        s = s - slopes[None, :, None, None, None] * dist.astype(jnp.float32)[None, None, None]
        s = jnp.where(dist[None, None, None] >= 0, s, NEG_INF)
        p = jax.nn.softmax(s, axis=-1)
        w = p[:, :, 0] - lam * p[:, :, 1]
        return jnp.einsum('bhqk,bkhe->bqhe', w.astype(v.dtype), v)

    o = lax.map(one_block, (qb, starts))
    return jnp.moveaxis(o, 0, 1).reshape(bsz, seq_len, ATT_HEADS, V_DIM)


def conv_glu_ffn(h, w_up, conv_w, conv_b, w_down):
    seq_len = h.shape[1]
    up = h @ w_up
    a, b = up[..., :D_FF], up[..., D_FF:]
    a_pad = jnp.pad(a, ((0, 0), (CONV_W - 1, 0), (0, 0)))
    c = (a_pad[:, 0:seq_len] * conv_w[0] + a_pad[:, 1:seq_len + 1] * conv_w[1]
         + a_pad[:, 2:seq_len + 2] * conv_w[2] + conv_b)
    return (jax.nn.gelu(c) * b) @ w_down


def setup_inputs(seed: int = 0) -> dict:
    key = jax.random.key(seed)
    ks = jax.random.split(key, 32)
    f32 = jnp.float32
    nrm = lambda k, shp, s: s * jax.random.normal(k, shp, f32)
    L_ = DEPTH
    return {
        'x': jax.random.normal(ks[0], (BATCH, SEQ, D_MODEL), f32),
        'meta_tokens': nrm(ks[1], (N_META, D_MODEL), 1.0),
        'norm1_g': 1.0 + nrm(ks[2], (L_, D_MODEL), 0.02),
        'w_in': nrm(ks[3], (L_, D_MODEL, IN_COLS), D_MODEL ** -0.5),
        'ssm_a_re': -0.5 + nrm(ks[4], (L_, SSM_GROUPS, SSM_STATE), 0.01),
        'ssm_a_im': math.pi * jnp.arange(SSM_STATE, dtype=f32)[None, None, :]
                    + nrm(ks[5], (L_, SSM_GROUPS, SSM_STATE), 0.01),
        'ssm_log_dt': jax.random.uniform(ks[6], (L_, SSM_GROUPS), f32,
                                         math.log(DT_MIN), math.log(DT_MAX)),
        'ssm_b_re': nrm(ks[7], (L_, SSM_GROUPS, SSM_STATE, SSM_GROUP), (2 * SSM_GROUP) ** -0.5),
        'ssm_b_im': nrm(ks[8], (L_, SSM_GROUPS, SSM_STATE, SSM_GROUP), (2 * SSM_GROUP) ** -0.5),
        'ssm_c_re': nrm(ks[9], (L_, SSM_GROUPS, SSM_GROUP, SSM_STATE), (2 * SSM_STATE) ** -0.5),
        'ssm_c_im': nrm(ks[10], (L_, SSM_GROUPS, SSM_GROUP, SSM_STATE), (2 * SSM_STATE) ** -0.5),
        'ssm_d': nrm(ks[11], (L_, SSM_GROUPS, SSM_GROUP), 1.0),
        'ssm_glu_w': nrm(ks[12], (L_, SSM_WIDTH, SSM_WIDTH), SSM_WIDTH ** -0.5),
        'ssm_glu_b': nrm(ks[13], (L_, SSM_WIDTH), 0.01),
        'q_norm_g': 1.0 + nrm(ks[14], (L_, QK_DIM), 0.02),
        'k_norm_g': 1.0 + nrm(ks[15], (L_, QK_DIM), 0.02),
        'lam_q1': nrm(ks[16], (L_, QK_DIM), 0.1),
        'lam_k1': nrm(ks[17], (L_, QK_DIM), 0.1),
        'lam_q2': nrm(ks[18], (L_, QK_DIM), 0.1),
        'lam_k2': nrm(ks[19], (L_, QK_DIM), 0.1),
        'subln_g': 1.0 + nrm(ks[20], (L_, V_DIM), 0.02),
        'w_ssm_out': nrm(ks[21], (L_, SSM_WIDTH, D_MODEL), SSM_WIDTH ** -0.5),
        'w_att_out': nrm(ks[22], (L_, ATT_WIDTH, D_MODEL), ATT_WIDTH ** -0.5),
        'w_o': nrm(ks[23], (L_, D_MODEL, D_MODEL), D_MODEL ** -0.5),
        'norm2_g': 1.0 + nrm(ks[24], (L_, D_MODEL), 0.02),
        'w_up': nrm(ks[25], (L_, D_MODEL, 2 * D_FF), D_MODEL ** -0.5),
        'conv_w': nrm(ks[26], (L_, CONV_W, D_FF), CONV_W ** -0.5),
        'conv_b': nrm(ks[27], (L_, D_FF), 0.01),
        'w_down': nrm(ks[28], (L_, D_FF, D_MODEL), D_FF ** -0.5),
    }


def reference(x, meta_tokens, norm1_g, w_in, ssm_a_re, ssm_a_im, ssm_log_dt, ssm_b_re, ssm_b_im,
              ssm_c_re, ssm_c_im, ssm_d, ssm_glu_w, ssm_glu_b, q_norm_g, k_norm_g,
              lam_q1, lam_k1, lam_q2, lam_k2, subln_g, w_ssm_out, w_att_out, w_o,
              norm2_g, w_up, conv_w, conv_b, w_down):
    bsz, seq, _ = x.shape
    seq_len = seq + N_META
    seq_pad = -(-seq_len // BLOCK_Q) * BLOCK_Q
    meta = jnp.broadcast_to(meta_tokens[None].astype(x.dtype), (bsz, N_META, D_MODEL))
    h = jnp.concatenate([meta, x, jnp.zeros((bsz, seq_pad - seq_len, D_MODEL), x.dtype)], axis=1)
    slopes = 2.0 ** (-8.0 * jnp.arange(1, ATT_HEADS + 1, dtype=jnp.float32) / ATT_HEADS)
    q_scale = QK_DIM ** -0.5

    for l in range(DEPTH):
        lam_init = 0.8 - 0.6 * math.exp(-0.3 * l)
        hn = rmsnorm(h, norm1_g[l])
        proj = hn @ w_in[l]
        u, q, k, v, g_ssm, g_att = jnp.split(proj, SPLITS, axis=-1)
        y_ssm = s5_mixer(u, ssm_a_re[l], ssm_a_im[l], ssm_log_dt[l], ssm_b_re[l], ssm_b_im[l],
                         ssm_c_re[l], ssm_c_im[l], ssm_d[l], ssm_glu_w[l], ssm_glu_b[l])
        q = rmsnorm(q.reshape(bsz, seq_pad, ATT_HEADS, 2, QK_DIM), q_norm_g[l]) * q_scale
        k = rmsnorm(k.reshape(bsz, seq_pad, ATT_HEADS, 2, QK_DIM), k_norm_g[l])
        v = v.reshape(bsz, seq_pad, ATT_HEADS, V_DIM)
        lam = (jnp.exp(jnp.sum(lam_q1[l] * lam_k1[l]).astype(jnp.float32))
               - jnp.exp(jnp.sum(lam_q2[l] * lam_k2[l]).astype(jnp.float32)) + lam_init)
        o = diff_attention(q, k, v, lam, slopes)
        y_att = (rmsnorm(o, subln_g[l]) * (1.0 - lam_init)).reshape(bsz, seq_pad, ATT_WIDTH)
        mixed = (jax.nn.sigmoid(g_ssm) * (y_ssm @ w_ssm_out[l])
                 + jax.nn.sigmoid(g_att) * (y_att @ w_att_out[l]))
        h = h + mixed @ w_o[l]
        h = h + conv_glu_ffn(rmsnorm(h, norm2_g[l]), w_up[l], conv_w[l], conv_b[l], w_down[l])

    return h[:, N_META:N_META + seq]
```

```python
import contextlib
import math
import numpy as np
import concourse.bass as bass
import concourse.mybir as mybir
from concourse.bass_utils import run_bass_kernel_spmd

F32 = mybir.dt.float32
BF16 = mybir.dt.bfloat16
AF = mybir.ActivationFunctionType
ALU = mybir.AluOpType
AX = mybir.AxisListType

D = 1024
SEQ = 4096
NMETA = 16
LP = 4224
NBLK = LP // 128
EPS = 1e-6
DFF = 2816
NFT = DFF // 128


class Buf:
    __slots__ = ("name", "last_w", "readers")

    def __init__(self, name):
        self.name = name
        self.last_w = None
        self.readers = {}


class Prog:
    ENGS = ("pe", "dve", "act", "pool", "sp")

    def __init__(self, nc, stack, n_dma_sems=20):
        self.nc = nc
        self.sems = {}
        for e in self.ENGS:
            self.sems[e] = stack.enter_context(nc.semaphore("s_" + e))
        self.dma_keys = []
        for i in range(n_dma_sems):
            k = "dma%d" % i
            self.sems[k] = stack.enter_context(nc.semaphore("s_" + k))
            self.dma_keys.append(k)
        self.cnt = {k: 0 for k in self.sems}
        self.seen = {e: {} for e in self.ENGS}
        self.lists = {e: [] for e in self.ENGS}
        self.dma_rr_map = {}
        self.nbuf = 0

    def buf(self, name=None):
        self.nbuf += 1
        return Buf(name or ("b%d" % self.nbuf))

    def bufs(self, n, name="b"):
        return [self.buf("%s%d" % (name, i)) for i in range(n)]

    def _wait(self, eng, key, val):
        if val <= 0:
            return
        if self.seen[eng].get(key, 0) >= val:
            return
        self.seen[eng][key] = val
        self.lists[eng].append(("wait", key, val))

    def _deps(self, eng, reads, writes):
        for b in reads:
            if b.last_w is not None:
                self._wait(eng, *b.last_w)
        for b in writes:
            if b.last_w is not None and not (b.last_w[0] == eng == "pe"):
                self._wait(eng, *b.last_w)
            for k, v in b.readers.items():
                if not (k == eng == "pe"):
                    self._wait(eng, k, v)

    def op(self, eng, fn, reads=(), writes=(), inc=True):
        self._deps(eng, reads, writes)
        if inc:
            self.cnt[eng] += 1
            idx = self.cnt[eng]
            self.lists[eng].append(("op", fn, eng, 1))
        else:
            idx = self.cnt[eng] + 1
            self.lists[eng].append(("op", fn, eng, 0))
        for b in reads:
            b.readers[eng] = idx
        for b in writes:
            b.last_w = (eng, idx)
            b.readers = {}
        return idx

    def dma(self, fn, reads=(), writes=(), eng="sp"):
        half = len(self.dma_keys) // 2
        pool_keys = self.dma_keys[:half] if eng == "pool" else self.dma_keys[half:]
        rr = self.dma_rr_map.get(eng == "pool", 0)
        self.dma_rr_map[eng == "pool"] = rr + 1
        key = pool_keys[rr % len(pool_keys)]
        self._deps(eng, reads, writes)
        self._wait(eng, key, self.cnt[key])
        self.cnt[key] += 16
        val = self.cnt[key]
        self.lists[eng].append(("op", fn, key, 16))
        for b in reads:
            b.readers[key] = val
        for b in writes:
            b.last_w = (key, val)
            b.readers = {}
        return key, val

    def barrier(self):
        for e in self.ENGS:
            for k, v in self.cnt.items():
                if k != e:
                    self._wait(e, k, v)

    def finish(self, final_waits):
        for k, v in final_waits:
            self._wait("sp", k, v)
        nc = self.nc
        handles = dict(pe=nc.tensor, dve=nc.vector, act=nc.scalar, pool=nc.gpsimd, sp=nc.sync)
        sems = self.sems
        lists = self.lists

        def replay(name):
            eng = handles[name]
            for it in lists[name]:
                if it[0] == "wait":
                    eng.wait_ge(sems[it[1]], it[2])
                else:
                    ins = it[1](eng)
                    if it[3]:
                        ins.then_inc(sems[it[2]], it[3])

        with nc.Block() as block:
            @block.sync
            def _(e):
                replay("sp")

            @block.tensor
            def _(e):
                replay("pe")

            @block.vector
            def _(e):
                replay("dve")

            @block.scalar
            def _(e):
                replay("act")

            @block.gpsimd
            def _(e):
                replay("pool")


TB = 384
GK = math.sqrt(2.0 / math.pi)
GC = 0.044715


class Ctx:
    pass


_PHASE_ID = [0]


def _common_consts(nc, P, st, C):
    _PHASE_ID[0] += 1
    tag = "_p%d" % _PHASE_ID[0]

    def sb(name, shape, dt):
        return st.enter_context(nc.sbuf_tensor(name + tag, shape, dt))
    C.sb = sb
    C.ident = sb("ident", [128, 128], BF16)
    C.b_ident = P.buf("ident")
    P.op("pool", lambda e: e.memset(C.ident[:], 1.0), writes=[C.b_ident])
    P.op("pool", lambda e: e.affine_select(out=C.ident[:], in_=C.ident[:], pattern=[[-1, 128]],
                                           compare_op=ALU.is_equal, fill=0.0, base=0, channel_multiplier=1),
         reads=[C.b_ident], writes=[C.b_ident])
    C.epsc = sb("epsc", [128, 1], F32)
    C.b_eps = P.buf("eps")
    P.op("pool", lambda e: e.memset(C.epsc[:], EPS), writes=[C.b_eps])


def load_weight_bf16(nc, P, dst, dst_buf, src2d, kt_n, ncols, col0=0, scale_col=None, extra_scale=None):
    for kt in range(kt_n):
        c = 0
        while c < ncols:
            w = min(2048, ncols - c)
            P.dma(lambda e, kt=kt, c=c, w=w: e.dma_start(
                out=dst[:, kt, c:c + w], in_=src2d[kt * 128:(kt + 1) * 128, col0 + c:col0 + c + w]),
                writes=[dst_buf], eng="pool")
            c += w


def norm_tile(nc, P, C, src_rows, tile_i, nblk, xs, xs_b, hnT, hnT_b, ps_t, ps_t_b, wk, blk0=None, col0=0):
    for j in range(nblk):
        r0 = ((tile_i * nblk if blk0 is None else blk0) + j) * 128
        P.dma(lambda e, j=j, r0=r0: e.dma_start(out=xs[:, j, :], in_=src_rows[r0:r0 + 128, :]), writes=[xs_b[j]])
    for j in range(nblk):
        P.op("act", lambda e, j=j: e.activation(out=wk.junk[:], in_=xs[:, j, :], func=AF.Square,
                                                accum_out=wk.ss[:, j:j + 1]),
             reads=[xs_b[j]], writes=[wk.junk_b, wk.ss_b[j]])
    P.op("act", lambda e: e.activation(out=wk.sd[:, 0:nblk], in_=wk.ss[:, 0:nblk], func=AF.Sqrt,
                                       bias=C.epsc[:], scale=1.0 / D),
         reads=list(wk.ss_b[:nblk]) + [C.b_eps], writes=[wk.sd_b])
    P.op("dve", lambda e: e.reciprocal(out=wk.rstd[:, 0:nblk], in_=wk.sd[:, 0:nblk]), reads=[wk.sd_b], writes=[wk.rstd_b])
    for j in range(nblk):
        P.op("dve", lambda e, j=j: e.tensor_scalar(out=wk.hn[:], in0=xs[:, j, :], scalar1=wk.rstd[:, j:j + 1],
                                                   scalar2=None, op0=ALU.mult),
             reads=[xs_b[j], wk.rstd_b], writes=[wk.hn_b])
        for kt in range(8):
            P.op("pe", lambda e, kt=kt: e.transpose(out=ps_t[:, kt, :], in_=wk.hn[:, kt * 128:(kt + 1) * 128],
                                                    identity=C.ident[:]),
                 reads=[wk.hn_b, C.b_ident], writes=[ps_t_b])
        P.op("act", lambda e, j=j: e.copy(out=hnT[:, :, col0 + j * 128:col0 + (j + 1) * 128], in_=ps_t[:, :, :]),
             reads=[ps_t_b], writes=[hnT_b])


def norm_block_stats(nc, P, C, src_rows, r0, j, xs, xs_b, wk, load=True):
    hn = wk.hn2[j % 2]; hn_b = wk.hn2_b[j % 2]
    if load:
        P.dma(lambda e: e.dma_start(out=xs[:, j, :], in_=src_rows[r0:r0 + 128, :]), writes=[xs_b[j]])
    P.op("act", lambda e: e.activation(out=hn[:], in_=xs[:, j, :], func=AF.Square, accum_out=wk.ss[:, j:j + 1]),
         reads=[xs_b[j]], writes=[hn_b, wk.ss_b[j]])
    P.op("act", lambda e: e.activation(out=wk.sd[:, j:j + 1], in_=wk.ss[:, j:j + 1], func=AF.Sqrt, bias=C.epsc[:], scale=1.0 / D),
         reads=[wk.ss_b[j], C.b_eps], writes=[wk.ss_b[j]])
    P.op("dve", lambda e: e.reciprocal(out=wk.rstd[:, j:j + 1], in_=wk.sd[:, j:j + 1]), reads=[wk.ss_b[j]], writes=[wk.ss_b[j]])
    P.op("dve", lambda e: e.tensor_scalar(out=hn[:], in0=xs[:, j, :], scalar1=wk.rstd[:, j:j + 1], scalar2=None, op0=ALU.mult),
         reads=[xs_b[j], wk.ss_b[j]], writes=[hn_b])


def norm_block_T(nc, P, C, j, hnT, hnT_b, ps_t, ps_t_b, wk):
    hn = wk.hn2[j % 2]; hn_b = wk.hn2_b[j % 2]
    for kt in range(8):
        P.op("pe", lambda e, kt=kt: e.transpose(out=ps_t[:, kt, :], in_=hn[:, kt * 128:(kt + 1) * 128], identity=C.ident[:]),
             reads=[hn_b, C.b_ident], writes=[ps_t_b])
    P.op("act", lambda e: e.copy(out=hnT[:, :, j * 128:(j + 1) * 128], in_=ps_t[:, :, :]), reads=[ps_t_b], writes=[hnT_b])


def alloc_norm_work(P, sb, tag, double_hn=False):
    wk = Ctx()
    wk.ss = sb("ss" + tag, [128, 4], F32); wk.ss_b = P.bufs(4)
    wk.sd = sb("sd" + tag, [128, 4], F32); wk.sd_b = P.buf()
    wk.rstd = sb("rstd" + tag, [128, 4], F32); wk.rstd_b = P.buf()
    wk.hn = sb("hn" + tag, [128, D], BF16); wk.hn_b = P.buf()
    wk.junk = wk.hn; wk.junk_b = wk.hn_b
    if double_hn:
        wk.hn2 = [wk.hn, sb("hnB" + tag, [128, D], BF16)]; wk.hn2_b = [wk.hn_b, P.buf()]
    else:
        wk.hn2 = [wk.hn, wk.hn]; wk.hn2_b = [wk.hn_b, wk.hn_b]
    return wk


def fold_gain(nc, P, C, w_sb, w_buf, g_dram_row, kt_n, ncols, tag):
    gcol = C.sb("gcol" + tag, [128, kt_n], F32)
    gb = P.buf()
    with nc.allow_non_contiguous_dma(reason="tiny gain vector"):
        P.dma(lambda e: e.dma_start(out=gcol[:], in_=g_dram_row.rearrange("(kt p) -> p kt", p=128), allow_slow_non_contiguous=True), writes=[gb])
    for kt in range(kt_n):
        eng = "dve"
        P.op(eng, lambda e, kt=kt: e.tensor_scalar(out=w_sb[:, kt, 0:ncols], in0=w_sb[:, kt, 0:ncols],
                                                   scalar1=gcol[:, kt:kt + 1], scalar2=None, op0=ALU.mult),
             reads=[gb, w_buf], writes=[w_buf])


def phase_ffn(nc, P, A, lp, n_out, wup_pre=None):
    ntile = lp // TB
    with contextlib.ExitStack() as st:
        C = Ctx()
        _common_consts(nc, P, st, C)
        sb = C.sb
        if wup_pre is None:
            wup = sb("wup", [128, 8, 2 * DFF], BF16); wup_b = P.buf("wup")
            load_weight_bf16(nc, P, wup, wup_b, A.w_up, 8, 2 * DFF)
        else:
            wup, wup_b = wup_pre
        wdn = sb("wdn", [128, NFT, D], BF16); wdn_b = P.buf("wdn")
        load_weight_bf16(nc, P, wdn, wdn_b, A.w_down, NFT, D)
        fold_gain(nc, P, C, wup, wup_b, A.norm2_g, 8, 2 * DFF, "c")
        for kt in range(8):
            P.op("dve", lambda e, kt=kt: e.tensor_scalar(out=wup[:, kt, DFF:2 * DFF], in0=wup[:, kt, DFF:2 * DFF], scalar1=0.5,
                                                         scalar2=None, op0=ALU.mult), reads=[wup_b], writes=[wup_b])
        cw = sb("cw", [128, 3, NFT], F32); cb = sb("cb", [128, NFT], F32); cwb = P.buf("cw")
        with nc.allow_non_contiguous_dma(reason="tiny conv params"):
            for k in range(3):
                P.dma(lambda e, k=k: e.dma_start(out=cw[:, k, :], in_=A.conv_w[k, :].rearrange("(f p) -> p f", p=128), allow_slow_non_contiguous=True), writes=[cwb])
            P.dma(lambda e: e.dma_start(out=cb[:], in_=A.conv_b.rearrange("(f p) -> p f", p=128), allow_slow_non_contiguous=True), writes=[cwb])
        dg = sb("dg", [128, 3, NFT, 128], BF16); dg_b = P.buf("dg")
        for k in range(3):
            for ft in range(NFT):
                eng = "dve" if (k * NFT + ft) % 2 == 0 else "pool"
                P.op(eng, lambda e, k=k, ft=ft: e.tensor_scalar(out=dg[:, k, ft, :], in0=C.ident[:], scalar1=cw[:, k, ft:ft + 1],
                                                                scalar2=None, op0=ALU.mult),
                     reads=[cwb, C.b_ident], writes=[dg_b])
        xs1 = sb("xs", [128, 3, D], F32); xs = [xs1, xs1]
        xs1_b = P.bufs(3, "xs_"); xs_b = [xs1_b, xs1_b]
        hnT = sb("hnT", [128, 8, TB], BF16); hnT_b = P.buf("hnT")
        wk = alloc_norm_work(P, sb, "c", double_hn=True)
        a_sb = sb("a_sb", [128, NFT, TB + 2], BF16); a_b = P.bufs(NFT, "a")
        P.op("pool", lambda e: e.memset(a_sb[:, :, 0:2], 0.0), writes=a_b)
        cbt1 = sb("cbt", [128, TB], F32); cbt = [cbt1, cbt1]; c1b = P.buf("cbt"); cbt_b = [c1b, c1b]
        sq1 = sb("sq", [128, TB], F32); sq = [sq1, sq1]; s1b = P.buf("sq"); sq_b = [s1b, s1b]
        inn = sq; inn_b = sq_b; th = sq; th_b = sq_b; gg = sq; gg_b = sq_b
        gb = sb("gb", [128, NFT, TB], BF16); gb_b = P.bufs(NFT, "gb")
        ps_t = st.enter_context(nc.psum_tensor("ps_tc", [128, 8, 128], BF16)); ps_t_b = P.buf()
        psA = [st.enter_context(nc.psum_tensor("psA%d" % i, [128, 512], F32)) for i in range(2)]; psA_b = P.bufs(2)
        psV = st.enter_context(nc.psum_tensor("psV", [128, 512], F32)); psV_b = P.buf()
        psB = [st.enter_context(nc.psum_tensor("psB%d" % i, [128, 512], F32)) for i in range(2)]; psB_b = P.bufs(2)
        psD = [st.enter_context(nc.psum_tensor("psD%d" % i, [128, 512], F32)) for i in range(2)]; psD_b = P.bufs(2)
        fin = []
        nd = 0
        for ti in range(ntile):
            s = ti % 2
            if ti == 0:
                for j in range(3):
                    norm_block_stats(nc, P, C, A.h1, j * 128, j, xs[s], xs_b[s], wk)
                    norm_block_T(nc, P, C, j, hnT, hnT_b, ps_t, ps_t_b, wk)
            def emit_a(ft):
                r = ft % 2
                for kt in range(8):
                    P.op("pe", lambda e, kt=kt, ft=ft, r=r: e.matmul(psA[r][:, 0:TB], lhsT=wup[:, kt, ft * 128:(ft + 1) * 128],
                                                                      rhs=hnT[:, kt, :], start=(kt == 0), stop=(kt == 7)),
                         reads=[wup_b, hnT_b], writes=[psA_b[r]], inc=(kt == 7))
                P.op("act", lambda e, ft=ft, r=r: e.copy(out=a_sb[:, ft, 2:TB + 2], in_=psA[r][:, 0:TB]),
                     reads=[psA_b[r]], writes=[a_b[ft]])
            emit_a(0)
            for ft in range(NFT):
                r = ft % 2
                if ft + 1 < NFT:
                    emit_a(ft + 1)
                for k in range(3):
                    P.op("pe", lambda e, k=k, ft=ft: e.matmul(psV[:, 0:TB], lhsT=dg[:, k, ft, :], rhs=a_sb[:, ft, k:k + TB],
                                                              start=(k == 0), stop=(k == 2)),
                         reads=[dg_b, a_b[ft]], writes=[psV_b], inc=(k == 2))
                P.op("pool", lambda e, ft=ft: e.tensor_copy(out=a_sb[:, ft, 0:2], in_=a_sb[:, ft, TB:TB + 2]),
                     reads=[a_b[ft]], writes=[a_b[ft]])
                P.op("act", lambda e, ft=ft, r=r: e.activation(out=cbt[r][:], in_=psV[:, 0:TB], func=AF.Identity,
                                                              bias=cb[:, ft:ft + 1], scale=1.0),
                     reads=[psV_b, cwb], writes=[cbt_b[r]])
                P.op("pool", lambda e, r=r: e.tensor_tensor(out=sq[r][:], in0=cbt[r][:], in1=cbt[r][:], op=ALU.mult),
                     reads=[cbt_b[r]], writes=[sq_b[r]])
                P.op("dve", lambda e, r=r: e.scalar_tensor_tensor(out=inn[r][:], in0=sq[r][:], scalar=1.0 / GC, in1=cbt[r][:],
                                                                  op0=ALU.add, op1=ALU.mult),
                     reads=[sq_b[r], cbt_b[r]], writes=[inn_b[r]])
                P.op("act", lambda e, r=r: e.activation(out=th[r][:], in_=inn[r][:], func=AF.Tanh, scale=GK * GC),
                     reads=[inn_b[r]], writes=[th_b[r]])
                P.op("dve", lambda e, r=r: e.scalar_tensor_tensor(out=gg[r][:], in0=th[r][:], scalar=1.0, in1=cbt[r][:],
                                                                  op0=ALU.add, op1=ALU.mult),
                     reads=[th_b[r], cbt_b[r]], writes=[gg_b[r]])
                for kt in range(8):
                    P.op("pe", lambda e, kt=kt, ft=ft, r=r: e.matmul(psB[r][:, 0:TB], lhsT=wup[:, kt, DFF + ft * 128:DFF + (ft + 1) * 128],
                                                                      rhs=hnT[:, kt, :], start=(kt == 0), stop=(kt == 7)),
                         reads=[wup_b, hnT_b], writes=[psB_b[r]], inc=(kt == 7))
                P.op("dve", lambda e, ft=ft, r=r: e.tensor_tensor(out=gb[:, ft, :], in0=gg[r][:], in1=psB[r][:, 0:TB], op=ALU.mult),
                     reads=[gg_b[r], psB_b[r]], writes=[gb_b[ft]])
            pend_T = []
            for j in range(3):
                blk = ti * 3 + j
                o = nd % 2
                nd += 1
                for half in range(2):
                    for ft in range(NFT):
                        P.op("pe", lambda e, ft=ft, j=j, half=half: e.matmul(psD[half][:], lhsT=gb[:, ft, j * 128:(j + 1) * 128],
                                                                               rhs=wdn[:, ft, half * 512:(half + 1) * 512],
                                                                               start=(ft == 0), stop=(ft == NFT - 1)),
                             reads=[gb_b[ft], wdn_b], writes=[psD_b[half]], inc=(ft == NFT - 1))
                    P.op("dve", lambda e, j=j, half=half, s=s: e.tensor_tensor(out=xs[s][:, j, half * 512:(half + 1) * 512],
                                                                              in0=xs[s][:, j, half * 512:(half + 1) * 512],
                                                                              in1=psD[half][:], op=ALU.add),
                         reads=[xs_b[s][j], psD_b[half]], writes=[xs_b[s][j]])
                t0 = blk * 128
                lo = max(t0, NMETA); hi = min(t0 + 128, NMETA + n_out)
                if hi > lo:
                    fin.append(P.dma(lambda e, j=j, s=s, lo=lo, hi=hi, t0=t0: e.dma_start(out=A.out[lo - NMETA:hi - NMETA, :],
                                                                                            in_=xs[s][lo - t0:hi - t0, j, :]),
                                     reads=[xs_b[s][j]]))
                if ti + 1 < ntile:
                    norm_block_stats(nc, P, C, A.h1, ((ti + 1) * 3 + j) * 128, j, xs[s], xs_b[s], wk)
                    pend_T.append(j)
                    if len(pend_T) > 1:
                        norm_block_T(nc, P, C, pend_T.pop(0), hnT, hnT_b, ps_t, ps_t_b, wk)
            while pend_T:
                norm_block_T(nc, P, C, pend_T.pop(0), hnT, hnT_b, ps_t, ps_t_b, wk)
        P.barrier()
    return fin


def build_program(lp=LP, n_out=SEQ, mode="full"):
    nc = bass.Bass("TRN2", target_bir_lowering=False)
    A = Ctx()

    def din(name, shape):
        return nc.dram_tensor(name, shape, F32, kind="ExternalInput").ap()
    A.out = nc.dram_tensor("out", [n_out, D], F32, kind="ExternalOutput").ap()
    A.w_up = din("w_up", [D, 2 * DFF]); A.w_down = din("w_down", [DFF, D])
    A.norm2_g = din("norm2_g", [D]); A.conv_w = din("conv_w", [3, DFF]); A.conv_b = din("conv_b", [DFF])
    if mode == "ffn":
        A.h1 = din("h1", [lp, D])
    else:
        A.h1 = nc.dram_tensor("h1", [lp, D], F32, kind="Internal").ap()
        A.h0 = din("h0", [lp, D])
        A.norm1_g = din("norm1_g", [D]); A.w_in = din("w_in", [D, 4096])
        A.ssm_a_re = din("ssm_a_re", [32, 64]); A.ssm_a_im = din("ssm_a_im", [32, 64]); A.ssm_log_dt = din("ssm_log_dt", [32])
        A.ssm_b_re = din("ssm_b_re", [32, 64, 16]); A.ssm_b_im = din("ssm_b_im", [32, 64, 16])
        A.ssm_c_re = din("ssm_c_re", [32, 16, 64]); A.ssm_c_im = din("ssm_c_im", [32, 16, 64])
        A.ssm_d = din("ssm_d", [32, 16]); A.ssm_glu_w = din("ssm_glu_w", [512, 512]); A.ssm_glu_b = din("ssm_glu_b", [512])
        A.q_norm_g = din("q_norm_g", [64]); A.k_norm_g = din("k_norm_g", [64])
        for nm in ("lam_q1", "lam_k1", "lam_q2", "lam_k2"):
            setattr(A, nm, din(nm, [64]))
        A.subln_g = din("subln_g", [128])
        A.w_ssm_out = din("w_ssm_out", [512, D]); A.w_att_out = din("w_att_out", [512, D]); A.w_o = din("w_o", [D, D])
        if mode == "dbg":
            A.ysT = nc.dram_tensor("ysT", [512, lp], BF16, kind="ExternalOutput").ap()
            A.yaT = nc.dram_tensor("yaT", [512, lp], BF16, kind="ExternalOutput").ap()
        else:
            A.ysT = nc.dram_tensor("ysT", [512, lp], BF16, kind="Internal").ap()
            A.yaT = nc.dram_tensor("yaT", [512, lp], BF16, kind="Internal").ap()
    with contextlib.ExitStack() as st:
        P = Prog(nc, st)
        A.ysT_b = P.buf("ysT"); A.yaT_b = P.buf("yaT"); A.h1_b = P.buf("h1")
        wup_pre = None
        if mode != "ffn":
            phase_mixers(nc, P, A, lp)
            wup = st.enter_context(nc.sbuf_tensor("wup_shared", [128, 8, 2 * DFF], BF16)); wup_b = P.buf("wup")
            wup_pre = (wup, wup_b)
            phase_merge(nc, P, A, lp, prefetch=lambda: load_weight_bf16(nc, P, wup, wup_b, A.w_up, 8, 2 * DFF))
        fin = phase_ffn(nc, P, A, lp, n_out, wup_pre=wup_pre)
        if mode == "dbg":
            fin = fin + [A.ysT_b.last_w, A.yaT_b.last_w]
        P.finish(fin)
    return nc


_NC_CACHE = {}


def kernel(**inputs):
    x = np.asarray(inputs["x"], dtype=np.float32)
    meta = np.asarray(inputs["meta_tokens"], dtype=np.float32)
    bsz = x.shape[0]
    if "full" not in _NC_CACHE:
        _NC_CACHE["full"] = build_program(LP, SEQ, "full")
    nc = _NC_CACHE["full"]
    pad = np.zeros((LP - NMETA - SEQ, D), np.float32)
    shared = {}
    for k, v in inputs.items():
        if k in ("x", "meta_tokens"):
            continue
        shared[k] = np.ascontiguousarray(np.asarray(v, dtype=np.float32)[0])
    in_maps = []
    for b in range(bsz):
        m = dict(shared)
        m["h0"] = np.concatenate([meta, x[b], pad], axis=0)
        in_maps.append(m)
    res = run_bass_kernel_spmd(nc, in_maps, core_ids=list(range(bsz)))
    return np.stack([np.asarray(r["out"], dtype=np.float32) for r in res.results], axis=0)


def phase_merge(nc, P, A, lp, prefetch=None):
    ntile = lp // TB
    with contextlib.ExitStack() as st:
        C = Ctx()
        _common_consts(nc, P, st, C)
        sb = C.sb
        wg = sb("wg", [128, 8, 2048], BF16); wg_b = P.buf("wg")
        load_weight_bf16(nc, P, wg, wg_b, A.w_in, 8, 2048, col0=2048)
        fold_gain(nc, P, C, wg, wg_b, A.norm1_g, 8, 2048, "b")
        wso = sb("wso", [128, 4, D], BF16); wso_b = P.buf("wso")
        wao = sb("wao", [128, 4, D], BF16); wao_b = P.buf("wao")
        wo = sb("wo", [128, 8, D], BF16); wo_b = P.buf("wo")
        load_weight_bf16(nc, P, wso, wso_b, A.w_ssm_out, 4, D)
        for kt in range(4):
            P.op("dve", lambda e, kt=kt: e.tensor_scalar(out=wso[:, kt, :], in0=wso[:, kt, :], scalar1=0.25, scalar2=None, op0=ALU.mult),
                 reads=[wso_b], writes=[wso_b])
        load_weight_bf16(nc, P, wao, wao_b, A.w_att_out, 4, D)
        load_weight_bf16(nc, P, wo, wo_b, A.w_o, 8, D)
        if prefetch is not None:
            prefetch()
        xs = sb("xsb", [128, 3, D], F32); xs_b = P.bufs(3, "xsb")
        hnT = sb("hnTb", [128, 8, TB], BF16); hnT_b = P.buf("hnTb")
        wk = alloc_norm_work(P, sb, "b", double_hn=True)
        gs = sb("gs", [128, 16, TB], BF16); gs_b = P.bufs(16, "gs")
        ys = sb("ys", [128, 4, TB], BF16); ys_b = P.buf("ys")
        ya = sb("ya", [128, 4, TB], BF16); ya_b = P.buf("ya")
        t1 = sb("t1", [128, TB], F32); t1_b = P.buf("t1")
        t2 = sb("t2", [128, TB], F32); t2_b = P.buf("t2")
        t1x = sb("t1x", [128, TB], F32); t1x_b = P.buf("t1x")
        t2x = sb("t2x", [128, TB], F32); t2x_b = P.buf("t2x")
        mx = sb("mx", [128, 8, TB], BF16); mx_b = P.bufs(8, "mx")
        ps_t = st.enter_context(nc.psum_tensor("ps_tb", [128, 8, 128], BF16)); ps_t_b = P.buf()
        psG = [st.enter_context(nc.psum_tensor("psG%d" % i, [128, 512], F32)) for i in range(2)]; psG_b = P.bufs(2)
        ps1 = st.enter_context(nc.psum_tensor("ps1", [128, 512], F32)); ps1_b = P.buf()
        ps2 = st.enter_context(nc.psum_tensor("ps2", [128, 512], F32)); ps2_b = P.buf()
        psO = [st.enter_context(nc.psum_tensor("psO%d" % i, [128, 512], F32)) for i in range(2)]; psO_b = P.bufs(2)
        for ti in range(ntile):
            c0 = ti * TB
            if ti == 0:
                for j in range(3):
                    norm_block_stats(nc, P, C, A.h0, j * 128, j, xs, xs_b, wk)
                    norm_block_T(nc, P, C, j, hnT, hnT_b, ps_t, ps_t_b, wk)
            P.dma(lambda e, c0=c0: e.dma_start(out=ys[:], in_=A.ysT[:, c0:c0 + TB].rearrange("(kt p) t -> p kt t", p=128)),
                  reads=[A.ysT_b], writes=[ys_b])
            P.dma(lambda e, c0=c0: e.dma_start(out=ya[:], in_=A.yaT[:, c0:c0 + TB].rearrange("(kt p) t -> p kt t", p=128)),
                  reads=[A.yaT_b], writes=[ya_b])
            for m in range(16):
                r = m % 2
                for kt in range(8):
                    P.op("pe", lambda e, kt=kt, m=m, r=r: e.matmul(psG[r][:, 0:TB], lhsT=wg[:, kt, m * 128:(m + 1) * 128],
                                                                   rhs=hnT[:, kt, :], start=(kt == 0), stop=(kt == 7)),
                         reads=[wg_b, hnT_b], writes=[psG_b[r]], inc=(kt == 7))
                P.op("act", lambda e, m=m, r=r: e.activation(out=gs[:, m, :], in_=psG[r][:, 0:TB], func=AF.Sigmoid),
                     reads=[psG_b[r]], writes=[gs_b[m]])
            for m in range(8):
                pa, pa_b = (ps1, ps1_b) if m % 2 == 0 else (psG[0], psG_b[0])
                pb, pb_b = (ps2, ps2_b) if m % 2 == 0 else (psG[1], psG_b[1])
                for kt in range(4):
                    P.op("pe", lambda e, kt=kt, m=m, pa=pa: e.matmul(pa[:, 0:TB], lhsT=wso[:, kt, m * 128:(m + 1) * 128], rhs=ys[:, kt, :],
                                                                     start=(kt == 0), stop=(kt == 3)),
                         reads=[wso_b, ys_b], writes=[pa_b], inc=(kt == 3))
                for kt in range(4):
                    P.op("pe", lambda e, kt=kt, m=m, pb=pb: e.matmul(pb[:, 0:TB], lhsT=wao[:, kt, m * 128:(m + 1) * 128], rhs=ya[:, kt, :],
                                                                     start=(kt == 0), stop=(kt == 3)),
                         reads=[wao_b, ya_b], writes=[pb_b], inc=(kt == 3))
                tA, tA_b = (t1, t1_b) if m % 2 == 0 else (t1x, t1x_b)
                tB, tB_b = (t2, t2_b) if m % 2 == 0 else (t2x, t2x_b)
                P.op("dve", lambda e, m=m, pa=pa, tA=tA: e.tensor_tensor(out=tA[:], in0=gs[:, m, :], in1=pa[:, 0:TB], op=ALU.mult),
                     reads=[gs_b[m], pa_b], writes=[tA_b])
                P.op("dve", lambda e, m=m, pb=pb, tB=tB: e.tensor_tensor(out=tB[:], in0=gs[:, 8 + m, :], in1=pb[:, 0:TB], op=ALU.mult),
                     reads=[gs_b[8 + m], pb_b], writes=[tB_b])
                P.op("pool", lambda e, m=m, tA=tA, tB=tB: e.tensor_tensor(out=mx[:, m, :], in0=tA[:], in1=tB[:], op=ALU.add),
                     reads=[tA_b, tB_b], writes=[mx_b[m]])
            pend_T = []
            for j in range(3):
                blk = ti * 3 + j
                for half in range(2):
                    for kt in range(8):
                        P.op("pe", lambda e, kt=kt, j=j, half=half: e.matmul(psO[half][:], lhsT=mx[:, kt, j * 128:(j + 1) * 128],
                                                                               rhs=wo[:, kt, half * 512:(half + 1) * 512],
                                                                               start=(kt == 0), stop=(kt == 7)),
                             reads=[mx_b[kt], wo_b], writes=[psO_b[half]], inc=(kt == 7))
                    P.op("dve", lambda e, j=j, half=half: e.tensor_tensor(out=xs[:, j, half * 512:(half + 1) * 512],
                                                                         in0=xs[:, j, half * 512:(half + 1) * 512],
                                                                         in1=psO[half][:], op=ALU.add),
                         reads=[xs_b[j], psO_b[half]], writes=[xs_b[j]])
                P.dma(lambda e, j=j, blk=blk: e.dma_start(out=A.h1[blk * 128:(blk + 1) * 128, :], in_=xs[:, j, :]),
                      reads=[xs_b[j]], writes=[A.h1_b])
                if ti + 1 < ntile:
                    norm_block_stats(nc, P, C, A.h0, ((ti + 1) * 3 + j) * 128, j, xs, xs_b, wk)
                    pend_T.append(j)
                    if len(pend_T) > 1:
                        norm_block_T(nc, P, C, pend_T.pop(0), hnT, hnT_b, ps_t, ps_t_b, wk)
            while pend_T:
                norm_block_T(nc, P, C, pend_T.pop(0), hnT, hnT_b, ps_t, ps_t_b, wk)
        P.barrier()


NH = 4
SLOPES = [2.0 ** (-8.0 * (h + 1) / NH) for h in range(NH)]
LAM_INIT = 0.8 - 0.6 * math.exp(-0.3 * 0)
TS = 128


def s5_setup(nc, P, C, A, st, S):
    sb = C.sb
    tp = lambda n, sh, dt=F32: sb("s5_" + n, sh, dt)
    are = tp("are", [128, 16]); aim = tp("aim", [128, 16]); ldt = tp("ldt", [128, 16])
    b_in = P.buf("s5in")
    P.dma(lambda e: e.dma_start(out=are[:], in_=A.ssm_a_re.rearrange("(a s) p -> (s p) a", s=2), allow_slow_non_contiguous=True), writes=[b_in])
    P.dma(lambda e: e.dma_start(out=aim[:], in_=A.ssm_a_im.rearrange("(a s) p -> (s p) a", s=2), allow_slow_non_contiguous=True), writes=[b_in])
    l1 = tp("l1", [1, 32]); ones1 = tp("ones1", [1, 128]); b_l1 = P.buf()
    P.dma(lambda e: e.dma_start(out=l1[:], in_=A.ssm_log_dt.rearrange("(o g) -> o g", o=1)), writes=[b_l1])
    P.op("pool", lambda e: e.memset(ones1[:], 1.0), writes=[b_l1])
    psx = S.ps_misc; psx_b = S.ps_misc_b
    P.op("pe", lambda e: e.matmul(psx[:, 0:32], lhsT=ones1[:], rhs=l1[:], start=True, stop=True), reads=[b_l1], writes=[psx_b])
    lv = psx[:, 0:32].rearrange("q (a s) -> q a s", s=2)
    P.op("dve", lambda e: e.tensor_copy(out=ldt[0:64, :], in_=lv[0:64, :, 0]), reads=[psx_b], writes=[b_in])
    P.op("dve", lambda e: e.tensor_copy(out=ldt[64:128, :], in_=lv[64:128, :, 1]), reads=[psx_b], writes=[b_in])
    dt = tp("dt", [128, 16]); rho = tp("rho", [128, 16]); th = tp("th", [128, 16]); mag = tp("mag", [128, 16])
    cs = tp("cs", [128, 16]); sn = tp("sn", [128, 16]); tmpa = tp("tmpa", [128, 16]); tmpb = tp("tmpb", [128, 16])
    halfpi = tp("halfpi", [128, 1]); bq = P.buf("s5q")
    P.op("pool", lambda e: e.memset(halfpi[:], math.pi / 2), writes=[bq])
    P.op("act", lambda e: e.activation(out=dt[:], in_=ldt[:], func=AF.Exp), reads=[b_in], writes=[bq])
    P.op("dve", lambda e: e.tensor_tensor(out=rho[:], in0=are[:], in1=dt[:], op=ALU.mult), reads=[b_in, bq], writes=[bq])
    P.op("dve", lambda e: e.tensor_tensor(out=th[:], in0=aim[:], in1=dt[:], op=ALU.mult), reads=[b_in, bq], writes=[bq])
    P.op("act", lambda e: e.activation(out=mag[:], in_=rho[:], func=AF.Exp), reads=[bq], writes=[bq])
    P.op("act", lambda e: e.activation(out=sn[:], in_=th[:], func=AF.Sin, scale=1.0 / 16), reads=[bq], writes=[bq])
    P.op("act", lambda e: e.activation(out=cs[:], in_=th[:], func=AF.Sin, scale=1.0 / 16, bias=halfpi[:]), reads=[bq], writes=[bq])

    def cdouble(c, s_):
        P.op("dve", lambda e: e.tensor_tensor(out=tmpa[:], in0=c[:], in1=s_[:], op=ALU.mult), reads=[bq], writes=[bq])
        P.op("dve", lambda e: e.tensor_tensor(out=tmpb[:], in0=s_[:], in1=s_[:], op=ALU.mult), reads=[bq], writes=[bq])
        P.op("dve", lambda e: e.tensor_tensor(out=c[:], in0=c[:], in1=c[:], op=ALU.mult), reads=[bq], writes=[bq])
        P.op("dve", lambda e: e.tensor_tensor(out=c[:], in0=c[:], in1=tmpb[:], op=ALU.subtract), reads=[bq], writes=[bq])
        P.op("dve", lambda e: e.tensor_scalar(out=s_[:], in0=tmpa[:], scalar1=2.0, scalar2=None, op0=ALU.mult), reads=[bq], writes=[bq])
    for _ in range(4):
        cdouble(cs, sn)
    S.c1 = cs; S.s1 = sn; S.mag = mag; S.bq = bq
    lre = tp("lre", [128, 16]); lim = tp("lim", [128, 16]); den = tp("den", [128, 16])
    fre = tp("fre", [128, 16]); fim = tp("fim", [128, 16]); nfim = tp("nfim", [128, 16])
    P.op("dve", lambda e: e.tensor_tensor(out=lre[:], in0=mag[:], in1=cs[:], op=ALU.mult), reads=[bq], writes=[bq])
    P.op("dve", lambda e: e.tensor_tensor(out=lim[:], in0=mag[:], in1=sn[:], op=ALU.mult), reads=[bq], writes=[bq])
    P.op("dve", lambda e: e.tensor_tensor(out=den[:], in0=are[:], in1=are[:], op=ALU.mult), reads=[b_in, bq], writes=[bq])
    P.op("dve", lambda e: e.tensor_tensor(out=tmpa[:], in0=aim[:], in1=aim[:], op=ALU.mult), reads=[b_in, bq], writes=[bq])
    P.op("dve", lambda e: e.tensor_tensor(out=den[:], in0=den[:], in1=tmpa[:], op=ALU.add), reads=[bq], writes=[bq])
    P.op("dve", lambda e: e.reciprocal(out=den[:], in_=den[:]), reads=[bq], writes=[bq])
    P.op("dve", lambda e: e.tensor_scalar(out=tmpb[:], in0=lre[:], scalar1=-1.0, scalar2=None, op0=ALU.add), reads=[bq], writes=[bq])
    P.op("dve", lambda e: e.tensor_tensor(out=fre[:], in0=tmpb[:], in1=are[:], op=ALU.mult), reads=[bq, b_in], writes=[bq])
    P.op("dve", lambda e: e.tensor_tensor(out=tmpa[:], in0=lim[:], in1=aim[:], op=ALU.mult), reads=[bq, b_in], writes=[bq])
    P.op("dve", lambda e: e.tensor_tensor(out=fre[:], in0=fre[:], in1=tmpa[:], op=ALU.add), reads=[bq], writes=[bq])
    P.op("dve", lambda e: e.tensor_tensor(out=fre[:], in0=fre[:], in1=den[:], op=ALU.mult), reads=[bq], writes=[bq])
    P.op("dve", lambda e: e.tensor_tensor(out=fim[:], in0=lim[:], in1=are[:], op=ALU.mult), reads=[bq, b_in], writes=[bq])
    P.op("dve", lambda e: e.tensor_tensor(out=tmpa[:], in0=tmpb[:], in1=aim[:], op=ALU.mult), reads=[bq, b_in], writes=[bq])
    P.op("dve", lambda e: e.tensor_tensor(out=fim[:], in0=fim[:], in1=tmpa[:], op=ALU.subtract), reads=[bq], writes=[bq])
    P.op("dve", lambda e: e.tensor_tensor(out=fim[:], in0=fim[:], in1=den[:], op=ALU.mult), reads=[bq], writes=[bq])
    P.op("dve", lambda e: e.tensor_scalar(out=nfim[:], in0=fim[:], scalar1=-1.0, scalar2=None, op0=ALU.mult), reads=[bq], writes=[bq])
    al = getattr(S, "alias", {})
    bre = al["bre"] if "bre" in al else tp("bre", [128, 16, 16])[:]
    bim = al["bim"] if "bim" in al else tp("bim", [128, 16, 16])[:]
    bb = P.buf("s5b")
    P.dma(lambda e: e.dma_start(out=bre, in_=A.ssm_b_re.rearrange("(a s) p c -> (s p) a c", s=2)), writes=[bb])
    P.dma(lambda e: e.dma_start(out=bim, in_=A.ssm_b_im.rearrange("(a s) p c -> (s p) a c", s=2)), writes=[bb])
    wre = al["wre"] if "wre" in al else tp("wre", [128, 16, 128], BF16)[:]
    wim = al["wim"] if "wim" in al else tp("wim", [128, 16, 128], BF16)[:]
    bw = P.buf("s5w")
    P.op("pool", lambda e: e.memset(wre, 0.0), writes=[bw])
    P.op("pool", lambda e: e.memset(wim, 0.0), writes=[bw])
    tA, tB, BR, BI = [(al[nm] if nm in al else tp(nm, [128, 16, 16])[:]) for nm in ("tA", "tB", "BR", "BI")]
    bc3 = lambda t: t[:, :].unsqueeze(2).broadcast_to([128, 16, 16])
    P.op("dve", lambda e: e.tensor_tensor(out=tA, in0=bre, in1=bc3(fre), op=ALU.mult), reads=[bb, bq], writes=[bq])
    P.op("dve", lambda e: e.tensor_tensor(out=tB, in0=bim, in1=bc3(fim), op=ALU.mult), reads=[bb, bq], writes=[bq])
    P.op("dve", lambda e: e.tensor_tensor(out=BR, in0=tA, in1=tB, op=ALU.subtract), reads=[bq], writes=[bq])
    P.op("dve", lambda e: e.tensor_tensor(out=tA, in0=bim, in1=bc3(fre), op=ALU.mult), reads=[bb, bq], writes=[bq])
    P.op("dve", lambda e: e.tensor_tensor(out=tB, in0=bre, in1=bc3(fim), op=ALU.mult), reads=[bb, bq], writes=[bq])
    P.op("dve", lambda e: e.tensor_tensor(out=BI, in0=tA, in1=tB, op=ALU.add), reads=[bq], writes=[bq])
    for aq in range(4):
        for hf in range(2):
            c0 = 32 * aq + 16 * hf
            P.op("dve", lambda e, aq=aq, hf=hf, c0=c0: e.tensor_copy(out=wre[64 * hf:64 * hf + 64, aq::4, c0:c0 + 16], in_=BR[64 * hf:64 * hf + 64, aq::4, :]),
                 reads=[bq], writes=[bw])
            P.op("dve", lambda e, aq=aq, hf=hf, c0=c0: e.tensor_copy(out=wim[64 * hf:64 * hf + 64, aq::4, c0:c0 + 16], in_=BI[64 * hf:64 * hf + 64, aq::4, :]),
                 reads=[bq], writes=[bw])
    S.BBre = tp("BBre", [128, 16, 128], BF16); S.BBim = tp("BBim", [128, 16, 128], BF16); S.bBB = P.buf("BB")
    pst = S.ps_t; pst_b = S.ps_t_b
    for (src, dst) in ((wre, S.BBre), (wim, S.BBim)):
        for half in range(2):
            for a8 in range(8):
                a = half * 8 + a8
                P.op("pe", lambda e, a=a, a8=a8, src=src: e.transpose(out=pst[:, a8, :], in_=src[:, a, :], identity=C.ident[:]),
                     reads=[bw, C.b_ident], writes=[pst_b])
            P.op("dve", lambda e, half=half, dst=dst: e.tensor_copy(out=dst[:, half * 8:(half + 1) * 8, :], in_=pst[:, :, :]),
                 reads=[pst_b], writes=[S.bBB])
    cre = al["cre"] if "cre" in al else tp("cre", [128, 16, 16])[:]
    cim = al["cim"] if "cim" in al else tp("cim", [128, 16, 16])[:]
    bc = P.buf("s5c")
    for s_ in range(2):
        for a in range(16):
            P.dma(lambda e, s_=s_, a=a: e.dma_start(out=cre[64 * s_:64 * (s_ + 1), a, :], in_=A.ssm_c_re[2 * a + s_].rearrange("c p -> p c"),
                                                    allow_slow_non_contiguous=True), writes=[bc])
            P.dma(lambda e, s_=s_, a=a: e.dma_start(out=cim[64 * s_:64 * (s_ + 1), a, :], in_=A.ssm_c_im[2 * a + s_].rearrange("c p -> p c"),
                                                    allow_slow_non_contiguous=True), writes=[bc])
    S.CWre = tp("CWre", [128, 16, 32], BF16); S.CWim = tp("CWim", [128, 16, 32], BF16); S.bCW = P.buf("CW")
    P.op("pool", lambda e: e.memset(S.CWre[:], 0.0), writes=[S.bCW])
    P.op("pool", lambda e: e.memset(S.CWim[:], 0.0), writes=[S.bCW])
    P.op("dve", lambda e: e.tensor_copy(out=S.CWre[0:64, :, 0:16], in_=cre[0:64, :, :]), reads=[bc], writes=[S.bCW])
    P.op("dve", lambda e: e.tensor_copy(out=S.CWre[64:128, :, 16:32], in_=cre[64:128, :, :]), reads=[bc], writes=[S.bCW])
    P.op("dve", lambda e: e.tensor_scalar(out=S.CWim[0:64, :, 0:16], in0=cim[0:64, :, :], scalar1=-1.0, scalar2=None, op0=ALU.mult),
         reads=[bc], writes=[S.bCW])
    P.op("dve", lambda e: e.tensor_scalar(out=S.CWim[64:128, :, 16:32], in0=cim[64:128, :, :], scalar1=-1.0, scalar2=None, op0=ALU.mult),
         reads=[bc], writes=[S.bCW])
    S.dcol = tp("dcol", [128, 4]); S.b_d = P.buf("dcol")
    P.dma(lambda e: e.dma_start(out=S.dcol[:], in_=A.ssm_d.rearrange("(ct g) c -> (g c) ct", ct=4), allow_slow_non_contiguous=True), writes=[S.b_d])
    S.cosT = tp("cosT", [128, 16, TS]); S.sinT = tp("sinT", [128, 16, TS]); S.R = tp("R", [128, 16, TS]); S.bT = P.buf("tables")
    P.op("pool", lambda e: e.memset(S.cosT[:, :, 0:1], 1.0), writes=[S.bT])
    P.op("pool", lambda e: e.memset(S.sinT[:, :, 0:1], 0.0), writes=[S.bT])
    pc = tp("pc", [128, 16]); ps_ = tp("ps", [128, 16])
    P.op("dve", lambda e: e.tensor_copy(out=pc[:], in_=cs[:]), reads=[bq], writes=[bq])
    P.op("dve", lambda e: e.tensor_copy(out=ps_[:], in_=sn[:]), reads=[bq], writes=[bq])
    tt = al["tt"] if "tt" in al else tp("tt", [128, 16, TS // 2])[:]
    n = 1
    while n < TS:
        pcb = pc[:, :].unsqueeze(2).broadcast_to([128, 16, n]); psb = ps_[:, :].unsqueeze(2).broadcast_to([128, 16, n])
        sc = S.cosT[:, :, 0:n]; ss_ = S.sinT[:, :, 0:n]; dc = S.cosT[:, :, n:2 * n]; ds = S.sinT[:, :, n:2 * n]; t_ = tt[:, :, 0:n]
        P.op("dve", lambda e, dc=dc, sc=sc, pcb=pcb: e.tensor_tensor(out=dc, in0=sc, in1=pcb, op=ALU.mult), reads=[S.bT, bq], writes=[S.bT])
        P.op("dve", lambda e, t_=t_, ss_=ss_, psb=psb: e.tensor_tensor(out=t_, in0=ss_, in1=psb, op=ALU.mult), reads=[S.bT, bq], writes=[bq])
        P.op("dve", lambda e, dc=dc, t_=t_: e.tensor_tensor(out=dc, in0=dc, in1=t_, op=ALU.subtract), reads=[S.bT, bq], writes=[S.bT])
        P.op("dve", lambda e, ds=ds, ss_=ss_, pcb=pcb: e.tensor_tensor(out=ds, in0=ss_, in1=pcb, op=ALU.mult), reads=[S.bT, bq], writes=[S.bT])
        P.op("dve", lambda e, t_=t_, sc=sc, psb=psb: e.tensor_tensor(out=t_, in0=sc, in1=psb, op=ALU.mult), reads=[S.bT, bq], writes=[bq])
        P.op("dve", lambda e, ds=ds, t_=t_: e.tensor_tensor(out=ds, in0=ds, in1=t_, op=ALU.add), reads=[S.bT, bq], writes=[S.bT])
        cdouble(pc, ps_)
        n *= 2
    S.cN = pc; S.sN = ps_
    P.op("pool", lambda e: e.tensor_copy(out=S.R[:], in_=mag[:, :].unsqueeze(2).broadcast_to([128, 16, TS])), reads=[bq], writes=[S.bT])
    S.init_re = tp("init_re", [128, 16]); S.init_im = tp("init_im", [128, 16]); S.b_init = P.bufs(4, "init")
    P.op("pool", lambda e: e.memset(S.init_re[:], 0.0), writes=S.b_init)
    P.op("pool", lambda e: e.memset(S.init_im[:], 0.0), writes=S.b_init)


def phase_mixers(nc, P, A, lp):
    ntile = lp // TB
    nblk = lp // 128
    with contextlib.ExitStack() as st:
        C = Ctx()
        _common_consts(nc, P, st, C)
        sb = C.sb
        S = Ctx()
        ps_t = st.enter_context(nc.psum_tensor("ps_ta", [128, 8, 128], BF16)); ps_t_b = P.buf()
        psP = [st.enter_context(nc.psum_tensor("psP%d" % i, [128, 512], F32)) for i in range(2)]; psP_b = P.bufs(2)
        psQ = st.enter_context(nc.psum_tensor("psQ", [128, 512], F32)); psQ_b = P.buf("psQ"); psQr_b = [psQ_b, psQ_b, psQ_b]
        psS = [st.enter_context(nc.psum_tensor("psS%d" % i, [128, 512], F32)) for i in range(2)]; psS_b = P.bufs(2)
        psAccF = [st.enter_context(nc.psum_tensor("psAcc%d" % i, [128, 512], F32)) for i in range(2)]; psAcc_b = P.bufs(2)
        psAcc = [t[:, 0:387].rearrange("q (s e) -> q s e", s=3) for t in psAccF]
        S.ps_misc = psQ; S.ps_misc_b = psQ_b; S.ps_t = ps_t; S.ps_t_b = ps_t_b
        khist = sb("khist", [128, NH, lp], BF16); kh_b = P.bufs(ntile, "kh")
        vhist = sb("vhist", [128, nblk, NH, 129], BF16); vh_b = P.bufs(ntile, "vh")
        WS = []
        for k in range(2):
            W = Ctx()
            for nm in ("xtr", "xti", "tm1", "tm2", "wr", "wi"):
                setattr(W, nm, sb("%s%d" % (nm, k), [128, 4, TS], F32)); setattr(W, nm + "_b", P.buf())
            for nm in ("Sr", "Si"):
                setattr(W, nm, sb("%s%d" % (nm, k), [128, 4, TS], BF16)); setattr(W, nm + "_b", P.buf())
            for nm in ("yv", "yq"):
                setattr(W, nm, sb("%s%d" % (nm, k), [128, TS], F32)); setattr(W, nm + "_b", P.buf())
            for nm in ("c4", "c4b"):
                setattr(W, nm, sb("%s%d" % (nm, k), [128, 4], F32)); setattr(W, nm + "_b", P.buf())
            WS.append(W)
        xs = sb("xsa", [128, 1, D], F32); xs_b = P.bufs(1, "xsa")
        S.alias = {}
        S.alias["tt"] = xs[:, 0, :].rearrange("q (a k) -> q a k", a=16)
        if lp >= 2048:
            S.alias["wre"] = khist[:, 0, 0:2048].rearrange("q (a c) -> q a c", a=16)
            S.alias["wim"] = khist[:, 1, 0:2048].rearrange("q (a c) -> q a c", a=16)
        for nm, t in (("bre", WS[0].xtr), ("bim", WS[0].xti), ("cre", WS[0].tm1), ("cim", WS[0].tm2)):
            S.alias[nm] = t[:, 0:2, :].rearrange("q i (x c) -> q (i x) c", c=16)
        for nm, t in (("tA", WS[1].xtr), ("tB", WS[1].xti), ("BR", WS[1].tm1), ("BI", WS[1].tm2)):
            S.alias[nm] = t[:, 0:2, :].rearrange("q i (x c) -> q (i x) c", c=16)
        s5_setup(nc, P, C, A, st, S)
        P.barrier()
        P.op("pool", lambda e: e.memset(vhist[:, :, :, 128:129], 1.0), writes=vh_b)
        wa = sb("wa", [128, 8, 2048], BF16); wa_b = P.buf("wa")
        load_weight_bf16(nc, P, wa, wa_b, A.w_in, 8, 2048, col0=0)
        fold_gain(nc, P, C, wa, wa_b, A.norm1_g, 8, 2048, "a")
        glu = sb("glu", [128, 4, 512], BF16); glu_b = P.buf("glu")
        load_weight_bf16(nc, P, glu, glu_b, A.ssm_glu_w, 4, 512)
        for kt in range(4):
            P.op("pool", lambda e, kt=kt: e.tensor_scalar(out=glu[:, kt, :], in0=glu[:, kt, :], scalar1=0.5, scalar2=None, op0=ALU.mult),
                 reads=[glu_b], writes=[glu_b])
        glub = sb("glub", [128, 4], F32); misc_b = P.buf("misc")
        P.dma(lambda e: e.dma_start(out=glub[:], in_=A.ssm_glu_b.rearrange("(m p) -> p m", p=128), allow_slow_non_contiguous=True), writes=[misc_b])
        P.op("dve", lambda e: e.tensor_scalar(out=glub[:], in0=glub[:], scalar1=0.5, scalar2=None, op0=ALU.mult), reads=[misc_b], writes=[misc_b])
        gq = sb("gq", [128, 1], F32); gk = sb("gk", [128, 1], F32)
        for h2 in range(2):
            P.dma(lambda e, h2=h2: e.dma_start(out=gq[64 * h2:64 * h2 + 64, :], in_=A.q_norm_g.rearrange("(p o) -> p o", o=1)), writes=[misc_b])
            P.dma(lambda e, h2=h2: e.dma_start(out=gk[64 * h2:64 * h2 + 64, :], in_=A.k_norm_g.rearrange("(p o) -> p o", o=1)), writes=[misc_b])
        P.op("dve", lambda e: e.tensor_scalar(out=gq[:], in0=gq[:], scalar1=64 ** -0.5, scalar2=None, op0=ALU.mult), reads=[misc_b], writes=[misc_b])
        gsub = sb("gsub", [128, 1], F32)
        P.dma(lambda e: e.dma_start(out=gsub[:], in_=A.subln_g.rearrange("(p o) -> p o", o=1)), writes=[misc_b])
        P.op("dve", lambda e: e.tensor_scalar(out=gsub[:], in0=gsub[:], scalar1=1.0 - LAM_INIT, scalar2=None, op0=ALU.mult), reads=[misc_b], writes=[misc_b])
        lq = sb("lq", [1, 4, 64], F32); lpr = sb("lpr", [1, 2, 64], F32); lsum = sb("lsum", [1, 2], F32); lam1 = sb("lam1", [1, 2], F32)
        ones1 = sb("ones1a", [1, 128], F32); nlam = sb("nlam", [128, 1], F32); lam_b = P.buf("lam")
        for i, nm in enumerate(("lam_q1", "lam_k1", "lam_q2", "lam_k2")):
            P.dma(lambda e, i=i, nm=nm: e.dma_start(out=lq[:, i, :], in_=getattr(A, nm).rearrange("(o d) -> o d", o=1)), writes=[lam_b])
        P.op("pool", lambda e: e.memset(ones1[:], 1.0), writes=[lam_b])
        P.op("dve", lambda e: e.tensor_tensor(out=lpr[:, 0, :], in0=lq[:, 0, :], in1=lq[:, 1, :], op=ALU.mult), reads=[lam_b], writes=[lam_b])
        P.op("dve", lambda e: e.tensor_tensor(out=lpr[:, 1, :], in0=lq[:, 2, :], in1=lq[:, 3, :], op=ALU.mult), reads=[lam_b], writes=[lam_b])
        P.op("dve", lambda e: e.tensor_reduce(out=lsum[:], in_=lpr[:], axis=AX.X, op=ALU.add), reads=[lam_b], writes=[lam_b])
        P.op("act", lambda e: e.activation(out=lsum[:], in_=lsum[:], func=AF.Exp), reads=[lam_b], writes=[lam_b])
        P.op("dve", lambda e: e.tensor_tensor(out=lam1[:, 0:1], in0=lsum[:, 1:2], in1=lsum[:, 0:1], op=ALU.subtract), reads=[lam_b], writes=[lam_b])
        P.op("dve", lambda e: e.tensor_tensor(out=lam1[:, 1:2], in0=lsum[:, 1:2], in1=lsum[:, 0:1], op=ALU.subtract), reads=[lam_b], writes=[lam_b])
        P.op("dve", lambda e: e.tensor_scalar(out=lam1[:], in0=lam1[:], scalar1=-LAM_INIT, scalar2=None, op0=ALU.add), reads=[lam_b], writes=[lam_b])
        P.op("pe", lambda e: e.matmul(psQ[:, 0:2], lhsT=ones1[:], rhs=lam1[:], start=True, stop=True), reads=[lam_b], writes=[psQ_b])
        P.op("dve", lambda e: e.tensor_copy(out=nlam[:], in_=psQ[:, 0:1]), reads=[psQ_b], writes=[lam_b])
        bones = sb("bones", [128, 128], BF16)
        P.op("pool", lambda e: e.memset(bones[:], 0.0), writes=[misc_b])
        P.op("pool", lambda e: e.memset(bones[0:64, 0:64], 1.0), writes=[misc_b])
        P.op("pool", lambda e: e.memset(bones[64:128, 64:128], 1.0), writes=[misc_b])
        cmask = sb("cmask", [128, 128], BF16)
        P.op("pool", lambda e: e.memset(cmask[:], -30000.0), writes=[misc_b])
        P.op("pool", lambda e: e.affine_select(out=cmask[:], in_=cmask[:], pattern=[[-1, 128]], compare_op=ALU.is_gt, fill=0.0,
                                               base=0, channel_multiplier=1), reads=[misc_b], writes=[misc_b])
        kidx = sb("kidx", [128, 1], F32)
        P.op("pool", lambda e: e.iota(kidx[:], pattern=[[0, 1]], base=0, channel_multiplier=1, allow_small_or_imprecise_dtypes=True), writes=[misc_b])
        abias = sb("abias", [128, NH, nblk], F32)
        for h in range(NH):
            for dl in range(nblk):
                P.op("pool", lambda e, h=h, dl=dl: e.tensor_scalar(out=abias[:, h, dl:dl + 1], in0=kidx[:], scalar1=SLOPES[h],
                                                                   scalar2=-SLOPES[h] * 128.0 * dl, op0=ALU.mult, op1=ALU.add),
                     reads=[misc_b], writes=[misc_b])
        hnT = sb("hnTa", [128, 8, TB], BF16); hnT_b = P.buf("hnTa")
        wk = alloc_norm_work(P, sb, "a")
        uT = sb("uT", [128, 4, TB], BF16); uT_b = P.buf("uT")
        qz = [sb("qz%d" % c, [128, NH, TB], BF16) for c in range(2)]; qT_b = P.bufs(NH, "qT")
        P.op("pool", lambda e: e.memset(qz[0][:], 0.0), writes=qT_b)
        P.op("pool", lambda e: e.memset(qz[1][:], 0.0), writes=qT_b)
        sqb = sb("sqb", [128, TB], BF16); sqb_b = P.buf("sqb")
        lnv = sb("lnv", [128, TB], F32); lnv_b = P.buf("lnv")
        lnv3 = lnv[:, 0:384].rearrange("q (s e) -> q s e", s=3)
        ET = [sb("ET%d" % i, [128, TB], BF16) for i in range(3)]; ET_b = P.bufs(3, "ET")
        osb3 = sb("osb3", [128, 3, 128], F32); osb_b = P.buf("osb")
        onb3 = sb("onb3", [128, 3, 128], BF16); onb_b = P.buf("onb")
        rz3 = sb("rz3", [128, 4, 3], F32); rz_b = P.buf("rz")
        yaT = sb("yaTt", [128, NH, TB], BF16); yaT_b = P.buf("yaTt")
        ysT = sb("ysTt", [128, 4, TB], BF16); ysT_b = P.buf("ysTt")
        ygT = sb("ygT", [128, 4, TS], BF16); ygT_b = P.bufs(4, "ygT")
        sg = sb("sg", [128, TS], F32); sg_b = P.buf("sg")
        for ti in range(ntile):
            c0 = ti * TB
            for j in range(3):
                norm_tile(nc, P, C, A.h0, ti, 1, xs, xs_b, hnT, hnT_b, ps_t, ps_t_b, wk, blk0=ti * 3 + j, col0=j * 128)
            def emit_proj(m):
                r = m % 2
                for kt in range(8):
                    P.op("pe", lambda e, kt=kt, m=m, r=r: e.matmul(psP[r][:, 0:TB], lhsT=wa[:, kt, m * 128:(m + 1) * 128], rhs=hnT[:, kt, :],
                                                                   start=(kt == 0), stop=(kt == 7)), reads=[wa_b, hnT_b], writes=[psP_b[r]], inc=(kt == 7))
            emit_proj(0)
            for m in range(12):
                r = m % 2
                if m < 4:
                    emit_proj(m + 1)
                    P.op("act", lambda e, m=m, r=r: e.copy(out=uT[:, m, :], in_=psP[r][:, 0:TB]), reads=[psP_b[r]], writes=[uT_b])
                    continue
                h = (m - 4) % 4
                isq = m < 8
                P.op("act", lambda e, r=r: e.activation(out=sqb[:], in_=psP[r][:, 0:TB], func=AF.Square), reads=[psP_b[r]], writes=[sqb_b])
                if m + 1 < 12:
                    emit_proj(m + 1)
                P.op("pe", lambda e: e.matmul(psQ[:, 0:TB], lhsT=bones[:], rhs=sqb[:], start=True, stop=True), reads=[misc_b, sqb_b], writes=[psQ_b])
                P.op("act", lambda e: e.activation(out=lnv[:], in_=psQ[:, 0:TB], func=AF.Ln, bias=C.epsc[:], scale=1.0 / 64),
                     reads=[psQ_b, C.b_eps], writes=[lnv_b])
                P.op("act", lambda e: e.activation(out=lnv[:], in_=lnv[:], func=AF.Exp, scale=-0.5), reads=[lnv_b], writes=[lnv_b])
                if isq:
                    for c in range(2):
                        P.op("dve", lambda e, h=h, r=r, c=c: e.scalar_tensor_tensor(out=qz[c][64 * c:64 * c + 64, h, :], in0=psP[r][64 * c:64 * c + 64, 0:TB],
                                                                                   scalar=gq[64 * c:64 * c + 64, 0:1], in1=lnv[64 * c:64 * c + 64, :],
                                                                                   op0=ALU.mult, op1=ALU.mult),
                             reads=[psP_b[r], lnv_b, misc_b], writes=[qT_b[h]])
                else:
                    P.op("dve", lambda e, h=h, r=r, c0=c0: e.scalar_tensor_tensor(out=khist[:, h, c0:c0 + TB], in0=psP[r][:, 0:TB], scalar=gk[:, 0:1],
                                                                                 in1=lnv[:], op0=ALU.mult, op1=ALU.mult),
                         reads=[psP_b[r], lnv_b, misc_b], writes=[kh_b[ti]])
            for j in range(3):
                r = j % 2
                for kt in range(8):
                    P.op("pe", lambda e, kt=kt, j=j, r=r: e.matmul(psP[r][:], lhsT=hnT[:, kt, j * 128:(j + 1) * 128], rhs=wa[:, kt, 1536:2048],
                                                                   start=(kt == 0), stop=(kt == 7)), reads=[wa_b, hnT_b], writes=[psP_b[r]], inc=(kt == 7))
                P.op("act", lambda e, j=j, r=r, ti=ti: e.copy(out=vhist[:, ti * 3 + j, :, 0:128], in_=psP[r][:].rearrange("q (h e) -> q h e", h=NH)),
                     reads=[psP_b[r]], writes=[vh_b[ti]])
            NCH = (TB // TS) * 4

            def s5_X(k):
                sub, ct = divmod(k, 4); cs_ = sub * TS
                for i in range(4):
                    a = 4 * ct + i
                    P.op("pe", lambda e, a=a, i=i, ct=ct, cs_=cs_: e.matmul(psP[0][:, i * TS:(i + 1) * TS], lhsT=S.BBre[:, a, :], rhs=uT[:, ct, cs_:cs_ + TS],
                                                                             start=True, stop=True), reads=[S.bBB, uT_b], writes=[psP_b[0]])
                    P.op("pe", lambda e, a=a, i=i, ct=ct, cs_=cs_: e.matmul(psP[1][:, i * TS:(i + 1) * TS], lhsT=S.BBim[:, a, :], rhs=uT[:, ct, cs_:cs_ + TS],
                                                                             start=True, stop=True), reads=[S.bBB, uT_b], writes=[psP_b[1]])

            def s5_rot_in(k):
                sub, ct = divmod(k, 4); W = WS[k % 2]
                Xr = psP[0][:].rearrange("q (i t) -> q i t", i=4); Xi = psP[1][:].rearrange("q (i t) -> q i t", i=4)
                cT = S.cosT[:, 4 * ct:4 * ct + 4, :]; sT = S.sinT[:, 4 * ct:4 * ct + 4, :]
                P.op("dve", lambda e: e.tensor_tensor(out=W.xtr[:], in0=Xr, in1=cT, op=ALU.mult), reads=[psP_b[0], S.bT], writes=[W.xtr_b])
                P.op("dve", lambda e: e.tensor_tensor(out=W.tm1[:], in0=Xi, in1=sT, op=ALU.mult), reads=[psP_b[1], S.bT], writes=[W.tm1_b])
                P.op("dve", lambda e: e.tensor_tensor(out=W.xti[:], in0=Xi, in1=cT, op=ALU.mult), reads=[psP_b[1], S.bT], writes=[W.xti_b])
                P.op("dve", lambda e: e.tensor_tensor(out=W.tm2[:], in0=Xr, in1=sT, op=ALU.mult), reads=[psP_b[0], S.bT], writes=[W.tm2_b])

            def s5_scan(k):
                sub, ct = divmod(k, 4); W = WS[k % 2]
                cT = S.cosT[:, 4 * ct:4 * ct + 4, :]; sT = S.sinT[:, 4 * ct:4 * ct + 4, :]
                P.op("dve", lambda e: e.tensor_tensor(out=W.xtr[:], in0=W.xtr[:], in1=W.tm1[:], op=ALU.add), reads=[W.xtr_b, W.tm1_b], writes=[W.xtr_b])
                P.op("dve", lambda e: e.tensor_tensor(out=W.xti[:], in0=W.xti[:], in1=W.tm2[:], op=ALU.subtract), reads=[W.xti_b, W.tm2_b], writes=[W.xti_b])
                for i in range(4):
                    a = 4 * ct + i
                    P.op("dve", lambda e, a=a, i=i: e.tensor_tensor_scan(out=W.wr[:, i, :], data0=S.R[:, a, :], data1=W.xtr[:, i, :],
                                                                         initial=S.init_re[:, a:a + 1], op0=ALU.mult, op1=ALU.add),
                         reads=[S.bT, W.xtr_b, S.b_init[ct]], writes=[W.wr_b])
                    P.op("dve", lambda e, a=a, i=i: e.tensor_tensor_scan(out=W.wi[:, i, :], data0=S.R[:, a, :], data1=W.xti[:, i, :],
                                                                         initial=S.init_im[:, a:a + 1], op0=ALU.mult, op1=ALU.add),
                         reads=[S.bT, W.xti_b, S.b_init[ct]], writes=[W.wi_b])
                P.op("pool", lambda e: e.tensor_tensor(out=W.tm1[:], in0=W.wi[:], in1=cT, op=ALU.mult), reads=[W.wi_b, S.bT], writes=[W.tm1_b])
                P.op("pool", lambda e: e.tensor_tensor(out=W.tm2[:], in0=W.wr[:], in1=sT, op=ALU.mult), reads=[W.wr_b, S.bT], writes=[W.tm2_b])
                P.op("dve", lambda e: e.tensor_tensor(out=W.xtr[:], in0=W.wr[:], in1=cT, op=ALU.mult), reads=[W.wr_b, S.bT], writes=[W.xtr_b])
                P.op("dve", lambda e: e.tensor_tensor(out=W.xti[:], in0=W.wi[:], in1=sT, op=ALU.mult), reads=[W.wi_b, S.bT], writes=[W.xti_b])
                a0 = 4 * ct
                wl_r = W.wr[:, :, TS - 1]; wl_i = W.wi[:, :, TS - 1]
                P.op("dve", lambda e: e.tensor_tensor(out=W.c4[:], in0=wl_r, in1=S.cN[:, a0:a0 + 4], op=ALU.mult), reads=[W.wr_b, S.bq], writes=[W.c4_b])
                P.op("dve", lambda e: e.tensor_tensor(out=W.c4b[:], in0=wl_i, in1=S.sN[:, a0:a0 + 4], op=ALU.mult), reads=[W.wi_b, S.bq], writes=[W.c4b_b])
                P.op("dve", lambda e: e.tensor_tensor(out=S.init_re[:, a0:a0 + 4], in0=W.c4[:], in1=W.c4b[:], op=ALU.subtract), reads=[W.c4_b, W.c4b_b], writes=[S.b_init[ct]])
                P.op("dve", lambda e: e.tensor_tensor(out=W.c4[:], in0=wl_r, in1=S.sN[:, a0:a0 + 4], op=ALU.mult), reads=[W.wr_b, S.bq], writes=[W.c4_b])
                P.op("dve", lambda e: e.tensor_tensor(out=W.c4b[:], in0=wl_i, in1=S.cN[:, a0:a0 + 4], op=ALU.mult), reads=[W.wi_b, S.bq], writes=[W.c4b_b])
                P.op("dve", lambda e: e.tensor_tensor(out=S.init_im[:, a0:a0 + 4], in0=W.c4[:], in1=W.c4b[:], op=ALU.add), reads=[W.c4_b, W.c4b_b], writes=[S.b_init[ct]])
                P.op("dve", lambda e: e.tensor_tensor(out=W.Sr[:], in0=W.xtr[:], in1=W.xti[:], op=ALU.subtract), reads=[W.xtr_b, W.xti_b], writes=[W.Sr_b])
                P.op("dve", lambda e: e.tensor_tensor(out=W.Si[:], in0=W.tm1[:], in1=W.tm2[:], op=ALU.add), reads=[W.tm1_b, W.tm2_b], writes=[W.Si_b])

            def s5_Y(k):
                sub, ct = divmod(k, 4); W = WS[k % 2]; q = k % 2
                for i in range(4):
                    a = 4 * ct + i
                    P.op("pe", lambda e, a=a, i=i: e.matmul(psQ[32 * i:32 * i + 32, 0:TS], lhsT=S.CWre[:, a, :], rhs=W.Sr[:, i, :], start=True, stop=False,
                                                            tile_position=(0, 32 * i), skip_group_check=True), reads=[S.bCW, W.Sr_b], writes=[psQr_b[q]])
                    P.op("pe", lambda e, a=a, i=i: e.matmul(psQ[32 * i:32 * i + 32, 0:TS], lhsT=S.CWim[:, a, :], rhs=W.Si[:, i, :], start=False, stop=True,
                                                            tile_position=(0, 32 * i), skip_group_check=True), reads=[S.bCW, W.Si_b], writes=[psQr_b[q]])

            def s5_gelu(k):
                sub, ct = divmod(k, 4); W = WS[k % 2]; q = k % 2; cs_ = sub * TS
                P.op("dve", lambda e: e.scalar_tensor_tensor(out=W.yv[:], in0=uT[:, ct, cs_:cs_ + TS], scalar=S.dcol[:, ct:ct + 1], in1=psQ[:, 0:TS],
                                                             op0=ALU.mult, op1=ALU.add), reads=[uT_b, S.b_d, psQr_b[q]], writes=[W.yv_b])
                P.op("dve", lambda e: e.tensor_tensor(out=W.yq[:], in0=W.yv[:], in1=W.yv[:], op=ALU.mult), reads=[W.yv_b], writes=[W.yq_b])
                P.op("dve", lambda e: e.scalar_tensor_tensor(out=W.yq[:], in0=W.yq[:], scalar=1.0 / GC, in1=W.yv[:], op0=ALU.add, op1=ALU.mult),
                     reads=[W.yq_b, W.yv_b], writes=[W.yq_b])
                P.op("act", lambda e: e.activation(out=W.yq[:], in_=W.yq[:], func=AF.Tanh, scale=GK * GC), reads=[W.yq_b], writes=[W.yq_b])
                P.op("dve", lambda e: e.scalar_tensor_tensor(out=ygT[:, ct, :], in0=W.yq[:], scalar=1.0, in1=W.yv[:], op0=ALU.add, op1=ALU.mult),
                     reads=[W.yq_b, W.yv_b], writes=[ygT_b[ct]])

            def emit_glu(sub, ti=ti):
                cs_ = sub * TS
                for m in range(4):
                    for kt in range(4):
                        P.op("pe", lambda e, kt=kt, m=m: e.matmul(psQ[:, 0:TS], lhsT=glu[:, kt, m * 128:(m + 1) * 128], rhs=ygT[:, kt, :],
                                                                  start=(kt == 0), stop=(kt == 3), skip_group_check=True), reads=[glu_b, ygT_b[kt]], writes=[psQr_b[2]], inc=(kt == 3))
                    P.op("act", lambda e, m=m: e.activation(out=sg[:], in_=psQ[:, 0:TS], func=AF.Tanh, bias=glub[:, m:m + 1], scale=0.5),
                         reads=[psQr_b[2], misc_b], writes=[sg_b])
                    P.op("dve", lambda e, m=m, cs_=cs_: e.scalar_tensor_tensor(out=ysT[:, m, cs_:cs_ + TS], in0=sg[:], scalar=1.0, in1=ygT[:, m, :],
                                                                              op0=ALU.add, op1=ALU.mult), reads=[ygT_b[m], sg_b], writes=[ysT_b])
            units = [(kb, c) for kb in range(3 * ti + 3) for c in range(2)]

            def emit_scores(h, n, ti=ti, units=units):
                kb, c = units[n]
                n0 = max(0, kb - 3 * ti)
                r = n % 2
                ncol = TB - n0 * 128
                diag = kb >= 3 * ti
                P.op("pe", lambda e, kb=kb, c=c, n0=n0, r=r, ncol=ncol, diag=diag: e.matmul(
                    psS[r][:, 0:ncol], lhsT=khist[:, h, kb * 128:(kb + 1) * 128],
                    rhs=qz[c][:, h, n0 * 128:TB], start=True, stop=(not diag)),
                    reads=[kh_b[kb // 3], qT_b[h]], writes=[psS_b[r]])
                if diag:
                    P.op("pe", lambda e, r=r: e.matmul(psS[r][:, 0:128], lhsT=C.ident[:], rhs=cmask[:], start=False, stop=True),
                         reads=[C.b_ident, misc_b], writes=[psS_b[r]])

            def emit_att(h, n_lo, n_hi, ti=ti, units=units):
                emit_scores(h, n_lo)
                for n in range(n_lo, n_hi):
                    if n + 1 < n_hi:
                        emit_scores(h, n + 1)
                    kb, c = units[n]
                    n0 = max(0, kb - 3 * ti)
                    r = n % 2; eb = n % 3
                    ncol = TB - n0 * 128
                    dl2 = 3 * ti + 2 - kb
                    P.op("act", lambda e, h=h, dl2=dl2, r=r, eb=eb, ncol=ncol: e.activation(
                        out=ET[eb][:, 0:ncol], in_=psS[r][:, 0:ncol], func=AF.Exp, bias=abias[:, h, dl2:dl2 + 1], scale=1.0),
                        reads=[psS_b[r], misc_b], writes=[ET_b[eb]])
                    for sbk in range(n0, 3):
                        o0 = (sbk - n0) * 128
                        last = (kb == 3 * ti + sbk)
                        P.op("pe", lambda e, h=h, kb=kb, c=c, sbk=sbk, eb=eb, o0=o0, st_=(kb == 0 and sbk == 0), last=last: e.matmul(
                            psAcc[c][:, sbk, :], lhsT=ET[eb][:, o0:o0 + 128], rhs=vhist[:, kb, h, :], start=st_, stop=last,
                            skip_group_check=True), reads=[ET_b[eb], vh_b[kb // 3]], writes=[psAcc_b[c]])
                if n_hi < len(units):
                    return
                accS = [xs[:, 0, 387 * c_:387 * (c_ + 1)].rearrange("q (s e) -> q s e", s=3) for c_ in range(2)]
                P.op("dve", lambda e: e.tensor_copy(out=accS[0], in_=psAcc[0]), reads=[psAcc_b[0]], writes=[xs_b[0]])
                P.op("act", lambda e: e.copy(out=accS[1], in_=psAcc[1]), reads=[psAcc_b[1]], writes=[xs_b[0]])
                bc = lambda col: rz3[:, col, :].unsqueeze(2).broadcast_to([128, 3, 128])
                P.op("dve", lambda e: e.reciprocal(out=rz3[:, 0, :], in_=accS[0][:, :, 128]), reads=[xs_b[0]], writes=[rz_b])
                P.op("dve", lambda e: e.reciprocal(out=rz3[:, 1, :], in_=accS[1][:, :, 128]), reads=[xs_b[0]], writes=[rz_b])
                P.op("dve", lambda e: e.tensor_scalar(out=rz3[:, 1, :], in0=rz3[:, 1, :], scalar1=nlam[:, 0:1], scalar2=None, op0=ALU.mult),
                     reads=[rz_b, lam_b], writes=[rz_b])
                P.op("dve", lambda e: e.tensor_tensor(out=osb3[:], in0=accS[0][:, :, 0:128], in1=bc(0), op=ALU.mult), reads=[xs_b[0], rz_b], writes=[osb_b])
                P.op("dve", lambda e: e.tensor_tensor(out=lnv3, in0=accS[1][:, :, 0:128], in1=bc(1), op=ALU.mult), reads=[xs_b[0], rz_b], writes=[lnv_b])
                P.op("dve", lambda e: e.tensor_tensor(out=osb3[:], in0=osb3[:], in1=lnv3, op=ALU.add), reads=[osb_b, lnv_b], writes=[osb_b])
                P.op("dve", lambda e: e.tensor_tensor(out=lnv3, in0=osb3[:], in1=osb3[:], op=ALU.mult), reads=[osb_b], writes=[lnv_b])
                P.op("dve", lambda e: e.tensor_reduce(out=rz3[:, 2, :], in_=lnv3, axis=AX.X, op=ALU.add), reads=[lnv_b], writes=[rz_b])
                P.op("act", lambda e: e.activation(out=rz3[:, 2, :], in_=rz3[:, 2, :], func=AF.Sqrt, bias=C.epsc[:], scale=1.0 / 128), reads=[rz_b, C.b_eps], writes=[rz_b])
                P.op("dve", lambda e: e.reciprocal(out=rz3[:, 3, :], in_=rz3[:, 2, :]), reads=[rz_b], writes=[rz_b])
                P.op("dve", lambda e: e.tensor_tensor(out=onb3[:], in0=osb3[:], in1=bc(3), op=ALU.mult), reads=[osb_b, rz_b], writes=[onb_b])
                for sbk in range(3):
                    P.op("pe", lambda e, sbk=sbk: e.transpose(out=ps_t[:, sbk, :], in_=onb3[:, sbk, :], identity=C.ident[:]), reads=[onb_b, C.b_ident], writes=[ps_t_b])
                P.op("dve", lambda e, h=h: e.tensor_scalar(out=yaT[:, h, :], in0=ps_t[:, 0:3, :], scalar1=gsub[:, 0:1], scalar2=None, op0=ALU.mult),
                     reads=[ps_t_b, misc_b], writes=[yaT_b])
            att_items = []
            nu = len(units)
            cuts = [0, nu // 3, (2 * nu) // 3, nu]
            for h in range(NH):
                for part in range(3):
                    att_items.append(lambda h=h, part=part: emit_att(h, cuts[part], cuts[part + 1]))
            s5_X(0)
            for k in range(NCH + 2):
                if k < NCH:
                    s5_rot_in(k)
                    if k + 1 < NCH:
                        s5_X(k + 1)
                    s5_scan(k)
                if k < len(att_items):
                    att_items[k]()
                if 2 <= k <= NCH + 1:
                    s5_gelu(k - 2)
                    if (k - 2) % 4 == 3:
                        emit_glu((k - 2) // 4)
                if 1 <= k <= NCH:
                    s5_Y(k - 1)
            for k in range(NCH + 2, len(att_items)):
                att_items[k]()
            P.dma(lambda e, c0=c0: e.dma_start(out=A.ysT[:, c0:c0 + TB].rearrange("(kt p) t -> p kt t", p=128), in_=ysT[:]),
                  reads=[ysT_b], writes=[A.ysT_b])
            P.dma(lambda e, c0=c0: e.dma_start(out=A.yaT[:, c0:c0 + TB].rearrange("(kt p) t -> p kt t", p=128), in_=yaT[:]),
                  reads=[yaT_b], writes=[A.yaT_b])
        P.barrier()
```

```python
import contextlib
import math
import numpy as np
import concourse.bass as bass
import concourse.mybir as mybir
from concourse.bass_utils import run_bass_kernel_spmd

F32 = mybir.dt.float32
BF16 = mybir.dt.bfloat16
AF = mybir.ActivationFunctionType
ALU = mybir.AluOpType
AX = mybir.AxisListType

D = 1024
SEQ = 4096
NMETA = 16
LP = 4224
NBLK = LP // 128
EPS = 1e-6
DFF = 2816
NFT = DFF // 128


class Buf:
    __slots__ = ("name", "last_w", "readers")

    def __init__(self, name):
        self.name = name
        self.last_w = None
        self.readers = {}


class Prog:
    ENGS = ("pe", "dve", "act", "pool", "sp")

    def __init__(self, nc, stack, n_dma_sems=20):
        self.nc = nc
        self.sems = {}
        for e in self.ENGS:
            self.sems[e] = stack.enter_context(nc.semaphore("s_" + e))
        self.dma_keys = []
        for i in range(n_dma_sems):
            k = "dma%d" % i
            self.sems[k] = stack.enter_context(nc.semaphore("s_" + k))
            self.dma_keys.append(k)
        self.cnt = {k: 0 for k in self.sems}
        self.seen = {e: {} for e in self.ENGS}
        self.lists = {e: [] for e in self.ENGS}
        self.dma_rr_map = {}
        self.nbuf = 0

    def buf(self, name=None):
        self.nbuf += 1
        return Buf(name or ("b%d" % self.nbuf))

    def bufs(self, n, name="b"):
        return [self.buf("%s%d" % (name, i)) for i in range(n)]

    def _wait(self, eng, key, val):
        if val <= 0:
            return
        if self.seen[eng].get(key, 0) >= val:
            return
        self.seen[eng][key] = val
        self.lists[eng].append(("wait", key, val))

    def _deps(self, eng, reads, writes):
        for b in reads:
            if b.last_w is not None:
                self._wait(eng, *b.last_w)
        for b in writes:
            if b.last_w is not None and not (b.last_w[0] == eng == "pe"):
                self._wait(eng, *b.last_w)
            for k, v in b.readers.items():
                if not (k == eng == "pe"):
                    self._wait(eng, k, v)

    def op(self, eng, fn, reads=(), writes=()):
        self._deps(eng, reads, writes)
        self.cnt[eng] += 1
        idx = self.cnt[eng]
        self.lists[eng].append(("op", fn, eng, 1))
        for b in reads:
            b.readers[eng] = idx
        for b in writes:
            b.last_w = (eng, idx)
            b.readers = {}
        return idx

    def dma(self, fn, reads=(), writes=(), eng="sp"):
        half = len(self.dma_keys) // 2
        pool_keys = self.dma_keys[:half] if eng == "pool" else self.dma_keys[half:]
        rr = self.dma_rr_map.get(eng == "pool", 0)
        self.dma_rr_map[eng == "pool"] = rr + 1
        key = pool_keys[rr % len(pool_keys)]
        self._deps(eng, reads, writes)
        self._wait(eng, key, self.cnt[key])
        self.cnt[key] += 16
        val = self.cnt[key]
        self.lists[eng].append(("op", fn, key, 16))
        for b in reads:
            b.readers[key] = val
        for b in writes:
            b.last_w = (key, val)
            b.readers = {}
        return key, val

    def barrier(self):
        for e in self.ENGS:
            for k, v in self.cnt.items():
                if k != e:
                    self._wait(e, k, v)

    def finish(self, final_waits):
        for k, v in final_waits:
            self._wait("sp", k, v)
        nc = self.nc
        handles = dict(pe=nc.tensor, dve=nc.vector, act=nc.scalar, pool=nc.gpsimd, sp=nc.sync)
        sems = self.sems
        lists = self.lists

        def replay(name):
            eng = handles[name]
            for it in lists[name]:
                if it[0] == "wait":
                    eng.wait_ge(sems[it[1]], it[2])
                else:
                    ins = it[1](eng)
                    ins.then_inc(sems[it[2]], it[3])

        with nc.Block() as block:
            @block.sync
            def _(e):
                replay("sp")

            @block.tensor
            def _(e):
                replay("pe")

            @block.vector
            def _(e):
                replay("dve")

            @block.scalar
            def _(e):
                replay("act")

            @block.gpsimd
            def _(e):
                replay("pool")


TB = 384
GK = math.sqrt(2.0 / math.pi)
GC = 0.044715


class Ctx:
    pass


_PHASE_ID = [0]


def _common_consts(nc, P, st, C):
    _PHASE_ID[0] += 1
    tag = "_p%d" % _PHASE_ID[0]

    def sb(name, shape, dt):
        return st.enter_context(nc.sbuf_tensor(name + tag, shape, dt))
    C.sb = sb
    C.ident = sb("ident", [128, 128], BF16)
    C.b_ident = P.buf("ident")
    P.op("pool", lambda e: e.memset(C.ident[:], 1.0), writes=[C.b_ident])
    P.op("pool", lambda e: e.affine_select(out=C.ident[:], in_=C.ident[:], pattern=[[-1, 128]],
                                           compare_op=ALU.is_equal, fill=0.0, base=0, channel_multiplier=1),
         reads=[C.b_ident], writes=[C.b_ident])
    C.epsc = sb("epsc", [128, 1], F32)
    C.b_eps = P.buf("eps")
    P.op("pool", lambda e: e.memset(C.epsc[:], EPS), writes=[C.b_eps])


def load_weight_bf16(nc, P, dst, dst_buf, src2d, kt_n, ncols, col0=0, scale_col=None, extra_scale=None):
    for kt in range(kt_n):
        c = 0
        while c < ncols:
            w = min(2048, ncols - c)
            P.dma(lambda e, kt=kt, c=c, w=w: e.dma_start(
                out=dst[:, kt, c:c + w], in_=src2d[kt * 128:(kt + 1) * 128, col0 + c:col0 + c + w]),
                writes=[dst_buf], eng="pool")
            c += w


def norm_tile(nc, P, C, src_rows, tile_i, nblk, xs, xs_b, hnT, hnT_b, ps_t, ps_t_b, wk, blk0=None, col0=0):
    for j in range(nblk):
        r0 = ((tile_i * nblk if blk0 is None else blk0) + j) * 128
        P.dma(lambda e, j=j, r0=r0: e.dma_start(out=xs[:, j, :], in_=src_rows[r0:r0 + 128, :]), writes=[xs_b[j]])
    for j in range(nblk):
        P.op("act", lambda e, j=j: e.activation(out=wk.junk[:], in_=xs[:, j, :], func=AF.Square,
                                                accum_out=wk.ss[:, j:j + 1]),
             reads=[xs_b[j]], writes=[wk.junk_b, wk.ss_b[j]])
    P.op("act", lambda e: e.activation(out=wk.sd[:, 0:nblk], in_=wk.ss[:, 0:nblk], func=AF.Sqrt,
                                       bias=C.epsc[:], scale=1.0 / D),
         reads=list(wk.ss_b[:nblk]) + [C.b_eps], writes=[wk.sd_b])
    P.op("dve", lambda e: e.reciprocal(out=wk.rstd[:, 0:nblk], in_=wk.sd[:, 0:nblk]), reads=[wk.sd_b], writes=[wk.rstd_b])
    for j in range(nblk):
        P.op("dve", lambda e, j=j: e.tensor_scalar(out=wk.hn[:], in0=xs[:, j, :], scalar1=wk.rstd[:, j:j + 1],
                                                   scalar2=None, op0=ALU.mult),
             reads=[xs_b[j], wk.rstd_b], writes=[wk.hn_b])
        for kt in range(8):
            P.op("pe", lambda e, kt=kt: e.transpose(out=ps_t[:, kt, :], in_=wk.hn[:, kt * 128:(kt + 1) * 128],
                                                    identity=C.ident[:]),
                 reads=[wk.hn_b, C.b_ident], writes=[ps_t_b])
        P.op("act", lambda e, j=j: e.copy(out=hnT[:, :, col0 + j * 128:col0 + (j + 1) * 128], in_=ps_t[:, :, :]),
             reads=[ps_t_b], writes=[hnT_b])


def norm_block_stats(nc, P, C, src_rows, r0, j, xs, xs_b, wk, load=True):
    hn = wk.hn2[j % 2]; hn_b = wk.hn2_b[j % 2]
    if load:
        P.dma(lambda e: e.dma_start(out=xs[:, j, :], in_=src_rows[r0:r0 + 128, :]), writes=[xs_b[j]])
    P.op("act", lambda e: e.activation(out=hn[:], in_=xs[:, j, :], func=AF.Square, accum_out=wk.ss[:, j:j + 1]),
         reads=[xs_b[j]], writes=[hn_b, wk.ss_b[j]])
    P.op("act", lambda e: e.activation(out=wk.sd[:, j:j + 1], in_=wk.ss[:, j:j + 1], func=AF.Sqrt, bias=C.epsc[:], scale=1.0 / D),
         reads=[wk.ss_b[j], C.b_eps], writes=[wk.ss_b[j]])
    P.op("dve", lambda e: e.reciprocal(out=wk.rstd[:, j:j + 1], in_=wk.sd[:, j:j + 1]), reads=[wk.ss_b[j]], writes=[wk.ss_b[j]])
    P.op("dve", lambda e: e.tensor_scalar(out=hn[:], in0=xs[:, j, :], scalar1=wk.rstd[:, j:j + 1], scalar2=None, op0=ALU.mult),
         reads=[xs_b[j], wk.ss_b[j]], writes=[hn_b])


def norm_block_T(nc, P, C, j, hnT, hnT_b, ps_t, ps_t_b, wk):
    hn = wk.hn2[j % 2]; hn_b = wk.hn2_b[j % 2]
    for kt in range(8):
        P.op("pe", lambda e, kt=kt: e.transpose(out=ps_t[:, kt, :], in_=hn[:, kt * 128:(kt + 1) * 128], identity=C.ident[:]),
             reads=[hn_b, C.b_ident], writes=[ps_t_b])
    P.op("act", lambda e: e.copy(out=hnT[:, :, j * 128:(j + 1) * 128], in_=ps_t[:, :, :]), reads=[ps_t_b], writes=[hnT_b])


def alloc_norm_work(P, sb, tag, double_hn=False):
    wk = Ctx()
    wk.ss = sb("ss" + tag, [128, 4], F32); wk.ss_b = P.bufs(4)
    wk.sd = sb("sd" + tag, [128, 4], F32); wk.sd_b = P.buf()
    wk.rstd = sb("rstd" + tag, [128, 4], F32); wk.rstd_b = P.buf()
    wk.hn = sb("hn" + tag, [128, D], BF16); wk.hn_b = P.buf()
    wk.junk = wk.hn; wk.junk_b = wk.hn_b
    if double_hn:
        wk.hn2 = [wk.hn, sb("hnB" + tag, [128, D], BF16)]; wk.hn2_b = [wk.hn_b, P.buf()]
    else:
        wk.hn2 = [wk.hn, wk.hn]; wk.hn2_b = [wk.hn_b, wk.hn_b]
    return wk


def fold_gain(nc, P, C, w_sb, w_buf, g_dram_row, kt_n, ncols, tag):
    gcol = C.sb("gcol" + tag, [128, kt_n], F32)
    gb = P.buf()
    with nc.allow_non_contiguous_dma(reason="tiny gain vector"):
        P.dma(lambda e: e.dma_start(out=gcol[:], in_=g_dram_row.rearrange("(kt p) -> p kt", p=128), allow_slow_non_contiguous=True), writes=[gb])
    for kt in range(kt_n):
        eng = "dve"
        P.op(eng, lambda e, kt=kt: e.tensor_scalar(out=w_sb[:, kt, 0:ncols], in0=w_sb[:, kt, 0:ncols],
                                                   scalar1=gcol[:, kt:kt + 1], scalar2=None, op0=ALU.mult),
             reads=[gb, w_buf], writes=[w_buf])


def phase_ffn(nc, P, A, lp, n_out, wup_pre=None):
    ntile = lp // TB
    with contextlib.ExitStack() as st:
        C = Ctx()
        _common_consts(nc, P, st, C)
        sb = C.sb
        if wup_pre is None:
            wup = sb("wup", [128, 8, 2 * DFF], BF16); wup_b = P.buf("wup")
            load_weight_bf16(nc, P, wup, wup_b, A.w_up, 8, 2 * DFF)
        else:
            wup, wup_b = wup_pre
        wdn = sb("wdn", [128, NFT, D], BF16); wdn_b = P.buf("wdn")
        load_weight_bf16(nc, P, wdn, wdn_b, A.w_down, NFT, D)
        fold_gain(nc, P, C, wup, wup_b, A.norm2_g, 8, 2 * DFF, "c")
        for kt in range(8):
            P.op("dve", lambda e, kt=kt: e.tensor_scalar(out=wup[:, kt, DFF:2 * DFF], in0=wup[:, kt, DFF:2 * DFF], scalar1=0.5,
                                                         scalar2=None, op0=ALU.mult), reads=[wup_b], writes=[wup_b])
        cw = sb("cw", [128, 3, NFT], F32); cb = sb("cb", [128, NFT], F32); cwb = P.buf("cw")
        with nc.allow_non_contiguous_dma(reason="tiny conv params"):
            for k in range(3):
                P.dma(lambda e, k=k: e.dma_start(out=cw[:, k, :], in_=A.conv_w[k, :].rearrange("(f p) -> p f", p=128), allow_slow_non_contiguous=True), writes=[cwb])
            P.dma(lambda e: e.dma_start(out=cb[:], in_=A.conv_b.rearrange("(f p) -> p f", p=128), allow_slow_non_contiguous=True), writes=[cwb])
        dg = sb("dg", [128, 3, NFT, 128], BF16); dg_b = P.buf("dg")
        for k in range(3):
            for ft in range(NFT):
                eng = "dve" if (k * NFT + ft) % 2 == 0 else "pool"
                P.op(eng, lambda e, k=k, ft=ft: e.tensor_scalar(out=dg[:, k, ft, :], in0=C.ident[:], scalar1=cw[:, k, ft:ft + 1],
                                                                scalar2=None, op0=ALU.mult),
                     reads=[cwb, C.b_ident], writes=[dg_b])
        xs1 = sb("xs", [128, 3, D], F32); xs = [xs1, xs1]
        xs1_b = P.bufs(3, "xs_"); xs_b = [xs1_b, xs1_b]
        hnT = sb("hnT", [128, 8, TB], BF16); hnT_b = P.buf("hnT")
        wk = alloc_norm_work(P, sb, "c", double_hn=True)
        a_sb = sb("a_sb", [128, NFT, TB + 2], BF16); a_b = P.bufs(NFT, "a")
        P.op("pool", lambda e: e.memset(a_sb[:, :, 0:2], 0.0), writes=a_b)
        cbt1 = sb("cbt", [128, TB], F32); cbt = [cbt1, cbt1]; c1b = P.buf("cbt"); cbt_b = [c1b, c1b]
        sq1 = sb("sq", [128, TB], F32); sq = [sq1, sq1]; s1b = P.buf("sq"); sq_b = [s1b, s1b]
        inn = sq; inn_b = sq_b; th = sq; th_b = sq_b; gg = sq; gg_b = sq_b
        gb = sb("gb", [128, NFT, TB], BF16); gb_b = P.bufs(NFT, "gb")
        ps_t = st.enter_context(nc.psum_tensor("ps_tc", [128, 8, 128], BF16)); ps_t_b = P.buf()
        psA = [st.enter_context(nc.psum_tensor("psA%d" % i, [128, 512], F32)) for i in range(2)]; psA_b = P.bufs(2)
        psV = st.enter_context(nc.psum_tensor("psV", [128, 512], F32)); psV_b = P.buf()
        psB = [st.enter_context(nc.psum_tensor("psB%d" % i, [128, 512], F32)) for i in range(2)]; psB_b = P.bufs(2)
        psD = [st.enter_context(nc.psum_tensor("psD%d" % i, [128, 512], F32)) for i in range(2)]; psD_b = P.bufs(2)
        fin = []
        nd = 0
        for ti in range(ntile):
            s = ti % 2
            if ti == 0:
                for j in range(3):
                    norm_block_stats(nc, P, C, A.h1, j * 128, j, xs[s], xs_b[s], wk)
                    norm_block_T(nc, P, C, j, hnT, hnT_b, ps_t, ps_t_b, wk)
            def emit_a(ft):
                r = ft % 2
                for kt in range(8):
                    P.op("pe", lambda e, kt=kt, ft=ft, r=r: e.matmul(psA[r][:, 0:TB], lhsT=wup[:, kt, ft * 128:(ft + 1) * 128],
                                                                      rhs=hnT[:, kt, :], start=(kt == 0), stop=(kt == 7)),
                         reads=[wup_b, hnT_b], writes=[psA_b[r]])
                P.op("act", lambda e, ft=ft, r=r: e.copy(out=a_sb[:, ft, 2:TB + 2], in_=psA[r][:, 0:TB]),
                     reads=[psA_b[r]], writes=[a_b[ft]])
            emit_a(0)
            for ft in range(NFT):
                r = ft % 2
                if ft + 1 < NFT:
                    emit_a(ft + 1)
                for k in range(3):
                    P.op("pe", lambda e, k=k, ft=ft: e.matmul(psV[:, 0:TB], lhsT=dg[:, k, ft, :], rhs=a_sb[:, ft, k:k + TB],
                                                              start=(k == 0), stop=(k == 2)),
                         reads=[dg_b, a_b[ft]], writes=[psV_b])
                P.op("pool", lambda e, ft=ft: e.tensor_copy(out=a_sb[:, ft, 0:2], in_=a_sb[:, ft, TB:TB + 2]),
                     reads=[a_b[ft]], writes=[a_b[ft]])
                P.op("act", lambda e, ft=ft, r=r: e.activation(out=cbt[r][:], in_=psV[:, 0:TB], func=AF.Identity,
                                                              bias=cb[:, ft:ft + 1], scale=1.0),
                     reads=[psV_b, cwb], writes=[cbt_b[r]])
                P.op("pool", lambda e, r=r: e.tensor_tensor(out=sq[r][:], in0=cbt[r][:], in1=cbt[r][:], op=ALU.mult),
                     reads=[cbt_b[r]], writes=[sq_b[r]])
                P.op("dve", lambda e, r=r: e.scalar_tensor_tensor(out=inn[r][:], in0=sq[r][:], scalar=1.0 / GC, in1=cbt[r][:],
                                                                  op0=ALU.add, op1=ALU.mult),
                     reads=[sq_b[r], cbt_b[r]], writes=[inn_b[r]])
                P.op("act", lambda e, r=r: e.activation(out=th[r][:], in_=inn[r][:], func=AF.Tanh, scale=GK * GC),
                     reads=[inn_b[r]], writes=[th_b[r]])
                P.op("dve", lambda e, r=r: e.scalar_tensor_tensor(out=gg[r][:], in0=th[r][:], scalar=1.0, in1=cbt[r][:],
                                                                  op0=ALU.add, op1=ALU.mult),
                     reads=[th_b[r], cbt_b[r]], writes=[gg_b[r]])
                for kt in range(8):
                    P.op("pe", lambda e, kt=kt, ft=ft, r=r: e.matmul(psB[r][:, 0:TB], lhsT=wup[:, kt, DFF + ft * 128:DFF + (ft + 1) * 128],
                                                                      rhs=hnT[:, kt, :], start=(kt == 0), stop=(kt == 7)),
                         reads=[wup_b, hnT_b], writes=[psB_b[r]])
                P.op("dve", lambda e, ft=ft, r=r: e.tensor_tensor(out=gb[:, ft, :], in0=gg[r][:], in1=psB[r][:, 0:TB], op=ALU.mult),
                     reads=[gg_b[r], psB_b[r]], writes=[gb_b[ft]])
            pend_T = []
            for j in range(3):
                blk = ti * 3 + j
                o = nd % 2
                nd += 1
                for half in range(2):
                    for ft in range(NFT):
                        P.op("pe", lambda e, ft=ft, j=j, half=half: e.matmul(psD[half][:], lhsT=gb[:, ft, j * 128:(j + 1) * 128],
                                                                               rhs=wdn[:, ft, half * 512:(half + 1) * 512],
                                                                               start=(ft == 0), stop=(ft == NFT - 1)),
                             reads=[gb_b[ft], wdn_b], writes=[psD_b[half]])
                    P.op("dve", lambda e, j=j, half=half, s=s: e.tensor_tensor(out=xs[s][:, j, half * 512:(half + 1) * 512],
                                                                              in0=xs[s][:, j, half * 512:(half + 1) * 512],
                                                                              in1=psD[half][:], op=ALU.add),
                         reads=[xs_b[s][j], psD_b[half]], writes=[xs_b[s][j]])
                t0 = blk * 128
                lo = max(t0, NMETA); hi = min(t0 + 128, NMETA + n_out)
                if hi > lo:
                    fin.append(P.dma(lambda e, j=j, s=s, lo=lo, hi=hi, t0=t0: e.dma_start(out=A.out[lo - NMETA:hi - NMETA, :],
                                                                                            in_=xs[s][lo - t0:hi - t0, j, :]),
                                     reads=[xs_b[s][j]]))
                if ti + 1 < ntile:
                    norm_block_stats(nc, P, C, A.h1, ((ti + 1) * 3 + j) * 128, j, xs[s], xs_b[s], wk)
                    pend_T.append(j)
                    if len(pend_T) > 1:
                        norm_block_T(nc, P, C, pend_T.pop(0), hnT, hnT_b, ps_t, ps_t_b, wk)
            while pend_T:
                norm_block_T(nc, P, C, pend_T.pop(0), hnT, hnT_b, ps_t, ps_t_b, wk)
        P.barrier()
    return fin


def build_program(lp=LP, n_out=SEQ, mode="full"):
    nc = bass.Bass("TRN2", target_bir_lowering=False)
    A = Ctx()

    def din(name, shape):
        return nc.dram_tensor(name, shape, F32, kind="ExternalInput").ap()
    A.out = nc.dram_tensor("out", [n_out, D], F32, kind="ExternalOutput").ap()
    A.w_up = din("w_up", [D, 2 * DFF]); A.w_down = din("w_down", [DFF, D])
    A.norm2_g = din("norm2_g", [D]); A.conv_w = din("conv_w", [3, DFF]); A.conv_b = din("conv_b", [DFF])
    if mode == "ffn":
        A.h1 = din("h1", [lp, D])
    else:
        A.h1 = nc.dram_tensor("h1", [lp, D], F32, kind="Internal").ap()
        A.h0 = din("h0", [lp, D])
        A.norm1_g = din("norm1_g", [D]); A.w_in = din("w_in", [D, 4096])
        A.ssm_a_re = din("ssm_a_re", [32, 64]); A.ssm_a_im = din("ssm_a_im", [32, 64]); A.ssm_log_dt = din("ssm_log_dt", [32])
        A.ssm_b_re = din("ssm_b_re", [32, 64, 16]); A.ssm_b_im = din("ssm_b_im", [32, 64, 16])
        A.ssm_c_re = din("ssm_c_re", [32, 16, 64]); A.ssm_c_im = din("ssm_c_im", [32, 16, 64])
        A.ssm_d = din("ssm_d", [32, 16]); A.ssm_glu_w = din("ssm_glu_w", [512, 512]); A.ssm_glu_b = din("ssm_glu_b", [512])
        A.q_norm_g = din("q_norm_g", [64]); A.k_norm_g = din("k_norm_g", [64])
        for nm in ("lam_q1", "lam_k1", "lam_q2", "lam_k2"):
            setattr(A, nm, din(nm, [64]))
        A.subln_g = din("subln_g", [128])
        A.w_ssm_out = din("w_ssm_out", [512, D]); A.w_att_out = din("w_att_out", [512, D]); A.w_o = din("w_o", [D, D])
        if mode == "dbg":
            A.ysT = nc.dram_tensor("ysT", [512, lp], BF16, kind="ExternalOutput").ap()
            A.yaT = nc.dram_tensor("yaT", [512, lp], BF16, kind="ExternalOutput").ap()
        else:
            A.ysT = nc.dram_tensor("ysT", [512, lp], BF16, kind="Internal").ap()
            A.yaT = nc.dram_tensor("yaT", [512, lp], BF16, kind="Internal").ap()
    with contextlib.ExitStack() as st:
        P = Prog(nc, st)
        A.ysT_b = P.buf("ysT"); A.yaT_b = P.buf("yaT"); A.h1_b = P.buf("h1")
        wup_pre = None
        if mode != "ffn":
            phase_mixers(nc, P, A, lp)
            wup = st.enter_context(nc.sbuf_tensor("wup_shared", [128, 8, 2 * DFF], BF16)); wup_b = P.buf("wup")
            wup_pre = (wup, wup_b)
            phase_merge(nc, P, A, lp, prefetch=lambda: load_weight_bf16(nc, P, wup, wup_b, A.w_up, 8, 2 * DFF))
        fin = phase_ffn(nc, P, A, lp, n_out, wup_pre=wup_pre)
        if mode == "dbg":
            fin = fin + [A.ysT_b.last_w, A.yaT_b.last_w]
        P.finish(fin)
    return nc


_NC_CACHE = {}


def kernel(**inputs):
    x = np.asarray(inputs["x"], dtype=np.float32)
    meta = np.asarray(inputs["meta_tokens"], dtype=np.float32)
    bsz = x.shape[0]
    if "full" not in _NC_CACHE:
        _NC_CACHE["full"] = build_program(LP, SEQ, "full")
    nc = _NC_CACHE["full"]
    pad = np.zeros((LP - NMETA - SEQ, D), np.float32)
    shared = {}
    for k, v in inputs.items():
        if k in ("x", "meta_tokens"):
            continue
        shared[k] = np.ascontiguousarray(np.asarray(v, dtype=np.float32)[0])
    in_maps = []
    for b in range(bsz):
        m = dict(shared)
        m["h0"] = np.concatenate([meta, x[b], pad], axis=0)
        in_maps.append(m)
    res = run_bass_kernel_spmd(nc, in_maps, core_ids=list(range(bsz)))
    return np.stack([np.asarray(r["out"], dtype=np.float32) for r in res.results], axis=0)


def phase_merge(nc, P, A, lp, prefetch=None):
    ntile = lp // TB
    with contextlib.ExitStack() as st:
        C = Ctx()
        _common_consts(nc, P, st, C)
        sb = C.sb
        wg = sb("wg", [128, 8, 2048], BF16); wg_b = P.buf("wg")
        load_weight_bf16(nc, P, wg, wg_b, A.w_in, 8, 2048, col0=2048)
        fold_gain(nc, P, C, wg, wg_b, A.norm1_g, 8, 2048, "b")
        wso = sb("wso", [128, 4, D], BF16); wso_b = P.buf("wso")
        wao = sb("wao", [128, 4, D], BF16); wao_b = P.buf("wao")
        wo = sb("wo", [128, 8, D], BF16); wo_b = P.buf("wo")
        load_weight_bf16(nc, P, wso, wso_b, A.w_ssm_out, 4, D)
        for kt in range(4):
            P.op("dve", lambda e, kt=kt: e.tensor_scalar(out=wso[:, kt, :], in0=wso[:, kt, :], scalar1=0.25, scalar2=None, op0=ALU.mult),
                 reads=[wso_b], writes=[wso_b])
        load_weight_bf16(nc, P, wao, wao_b, A.w_att_out, 4, D)
        load_weight_bf16(nc, P, wo, wo_b, A.w_o, 8, D)
        if prefetch is not None:
            prefetch()
        xs = sb("xsb", [128, 3, D], F32); xs_b = P.bufs(3, "xsb")
        hnT = sb("hnTb", [128, 8, TB], BF16); hnT_b = P.buf("hnTb")
        wk = alloc_norm_work(P, sb, "b")
        gs = sb("gs", [128, 16, TB], BF16); gs_b = P.bufs(16, "gs")
        ys = sb("ys", [128, 4, TB], BF16); ys_b = P.buf("ys")
        ya = sb("ya", [128, 4, TB], BF16); ya_b = P.buf("ya")
        t1 = sb("t1", [128, TB], F32); t1_b = P.buf("t1")
        t2 = sb("t2", [128, TB], F32); t2_b = P.buf("t2")
        t1x = sb("t1x", [128, TB], F32); t1x_b = P.buf("t1x")
        t2x = sb("t2x", [128, TB], F32); t2x_b = P.buf("t2x")
        mx = sb("mx", [128, 8, TB], BF16); mx_b = P.bufs(8, "mx")
        ps_t = st.enter_context(nc.psum_tensor("ps_tb", [128, 8, 128], BF16)); ps_t_b = P.buf()
        psG = [st.enter_context(nc.psum_tensor("psG%d" % i, [128, 512], F32)) for i in range(2)]; psG_b = P.bufs(2)
        ps1 = st.enter_context(nc.psum_tensor("ps1", [128, 512], F32)); ps1_b = P.buf()
        ps2 = st.enter_context(nc.psum_tensor("ps2", [128, 512], F32)); ps2_b = P.buf()
        psO = [st.enter_context(nc.psum_tensor("psO%d" % i, [128, 512], F32)) for i in range(2)]; psO_b = P.bufs(2)
        for ti in range(ntile):
            c0 = ti * TB
            norm_tile(nc, P, C, A.h0, ti, 3, xs, xs_b, hnT, hnT_b, ps_t, ps_t_b, wk)
            P.dma(lambda e, c0=c0: e.dma_start(out=ys[:], in_=A.ysT[:, c0:c0 + TB].rearrange("(kt p) t -> p kt t", p=128)),
                  reads=[A.ysT_b], writes=[ys_b])
            P.dma(lambda e, c0=c0: e.dma_start(out=ya[:], in_=A.yaT[:, c0:c0 + TB].rearrange("(kt p) t -> p kt t", p=128)),
                  reads=[A.yaT_b], writes=[ya_b])
            for m in range(16):
                r = m % 2
                for kt in range(8):
                    P.op("pe", lambda e, kt=kt, m=m, r=r: e.matmul(psG[r][:, 0:TB], lhsT=wg[:, kt, m * 128:(m + 1) * 128],
                                                                   rhs=hnT[:, kt, :], start=(kt == 0), stop=(kt == 7)),
                         reads=[wg_b, hnT_b], writes=[psG_b[r]])
                P.op("act", lambda e, m=m, r=r: e.activation(out=gs[:, m, :], in_=psG[r][:, 0:TB], func=AF.Sigmoid),
                     reads=[psG_b[r]], writes=[gs_b[m]])
            for m in range(8):
                pa, pa_b = (ps1, ps1_b) if m % 2 == 0 else (psG[0], psG_b[0])
                pb, pb_b = (ps2, ps2_b) if m % 2 == 0 else (psG[1], psG_b[1])
                for kt in range(4):
                    P.op("pe", lambda e, kt=kt, m=m, pa=pa: e.matmul(pa[:, 0:TB], lhsT=wso[:, kt, m * 128:(m + 1) * 128], rhs=ys[:, kt, :],
                                                                     start=(kt == 0), stop=(kt == 3)),
                         reads=[wso_b, ys_b], writes=[pa_b])
                for kt in range(4):
                    P.op("pe", lambda e, kt=kt, m=m, pb=pb: e.matmul(pb[:, 0:TB], lhsT=wao[:, kt, m * 128:(m + 1) * 128], rhs=ya[:, kt, :],
                                                                     start=(kt == 0), stop=(kt == 3)),
                         reads=[wao_b, ya_b], writes=[pb_b])
                tA, tA_b = (t1, t1_b) if m % 2 == 0 else (t1x, t1x_b)
                tB, tB_b = (t2, t2_b) if m % 2 == 0 else (t2x, t2x_b)
                P.op("dve", lambda e, m=m, pa=pa, tA=tA: e.tensor_tensor(out=tA[:], in0=gs[:, m, :], in1=pa[:, 0:TB], op=ALU.mult),
                     reads=[gs_b[m], pa_b], writes=[tA_b])
                P.op("dve", lambda e, m=m, pb=pb, tB=tB: e.tensor_tensor(out=tB[:], in0=gs[:, 8 + m, :], in1=pb[:, 0:TB], op=ALU.mult),
                     reads=[gs_b[8 + m], pb_b], writes=[tB_b])
                P.op("pool", lambda e, m=m, tA=tA, tB=tB: e.tensor_tensor(out=mx[:, m, :], in0=tA[:], in1=tB[:], op=ALU.add),
                     reads=[tA_b, tB_b], writes=[mx_b[m]])
            for j in range(3):
                blk = ti * 3 + j
                for half in range(2):
                    for kt in range(8):
                        P.op("pe", lambda e, kt=kt, j=j, half=half: e.matmul(psO[half][:], lhsT=mx[:, kt, j * 128:(j + 1) * 128],
                                                                               rhs=wo[:, kt, half * 512:(half + 1) * 512],
                                                                               start=(kt == 0), stop=(kt == 7)),
                             reads=[mx_b[kt], wo_b], writes=[psO_b[half]])
                    P.op("dve", lambda e, j=j, half=half: e.tensor_tensor(out=xs[:, j, half * 512:(half + 1) * 512],
                                                                         in0=xs[:, j, half * 512:(half + 1) * 512],
                                                                         in1=psO[half][:], op=ALU.add),
                         reads=[xs_b[j], psO_b[half]], writes=[xs_b[j]])
                P.dma(lambda e, j=j, blk=blk: e.dma_start(out=A.h1[blk * 128:(blk + 1) * 128, :], in_=xs[:, j, :]),
                      reads=[xs_b[j]], writes=[A.h1_b])
        P.barrier()


NH = 4
SLOPES = [2.0 ** (-8.0 * (h + 1) / NH) for h in range(NH)]
LAM_INIT = 0.8 - 0.6 * math.exp(-0.3 * 0)
TS = 128


def s5_setup(nc, P, C, A, st, S):
    sb = C.sb
    tp = lambda n, sh, dt=F32: sb("s5_" + n, sh, dt)
    are = tp("are", [128, 16]); aim = tp("aim", [128, 16]); ldt = tp("ldt", [128, 16])
    b_in = P.buf("s5in")
    P.dma(lambda e: e.dma_start(out=are[:], in_=A.ssm_a_re.rearrange("(a s) p -> (s p) a", s=2), allow_slow_non_contiguous=True), writes=[b_in])
    P.dma(lambda e: e.dma_start(out=aim[:], in_=A.ssm_a_im.rearrange("(a s) p -> (s p) a", s=2), allow_slow_non_contiguous=True), writes=[b_in])
    l1 = tp("l1", [1, 32]); ones1 = tp("ones1", [1, 128]); b_l1 = P.buf()
    P.dma(lambda e: e.dma_start(out=l1[:], in_=A.ssm_log_dt.rearrange("(o g) -> o g", o=1)), writes=[b_l1])
    P.op("pool", lambda e: e.memset(ones1[:], 1.0), writes=[b_l1])
    psx = S.ps_misc; psx_b = S.ps_misc_b
    P.op("pe", lambda e: e.matmul(psx[:, 0:32], lhsT=ones1[:], rhs=l1[:], start=True, stop=True), reads=[b_l1], writes=[psx_b])
    lv = psx[:, 0:32].rearrange("q (a s) -> q a s", s=2)
    P.op("dve", lambda e: e.tensor_copy(out=ldt[0:64, :], in_=lv[0:64, :, 0]), reads=[psx_b], writes=[b_in])
    P.op("dve", lambda e: e.tensor_copy(out=ldt[64:128, :], in_=lv[64:128, :, 1]), reads=[psx_b], writes=[b_in])
    dt = tp("dt", [128, 16]); rho = tp("rho", [128, 16]); th = tp("th", [128, 16]); mag = tp("mag", [128, 16])
    cs = tp("cs", [128, 16]); sn = tp("sn", [128, 16]); tmpa = tp("tmpa", [128, 16]); tmpb = tp("tmpb", [128, 16])
    halfpi = tp("halfpi", [128, 1]); bq = P.buf("s5q")
    P.op("pool", lambda e: e.memset(halfpi[:], math.pi / 2), writes=[bq])
    P.op("act", lambda e: e.activation(out=dt[:], in_=ldt[:], func=AF.Exp), reads=[b_in], writes=[bq])
    P.op("dve", lambda e: e.tensor_tensor(out=rho[:], in0=are[:], in1=dt[:], op=ALU.mult), reads=[b_in, bq], writes=[bq])
    P.op("dve", lambda e: e.tensor_tensor(out=th[:], in0=aim[:], in1=dt[:], op=ALU.mult), reads=[b_in, bq], writes=[bq])
    P.op("act", lambda e: e.activation(out=mag[:], in_=rho[:], func=AF.Exp), reads=[bq], writes=[bq])
    P.op("act", lambda e: e.activation(out=sn[:], in_=th[:], func=AF.Sin, scale=1.0 / 16), reads=[bq], writes=[bq])
    P.op("act", lambda e: e.activation(out=cs[:], in_=th[:], func=AF.Sin, scale=1.0 / 16, bias=halfpi[:]), reads=[bq], writes=[bq])

    def cdouble(c, s_):
        P.op("dve", lambda e: e.tensor_tensor(out=tmpa[:], in0=c[:], in1=s_[:], op=ALU.mult), reads=[bq], writes=[bq])
        P.op("dve", lambda e: e.tensor_tensor(out=tmpb[:], in0=s_[:], in1=s_[:], op=ALU.mult), reads=[bq], writes=[bq])
        P.op("dve", lambda e: e.tensor_tensor(out=c[:], in0=c[:], in1=c[:], op=ALU.mult), reads=[bq], writes=[bq])
        P.op("dve", lambda e: e.tensor_tensor(out=c[:], in0=c[:], in1=tmpb[:], op=ALU.subtract), reads=[bq], writes=[bq])
        P.op("dve", lambda e: e.tensor_scalar(out=s_[:], in0=tmpa[:], scalar1=2.0, scalar2=None, op0=ALU.mult), reads=[bq], writes=[bq])
    for _ in range(4):
        cdouble(cs, sn)
    S.c1 = cs; S.s1 = sn; S.mag = mag; S.bq = bq
    lre = tp("lre", [128, 16]); lim = tp("lim", [128, 16]); den = tp("den", [128, 16])
    fre = tp("fre", [128, 16]); fim = tp("fim", [128, 16]); nfim = tp("nfim", [128, 16])
    P.op("dve", lambda e: e.tensor_tensor(out=lre[:], in0=mag[:], in1=cs[:], op=ALU.mult), reads=[bq], writes=[bq])
    P.op("dve", lambda e: e.tensor_tensor(out=lim[:], in0=mag[:], in1=sn[:], op=ALU.mult), reads=[bq], writes=[bq])
    P.op("dve", lambda e: e.tensor_tensor(out=den[:], in0=are[:], in1=are[:], op=ALU.mult), reads=[b_in, bq], writes=[bq])
    P.op("dve", lambda e: e.tensor_tensor(out=tmpa[:], in0=aim[:], in1=aim[:], op=ALU.mult), reads=[b_in, bq], writes=[bq])
    P.op("dve", lambda e: e.tensor_tensor(out=den[:], in0=den[:], in1=tmpa[:], op=ALU.add), reads=[bq], writes=[bq])
    P.op("dve", lambda e: e.reciprocal(out=den[:], in_=den[:]), reads=[bq], writes=[bq])
    P.op("dve", lambda e: e.tensor_scalar(out=tmpb[:], in0=lre[:], scalar1=-1.0, scalar2=None, op0=ALU.add), reads=[bq], writes=[bq])
    P.op("dve", lambda e: e.tensor_tensor(out=fre[:], in0=tmpb[:], in1=are[:], op=ALU.mult), reads=[bq, b_in], writes=[bq])
    P.op("dve", lambda e: e.tensor_tensor(out=tmpa[:], in0=lim[:], in1=aim[:], op=ALU.mult), reads=[bq, b_in], writes=[bq])
    P.op("dve", lambda e: e.tensor_tensor(out=fre[:], in0=fre[:], in1=tmpa[:], op=ALU.add), reads=[bq], writes=[bq])
    P.op("dve", lambda e: e.tensor_tensor(out=fre[:], in0=fre[:], in1=den[:], op=ALU.mult), reads=[bq], writes=[bq])
    P.op("dve", lambda e: e.tensor_tensor(out=fim[:], in0=lim[:], in1=are[:], op=ALU.mult), reads=[bq, b_in], writes=[bq])
    P.op("dve", lambda e: e.tensor_tensor(out=tmpa[:], in0=tmpb[:], in1=aim[:], op=ALU.mult), reads=[bq, b_in], writes=[bq])
    P.op("dve", lambda e: e.tensor_tensor(out=fim[:], in0=fim[:], in1=tmpa[:], op=ALU.subtract), reads=[bq], writes=[bq])
    P.op("dve", lambda e: e.tensor_tensor(out=fim[:], in0=fim[:], in1=den[:], op=ALU.mult), reads=[bq], writes=[bq])
    P.op("dve", lambda e: e.tensor_scalar(out=nfim[:], in0=fim[:], scalar1=-1.0, scalar2=None, op0=ALU.mult), reads=[bq], writes=[bq])
    al = getattr(S, "alias", {})
    bre = al["bre"] if "bre" in al else tp("bre", [128, 16, 16])[:]
    bim = al["bim"] if "bim" in al else tp("bim", [128, 16, 16])[:]
    bb = P.buf("s5b")
    P.dma(lambda e: e.dma_start(out=bre, in_=A.ssm_b_re.rearrange("(a s) p c -> (s p) a c", s=2)), writes=[bb])
    P.dma(lambda e: e.dma_start(out=bim, in_=A.ssm_b_im.rearrange("(a s) p c -> (s p) a c", s=2)), writes=[bb])
    wre = al["wre"] if "wre" in al else tp("wre", [128, 16, 128], BF16)[:]
    wim = al["wim"] if "wim" in al else tp("wim", [128, 16, 128], BF16)[:]
    bw = P.buf("s5w")
    P.op("pool", lambda e: e.memset(wre, 0.0), writes=[bw])
    P.op("pool", lambda e: e.memset(wim, 0.0), writes=[bw])
    tA, tB, BR, BI = [(al[nm] if nm in al else tp(nm, [128, 16, 16])[:]) for nm in ("tA", "tB", "BR", "BI")]
    bc3 = lambda t: t[:, :].unsqueeze(2).broadcast_to([128, 16, 16])
    P.op("dve", lambda e: e.tensor_tensor(out=tA, in0=bre, in1=bc3(fre), op=ALU.mult), reads=[bb, bq], writes=[bq])
    P.op("dve", lambda e: e.tensor_tensor(out=tB, in0=bim, in1=bc3(fim), op=ALU.mult), reads=[bb, bq], writes=[bq])
    P.op("dve", lambda e: e.tensor_tensor(out=BR, in0=tA, in1=tB, op=ALU.subtract), reads=[bq], writes=[bq])
    P.op("dve", lambda e: e.tensor_tensor(out=tA, in0=bim, in1=bc3(fre), op=ALU.mult), reads=[bb, bq], writes=[bq])
    P.op("dve", lambda e: e.tensor_tensor(out=tB, in0=bre, in1=bc3(fim), op=ALU.mult), reads=[bb, bq], writes=[bq])
    P.op("dve", lambda e: e.tensor_tensor(out=BI, in0=tA, in1=tB, op=ALU.add), reads=[bq], writes=[bq])
    for aq in range(4):
        for hf in range(2):
            c0 = 32 * aq + 16 * hf
            P.op("dve", lambda e, aq=aq, hf=hf, c0=c0: e.tensor_copy(out=wre[64 * hf:64 * hf + 64, aq::4, c0:c0 + 16], in_=BR[64 * hf:64 * hf + 64, aq::4, :]),
                 reads=[bq], writes=[bw])
            P.op("dve", lambda e, aq=aq, hf=hf, c0=c0: e.tensor_copy(out=wim[64 * hf:64 * hf + 64, aq::4, c0:c0 + 16], in_=BI[64 * hf:64 * hf + 64, aq::4, :]),
                 reads=[bq], writes=[bw])
    S.BBre = tp("BBre", [128, 16, 128], BF16); S.BBim = tp("BBim", [128, 16, 128], BF16); S.bBB = P.buf("BB")
    pst = S.ps_t; pst_b = S.ps_t_b
    for (src, dst) in ((wre, S.BBre), (wim, S.BBim)):
        for half in range(2):
            for a8 in range(8):
                a = half * 8 + a8
                P.op("pe", lambda e, a=a, a8=a8, src=src: e.transpose(out=pst[:, a8, :], in_=src[:, a, :], identity=C.ident[:]),
                     reads=[bw, C.b_ident], writes=[pst_b])
            P.op("dve", lambda e, half=half, dst=dst: e.tensor_copy(out=dst[:, half * 8:(half + 1) * 8, :], in_=pst[:, :, :]),
                 reads=[pst_b], writes=[S.bBB])
    cre = al["cre"] if "cre" in al else tp("cre", [128, 16, 16])[:]
    cim = al["cim"] if "cim" in al else tp("cim", [128, 16, 16])[:]
    bc = P.buf("s5c")
    for s_ in range(2):
        for a in range(16):
            P.dma(lambda e, s_=s_, a=a: e.dma_start(out=cre[64 * s_:64 * (s_ + 1), a, :], in_=A.ssm_c_re[2 * a + s_].rearrange("c p -> p c"),
                                                    allow_slow_non_contiguous=True), writes=[bc])
            P.dma(lambda e, s_=s_, a=a: e.dma_start(out=cim[64 * s_:64 * (s_ + 1), a, :], in_=A.ssm_c_im[2 * a + s_].rearrange("c p -> p c"),
                                                    allow_slow_non_contiguous=True), writes=[bc])
    S.CWre = tp("CWre", [128, 16, 32], BF16); S.CWim = tp("CWim", [128, 16, 32], BF16); S.bCW = P.buf("CW")
    P.op("pool", lambda e: e.memset(S.CWre[:], 0.0), writes=[S.bCW])
    P.op("pool", lambda e: e.memset(S.CWim[:], 0.0), writes=[S.bCW])
    P.op("dve", lambda e: e.tensor_copy(out=S.CWre[0:64, :, 0:16], in_=cre[0:64, :, :]), reads=[bc], writes=[S.bCW])
    P.op("dve", lambda e: e.tensor_copy(out=S.CWre[64:128, :, 16:32], in_=cre[64:128, :, :]), reads=[bc], writes=[S.bCW])
    P.op("dve", lambda e: e.tensor_scalar(out=S.CWim[0:64, :, 0:16], in0=cim[0:64, :, :], scalar1=-1.0, scalar2=None, op0=ALU.mult),
         reads=[bc], writes=[S.bCW])
    P.op("dve", lambda e: e.tensor_scalar(out=S.CWim[64:128, :, 16:32], in0=cim[64:128, :, :], scalar1=-1.0, scalar2=None, op0=ALU.mult),
         reads=[bc], writes=[S.bCW])
    S.dcol = tp("dcol", [128, 4]); S.b_d = P.buf("dcol")
    P.dma(lambda e: e.dma_start(out=S.dcol[:], in_=A.ssm_d.rearrange("(ct g) c -> (g c) ct", ct=4), allow_slow_non_contiguous=True), writes=[S.b_d])
    S.cosT = tp("cosT", [128, 16, TS]); S.sinT = tp("sinT", [128, 16, TS]); S.R = tp("R", [128, 16, TS]); S.bT = P.buf("tables")
    P.op("pool", lambda e: e.memset(S.cosT[:, :, 0:1], 1.0), writes=[S.bT])
    P.op("pool", lambda e: e.memset(S.sinT[:, :, 0:1], 0.0), writes=[S.bT])
    pc = tp("pc", [128, 16]); ps_ = tp("ps", [128, 16])
    P.op("dve", lambda e: e.tensor_copy(out=pc[:], in_=cs[:]), reads=[bq], writes=[bq])
    P.op("dve", lambda e: e.tensor_copy(out=ps_[:], in_=sn[:]), reads=[bq], writes=[bq])
    tt = al["tt"] if "tt" in al else tp("tt", [128, 16, TS // 2])[:]
    n = 1
    while n < TS:
        pcb = pc[:, :].unsqueeze(2).broadcast_to([128, 16, n]); psb = ps_[:, :].unsqueeze(2).broadcast_to([128, 16, n])
        sc = S.cosT[:, :, 0:n]; ss_ = S.sinT[:, :, 0:n]; dc = S.cosT[:, :, n:2 * n]; ds = S.sinT[:, :, n:2 * n]; t_ = tt[:, :, 0:n]
        P.op("dve", lambda e, dc=dc, sc=sc, pcb=pcb: e.tensor_tensor(out=dc, in0=sc, in1=pcb, op=ALU.mult), reads=[S.bT, bq], writes=[S.bT])
        P.op("dve", lambda e, t_=t_, ss_=ss_, psb=psb: e.tensor_tensor(out=t_, in0=ss_, in1=psb, op=ALU.mult), reads=[S.bT, bq], writes=[bq])
        P.op("dve", lambda e, dc=dc, t_=t_: e.tensor_tensor(out=dc, in0=dc, in1=t_, op=ALU.subtract), reads=[S.bT, bq], writes=[S.bT])
        P.op("dve", lambda e, ds=ds, ss_=ss_, pcb=pcb: e.tensor_tensor(out=ds, in0=ss_, in1=pcb, op=ALU.mult), reads=[S.bT, bq], writes=[S.bT])
        P.op("dve", lambda e, t_=t_, sc=sc, psb=psb: e.tensor_tensor(out=t_, in0=sc, in1=psb, op=ALU.mult), reads=[S.bT, bq], writes=[bq])
        P.op("dve", lambda e, ds=ds, t_=t_: e.tensor_tensor(out=ds, in0=ds, in1=t_, op=ALU.add), reads=[S.bT, bq], writes=[S.bT])
        cdouble(pc, ps_)
        n *= 2
    S.cN = pc; S.sN = ps_
    P.op("pool", lambda e: e.tensor_copy(out=S.R[:], in_=mag[:, :].unsqueeze(2).broadcast_to([128, 16, TS])), reads=[bq], writes=[S.bT])
    S.init_re = tp("init_re", [128, 16]); S.init_im = tp("init_im", [128, 16]); S.b_init = P.bufs(4, "init")
    P.op("pool", lambda e: e.memset(S.init_re[:], 0.0), writes=S.b_init)
    P.op("pool", lambda e: e.memset(S.init_im[:], 0.0), writes=S.b_init)


def phase_mixers(nc, P, A, lp):
    ntile = lp // TB
    nblk = lp // 128
    with contextlib.ExitStack() as st:
        C = Ctx()
        _common_consts(nc, P, st, C)
        sb = C.sb
        S = Ctx()
        ps_t = st.enter_context(nc.psum_tensor("ps_ta", [128, 8, 128], BF16)); ps_t_b = P.buf()
        psP = [st.enter_context(nc.psum_tensor("psP%d" % i, [128, 512], F32)) for i in range(2)]; psP_b = P.bufs(2)
        psQ = st.enter_context(nc.psum_tensor("psQ", [128, 512], F32)); psQ_b = P.buf("psQ"); psQr_b = [psQ_b, psQ_b, psQ_b]
        psS = [st.enter_context(nc.psum_tensor("psS%d" % i, [128, 512], F32)) for i in range(2)]; psS_b = P.bufs(2)
        psAccF = [st.enter_context(nc.psum_tensor("psAcc%d" % i, [128, 512], F32)) for i in range(2)]; psAcc_b = P.bufs(2)
        psAcc = [t[:, 0:387].rearrange("q (s e) -> q s e", s=3) for t in psAccF]
        S.ps_misc = psQ; S.ps_misc_b = psQ_b; S.ps_t = ps_t; S.ps_t_b = ps_t_b
        khist = sb("khist", [128, NH, lp], BF16); kh_b = P.bufs(ntile, "kh")
        vhist = sb("vhist", [128, nblk, NH, 129], BF16); vh_b = P.bufs(ntile, "vh")
        WS = []
        for k in range(2):
            W = Ctx()
            for nm in ("xtr", "xti", "tm1", "tm2", "wr", "wi"):
                setattr(W, nm, sb("%s%d" % (nm, k), [128, 4, TS], F32)); setattr(W, nm + "_b", P.buf())
            for nm in ("Sr", "Si"):
                setattr(W, nm, sb("%s%d" % (nm, k), [128, 4, TS], BF16)); setattr(W, nm + "_b", P.buf())
            for nm in ("yv", "yq"):
                setattr(W, nm, sb("%s%d" % (nm, k), [128, TS], F32)); setattr(W, nm + "_b", P.buf())
            for nm in ("c4", "c4b"):
                setattr(W, nm, sb("%s%d" % (nm, k), [128, 4], F32)); setattr(W, nm + "_b", P.buf())
            WS.append(W)
        xs = sb("xsa", [128, 1, D], F32); xs_b = P.bufs(1, "xsa")
        S.alias = {}
        S.alias["tt"] = xs[:, 0, :].rearrange("q (a k) -> q a k", a=16)
        if lp >= 2048:
            S.alias["wre"] = khist[:, 0, 0:2048].rearrange("q (a c) -> q a c", a=16)
            S.alias["wim"] = khist[:, 1, 0:2048].rearrange("q (a c) -> q a c", a=16)
        for nm, t in (("bre", WS[0].xtr), ("bim", WS[0].xti), ("cre", WS[0].tm1), ("cim", WS[0].tm2)):
            S.alias[nm] = t[:, 0:2, :].rearrange("q i (x c) -> q (i x) c", c=16)
        for nm, t in (("tA", WS[1].xtr), ("tB", WS[1].xti), ("BR", WS[1].tm1), ("BI", WS[1].tm2)):
            S.alias[nm] = t[:, 0:2, :].rearrange("q i (x c) -> q (i x) c", c=16)
        wa = sb("wa", [128, 8, 2048], BF16); wa_b = P.buf("wa")
        load_weight_bf16(nc, P, wa, wa_b, A.w_in, 8, 2048, col0=0)
        glu = sb("glu", [128, 4, 512], BF16); glu_b = P.buf("glu")
        load_weight_bf16(nc, P, glu, glu_b, A.ssm_glu_w, 4, 512)
        s5_setup(nc, P, C, A, st, S)
        P.barrier()
        P.op("pool", lambda e: e.memset(vhist[:, :, :, 128:129], 1.0), writes=vh_b)
        fold_gain(nc, P, C, wa, wa_b, A.norm1_g, 8, 2048, "a")
        for kt in range(4):
            P.op("pool", lambda e, kt=kt: e.tensor_scalar(out=glu[:, kt, :], in0=glu[:, kt, :], scalar1=0.5, scalar2=None, op0=ALU.mult),
                 reads=[glu_b], writes=[glu_b])
        glub = sb("glub", [128, 4], F32); misc_b = P.buf("misc")
        P.dma(lambda e: e.dma_start(out=glub[:], in_=A.ssm_glu_b.rearrange("(m p) -> p m", p=128), allow_slow_non_contiguous=True), writes=[misc_b])
        P.op("dve", lambda e: e.tensor_scalar(out=glub[:], in0=glub[:], scalar1=0.5, scalar2=None, op0=ALU.mult), reads=[misc_b], writes=[misc_b])
        gq = sb("gq", [128, 1], F32); gk = sb("gk", [128, 1], F32)
        for h2 in range(2):
            P.dma(lambda e, h2=h2: e.dma_start(out=gq[64 * h2:64 * h2 + 64, :], in_=A.q_norm_g.rearrange("(p o) -> p o", o=1)), writes=[misc_b])
            P.dma(lambda e, h2=h2: e.dma_start(out=gk[64 * h2:64 * h2 + 64, :], in_=A.k_norm_g.rearrange("(p o) -> p o", o=1)), writes=[misc_b])
        P.op("dve", lambda e: e.tensor_scalar(out=gq[:], in0=gq[:], scalar1=64 ** -0.5, scalar2=None, op0=ALU.mult), reads=[misc_b], writes=[misc_b])
        gsub = sb("gsub", [128, 1], F32)
        P.dma(lambda e: e.dma_start(out=gsub[:], in_=A.subln_g.rearrange("(p o) -> p o", o=1)), writes=[misc_b])
        P.op("dve", lambda e: e.tensor_scalar(out=gsub[:], in0=gsub[:], scalar1=1.0 - LAM_INIT, scalar2=None, op0=ALU.mult), reads=[misc_b], writes=[misc_b])
        lq = sb("lq", [1, 4, 64], F32); lpr = sb("lpr", [1, 2, 64], F32); lsum = sb("lsum", [1, 2], F32); lam1 = sb("lam1", [1, 2], F32)
        ones1 = sb("ones1a", [1, 128], F32); nlam = sb("nlam", [128, 1], F32); lam_b = P.buf("lam")
        for i, nm in enumerate(("lam_q1", "lam_k1", "lam_q2", "lam_k2")):
            P.dma(lambda e, i=i, nm=nm: e.dma_start(out=lq[:, i, :], in_=getattr(A, nm).rearrange("(o d) -> o d", o=1)), writes=[lam_b])
        P.op("pool", lambda e: e.memset(ones1[:], 1.0), writes=[lam_b])
        P.op("dve", lambda e: e.tensor_tensor(out=lpr[:, 0, :], in0=lq[:, 0, :], in1=lq[:, 1, :], op=ALU.mult), reads=[lam_b], writes=[lam_b])
        P.op("dve", lambda e: e.tensor_tensor(out=lpr[:, 1, :], in0=lq[:, 2, :], in1=lq[:, 3, :], op=ALU.mult), reads=[lam_b], writes=[lam_b])
        P.op("dve", lambda e: e.tensor_reduce(out=lsum[:], in_=lpr[:], axis=AX.X, op=ALU.add), reads=[lam_b], writes=[lam_b])
        P.op("act", lambda e: e.activation(out=lsum[:], in_=lsum[:], func=AF.Exp), reads=[lam_b], writes=[lam_b])
        P.op("dve", lambda e: e.tensor_tensor(out=lam1[:, 0:1], in0=lsum[:, 1:2], in1=lsum[:, 0:1], op=ALU.subtract), reads=[lam_b], writes=[lam_b])
        P.op("dve", lambda e: e.tensor_tensor(out=lam1[:, 1:2], in0=lsum[:, 1:2], in1=lsum[:, 0:1], op=ALU.subtract), reads=[lam_b], writes=[lam_b])
        P.op("dve", lambda e: e.tensor_scalar(out=lam1[:], in0=lam1[:], scalar1=-LAM_INIT, scalar2=None, op0=ALU.add), reads=[lam_b], writes=[lam_b])
        P.op("pe", lambda e: e.matmul(psQ[:, 0:2], lhsT=ones1[:], rhs=lam1[:], start=True, stop=True), reads=[lam_b], writes=[psQ_b])
        P.op("dve", lambda e: e.tensor_copy(out=nlam[:], in_=psQ[:, 0:1]), reads=[psQ_b], writes=[lam_b])
        bones = sb("bones", [128, 128], BF16)
        P.op("pool", lambda e: e.memset(bones[:], 0.0), writes=[misc_b])
        P.op("pool", lambda e: e.memset(bones[0:64, 0:64], 1.0), writes=[misc_b])
        P.op("pool", lambda e: e.memset(bones[64:128, 64:128], 1.0), writes=[misc_b])
        cmask = sb("cmask", [128, 128], BF16)
        P.op("pool", lambda e: e.memset(cmask[:], -30000.0), writes=[misc_b])
        P.op("pool", lambda e: e.affine_select(out=cmask[:], in_=cmask[:], pattern=[[-1, 128]], compare_op=ALU.is_gt, fill=0.0,
                                               base=0, channel_multiplier=1), reads=[misc_b], writes=[misc_b])
        kidx = sb("kidx", [128, 1], F32)
        P.op("pool", lambda e: e.iota(kidx[:], pattern=[[0, 1]], base=0, channel_multiplier=1, allow_small_or_imprecise_dtypes=True), writes=[misc_b])
        abias = sb("abias", [128, NH, nblk], F32)
        for h in range(NH):
            for dl in range(nblk):
                P.op("pool", lambda e, h=h, dl=dl: e.tensor_scalar(out=abias[:, h, dl:dl + 1], in0=kidx[:], scalar1=SLOPES[h],
                                                                   scalar2=-SLOPES[h] * 128.0 * dl, op0=ALU.mult, op1=ALU.add),
                     reads=[misc_b], writes=[misc_b])
        hnT = sb("hnTa", [128, 8, TB], BF16); hnT_b = P.buf("hnTa")
        wk = alloc_norm_work(P, sb, "a")
        uT = sb("uT", [128, 4, TB], BF16); uT_b = P.buf("uT")
        qz = [sb("qz%d" % c, [128, NH, TB], BF16) for c in range(2)]; qT_b = P.bufs(NH, "qT")
        P.op("pool", lambda e: e.memset(qz[0][:], 0.0), writes=qT_b)
        P.op("pool", lambda e: e.memset(qz[1][:], 0.0), writes=qT_b)
        sqb = sb("sqb", [128, TB], BF16); sqb_b = P.buf("sqb")
        lnv = sb("lnv", [128, TB], F32); lnv_b = P.buf("lnv")
        lnv3 = lnv[:, 0:384].rearrange("q (s e) -> q s e", s=3)
        ET = [sb("ET%d" % i, [128, TB], BF16) for i in range(3)]; ET_b = P.bufs(3, "ET")
        osb3 = sb("osb3", [128, 3, 128], F32); osb_b = P.buf("osb")
        onb3 = sb("onb3", [128, 3, 128], BF16); onb_b = P.buf("onb")
        rz3 = sb("rz3", [128, 4, 3], F32); rz_b = P.buf("rz")
        yaT = sb("yaTt", [128, NH, TB], BF16); yaT_b = P.buf("yaTt")
        ysT = sb("ysTt", [128, 4, TB], BF16); ysT_b = P.buf("ysTt")
        ygT = sb("ygT", [128, 4, TS], BF16); ygT_b = P.bufs(4, "ygT")
        sg = sb("sg", [128, TS], F32); sg_b = P.buf("sg")
        for ti in range(ntile):
            c0 = ti * TB
            for j in range(3):
                norm_tile(nc, P, C, A.h0, ti, 1, xs, xs_b, hnT, hnT_b, ps_t, ps_t_b, wk, blk0=ti * 3 + j, col0=j * 128)
            for m in range(12):
                r = m % 2
                for kt in range(8):
                    P.op("pe", lambda e, kt=kt, m=m, r=r: e.matmul(psP[r][:, 0:TB], lhsT=wa[:, kt, m * 128:(m + 1) * 128], rhs=hnT[:, kt, :],
                                                                   start=(kt == 0), stop=(kt == 7)), reads=[wa_b, hnT_b], writes=[psP_b[r]])
                if m < 4:
                    P.op("act", lambda e, m=m, r=r: e.copy(out=uT[:, m, :], in_=psP[r][:, 0:TB]), reads=[psP_b[r]], writes=[uT_b])
                    continue
                h = (m - 4) % 4
                isq = m < 8
                P.op("act", lambda e, r=r: e.activation(out=sqb[:], in_=psP[r][:, 0:TB], func=AF.Square), reads=[psP_b[r]], writes=[sqb_b])
                P.op("pe", lambda e: e.matmul(psQ[:, 0:TB], lhsT=bones[:], rhs=sqb[:], start=True, stop=True), reads=[misc_b, sqb_b], writes=[psQ_b])
                P.op("act", lambda e: e.activation(out=lnv[:], in_=psQ[:, 0:TB], func=AF.Ln, bias=C.epsc[:], scale=1.0 / 64),
                     reads=[psQ_b, C.b_eps], writes=[lnv_b])
                P.op("act", lambda e: e.activation(out=lnv[:], in_=lnv[:], func=AF.Exp, scale=-0.5), reads=[lnv_b], writes=[lnv_b])
                if isq:
                    for c in range(2):
                        P.op("dve", lambda e, h=h, r=r, c=c: e.scalar_tensor_tensor(out=qz[c][64 * c:64 * c + 64, h, :], in0=psP[r][64 * c:64 * c + 64, 0:TB],
                                                                                   scalar=gq[64 * c:64 * c + 64, 0:1], in1=lnv[64 * c:64 * c + 64, :],
                                                                                   op0=ALU.mult, op1=ALU.mult),
                             reads=[psP_b[r], lnv_b, misc_b], writes=[qT_b[h]])
                else:
                    P.op("dve", lambda e, h=h, r=r, c0=c0: e.scalar_tensor_tensor(out=khist[:, h, c0:c0 + TB], in0=psP[r][:, 0:TB], scalar=gk[:, 0:1],
                                                                                 in1=lnv[:], op0=ALU.mult, op1=ALU.mult),
                         reads=[psP_b[r], lnv_b, misc_b], writes=[kh_b[ti]])
            for j in range(3):
                r = j % 2
                for kt in range(8):
                    P.op("pe", lambda e, kt=kt, j=j, r=r: e.matmul(psP[r][:], lhsT=hnT[:, kt, j * 128:(j + 1) * 128], rhs=wa[:, kt, 1536:2048],
                                                                   start=(kt == 0), stop=(kt == 7)), reads=[wa_b, hnT_b], writes=[psP_b[r]])
                P.op("act", lambda e, j=j, r=r, ti=ti: e.copy(out=vhist[:, ti * 3 + j, :, 0:128], in_=psP[r][:].rearrange("q (h e) -> q h e", h=NH)),
                     reads=[psP_b[r]], writes=[vh_b[ti]])
            NCH = (TB // TS) * 4

            def s5_X(k):
                sub, ct = divmod(k, 4); cs_ = sub * TS
                for i in range(4):
                    a = 4 * ct + i
                    P.op("pe", lambda e, a=a, i=i, ct=ct, cs_=cs_: e.matmul(psP[0][:, i * TS:(i + 1) * TS], lhsT=S.BBre[:, a, :], rhs=uT[:, ct, cs_:cs_ + TS],
                                                                             start=True, stop=True), reads=[S.bBB, uT_b], writes=[psP_b[0]])
                    P.op("pe", lambda e, a=a, i=i, ct=ct, cs_=cs_: e.matmul(psP[1][:, i * TS:(i + 1) * TS], lhsT=S.BBim[:, a, :], rhs=uT[:, ct, cs_:cs_ + TS],
                                                                             start=True, stop=True), reads=[S.bBB, uT_b], writes=[psP_b[1]])

            def s5_rot_in(k):
                sub, ct = divmod(k, 4); W = WS[k % 2]
                Xr = psP[0][:].rearrange("q (i t) -> q i t", i=4); Xi = psP[1][:].rearrange("q (i t) -> q i t", i=4)
                cT = S.cosT[:, 4 * ct:4 * ct + 4, :]; sT = S.sinT[:, 4 * ct:4 * ct + 4, :]
                P.op("dve", lambda e: e.tensor_tensor(out=W.xtr[:], in0=Xr, in1=cT, op=ALU.mult), reads=[psP_b[0], S.bT], writes=[W.xtr_b])
                P.op("dve", lambda e: e.tensor_tensor(out=W.tm1[:], in0=Xi, in1=sT, op=ALU.mult), reads=[psP_b[1], S.bT], writes=[W.tm1_b])
                P.op("dve", lambda e: e.tensor_tensor(out=W.xti[:], in0=Xi, in1=cT, op=ALU.mult), reads=[psP_b[1], S.bT], writes=[W.xti_b])
                P.op("dve", lambda e: e.tensor_tensor(out=W.tm2[:], in0=Xr, in1=sT, op=ALU.mult), reads=[psP_b[0], S.bT], writes=[W.tm2_b])

            def s5_scan(k):
                sub, ct = divmod(k, 4); W = WS[k % 2]
                cT = S.cosT[:, 4 * ct:4 * ct + 4, :]; sT = S.sinT[:, 4 * ct:4 * ct + 4, :]
                P.op("dve", lambda e: e.tensor_tensor(out=W.xtr[:], in0=W.xtr[:], in1=W.tm1[:], op=ALU.add), reads=[W.xtr_b, W.tm1_b], writes=[W.xtr_b])
                P.op("dve", lambda e: e.tensor_tensor(out=W.xti[:], in0=W.xti[:], in1=W.tm2[:], op=ALU.subtract), reads=[W.xti_b, W.tm2_b], writes=[W.xti_b])
                for i in range(4):
                    a = 4 * ct + i
                    P.op("dve", lambda e, a=a, i=i: e.tensor_tensor_scan(out=W.wr[:, i, :], data0=S.R[:, a, :], data1=W.xtr[:, i, :],
                                                                         initial=S.init_re[:, a:a + 1], op0=ALU.mult, op1=ALU.add),
                         reads=[S.bT, W.xtr_b, S.b_init[ct]], writes=[W.wr_b])
                    P.op("dve", lambda e, a=a, i=i: e.tensor_tensor_scan(out=W.wi[:, i, :], data0=S.R[:, a, :], data1=W.xti[:, i, :],
                                                                         initial=S.init_im[:, a:a + 1], op0=ALU.mult, op1=ALU.add),
                         reads=[S.bT, W.xti_b, S.b_init[ct]], writes=[W.wi_b])
                P.op("pool", lambda e: e.tensor_tensor(out=W.tm1[:], in0=W.wi[:], in1=cT, op=ALU.mult), reads=[W.wi_b, S.bT], writes=[W.tm1_b])
                P.op("pool", lambda e: e.tensor_tensor(out=W.tm2[:], in0=W.wr[:], in1=sT, op=ALU.mult), reads=[W.wr_b, S.bT], writes=[W.tm2_b])
                P.op("dve", lambda e: e.tensor_tensor(out=W.xtr[:], in0=W.wr[:], in1=cT, op=ALU.mult), reads=[W.wr_b, S.bT], writes=[W.xtr_b])
                P.op("dve", lambda e: e.tensor_tensor(out=W.xti[:], in0=W.wi[:], in1=sT, op=ALU.mult), reads=[W.wi_b, S.bT], writes=[W.xti_b])
                a0 = 4 * ct
                wl_r = W.wr[:, :, TS - 1]; wl_i = W.wi[:, :, TS - 1]
                P.op("dve", lambda e: e.tensor_tensor(out=W.c4[:], in0=wl_r, in1=S.cN[:, a0:a0 + 4], op=ALU.mult), reads=[W.wr_b, S.bq], writes=[W.c4_b])
                P.op("dve", lambda e: e.tensor_tensor(out=W.c4b[:], in0=wl_i, in1=S.sN[:, a0:a0 + 4], op=ALU.mult), reads=[W.wi_b, S.bq], writes=[W.c4b_b])
                P.op("dve", lambda e: e.tensor_tensor(out=S.init_re[:, a0:a0 + 4], in0=W.c4[:], in1=W.c4b[:], op=ALU.subtract), reads=[W.c4_b, W.c4b_b], writes=[S.b_init[ct]])
                P.op("dve", lambda e: e.tensor_tensor(out=W.c4[:], in0=wl_r, in1=S.sN[:, a0:a0 + 4], op=ALU.mult), reads=[W.wr_b, S.bq], writes=[W.c4_b])
                P.op("dve", lambda e: e.tensor_tensor(out=W.c4b[:], in0=wl_i, in1=S.cN[:, a0:a0 + 4], op=ALU.mult), reads=[W.wi_b, S.bq], writes=[W.c4b_b])
                P.op("dve", lambda e: e.tensor_tensor(out=S.init_im[:, a0:a0 + 4], in0=W.c4[:], in1=W.c4b[:], op=ALU.add), reads=[W.c4_b, W.c4b_b], writes=[S.b_init[ct]])
                P.op("dve", lambda e: e.tensor_tensor(out=W.Sr[:], in0=W.xtr[:], in1=W.xti[:], op=ALU.subtract), reads=[W.xtr_b, W.xti_b], writes=[W.Sr_b])
                P.op("dve", lambda e: e.tensor_tensor(out=W.Si[:], in0=W.tm1[:], in1=W.tm2[:], op=ALU.add), reads=[W.tm1_b, W.tm2_b], writes=[W.Si_b])

            def s5_Y(k):
                sub, ct = divmod(k, 4); W = WS[k % 2]; q = k % 2
                for i in range(4):
                    a = 4 * ct + i
                    P.op("pe", lambda e, a=a, i=i: e.matmul(psQ[32 * i:32 * i + 32, 0:TS], lhsT=S.CWre[:, a, :], rhs=W.Sr[:, i, :], start=True, stop=False,
                                                            tile_position=(0, 32 * i), skip_group_check=True), reads=[S.bCW, W.Sr_b], writes=[psQr_b[q]])
                    P.op("pe", lambda e, a=a, i=i: e.matmul(psQ[32 * i:32 * i + 32, 0:TS], lhsT=S.CWim[:, a, :], rhs=W.Si[:, i, :], start=False, stop=True,
                                                            tile_position=(0, 32 * i), skip_group_check=True), reads=[S.bCW, W.Si_b], writes=[psQr_b[q]])

            def s5_gelu(k):
                sub, ct = divmod(k, 4); W = WS[k % 2]; q = k % 2; cs_ = sub * TS
                P.op("dve", lambda e: e.scalar_tensor_tensor(out=W.yv[:], in0=uT[:, ct, cs_:cs_ + TS], scalar=S.dcol[:, ct:ct + 1], in1=psQ[:, 0:TS],
                                                             op0=ALU.mult, op1=ALU.add), reads=[uT_b, S.b_d, psQr_b[q]], writes=[W.yv_b])
                P.op("dve", lambda e: e.tensor_tensor(out=W.yq[:], in0=W.yv[:], in1=W.yv[:], op=ALU.mult), reads=[W.yv_b], writes=[W.yq_b])
                P.op("dve", lambda e: e.scalar_tensor_tensor(out=W.yq[:], in0=W.yq[:], scalar=1.0 / GC, in1=W.yv[:], op0=ALU.add, op1=ALU.mult),
                     reads=[W.yq_b, W.yv_b], writes=[W.yq_b])
                P.op("act", lambda e: e.activation(out=W.yq[:], in_=W.yq[:], func=AF.Tanh, scale=GK * GC), reads=[W.yq_b], writes=[W.yq_b])
                P.op("dve", lambda e: e.scalar_tensor_tensor(out=ygT[:, ct, :], in0=W.yq[:], scalar=1.0, in1=W.yv[:], op0=ALU.add, op1=ALU.mult),
                     reads=[W.yq_b, W.yv_b], writes=[ygT_b[ct]])

            def emit_glu(sub, ti=ti):
                cs_ = sub * TS
                for m in range(4):
                    for kt in range(4):
                        P.op("pe", lambda e, kt=kt, m=m: e.matmul(psQ[:, 0:TS], lhsT=glu[:, kt, m * 128:(m + 1) * 128], rhs=ygT[:, kt, :],
                                                                  start=(kt == 0), stop=(kt == 3), skip_group_check=True), reads=[glu_b, ygT_b[kt]], writes=[psQr_b[2]])
                    P.op("act", lambda e, m=m: e.activation(out=sg[:], in_=psQ[:, 0:TS], func=AF.Tanh, bias=glub[:, m:m + 1], scale=0.5),
                         reads=[psQr_b[2], misc_b], writes=[sg_b])
                    P.op("dve", lambda e, m=m, cs_=cs_: e.scalar_tensor_tensor(out=ysT[:, m, cs_:cs_ + TS], in0=sg[:], scalar=1.0, in1=ygT[:, m, :],
                                                                              op0=ALU.add, op1=ALU.mult), reads=[ygT_b[m], sg_b], writes=[ysT_b])
            units = [(kb, c) for kb in range(3 * ti + 3) for c in range(2)]

            def emit_scores(h, n, ti=ti, units=units):
                kb, c = units[n]
                n0 = max(0, kb - 3 * ti)
                r = n % 2
                ncol = TB - n0 * 128
                diag = kb >= 3 * ti
                P.op("pe", lambda e, kb=kb, c=c, n0=n0, r=r, ncol=ncol, diag=diag: e.matmul(
                    psS[r][:, 0:ncol], lhsT=khist[:, h, kb * 128:(kb + 1) * 128],
                    rhs=qz[c][:, h, n0 * 128:TB], start=True, stop=(not diag)),
                    reads=[kh_b[kb // 3], qT_b[h]], writes=[psS_b[r]])
                if diag:
                    P.op("pe", lambda e, r=r: e.matmul(psS[r][:, 0:128], lhsT=C.ident[:], rhs=cmask[:], start=False, stop=True),
                         reads=[C.b_ident, misc_b], writes=[psS_b[r]])

            def emit_att(h, n_lo, n_hi, ti=ti, units=units):
                emit_scores(h, n_lo)
                for n in range(n_lo, n_hi):
                    if n + 1 < n_hi:
                        emit_scores(h, n + 1)
                    kb, c = units[n]
                    n0 = max(0, kb - 3 * ti)
                    r = n % 2; eb = n % 3
                    ncol = TB - n0 * 128
                    dl2 = 3 * ti + 2 - kb
                    P.op("act", lambda e, h=h, dl2=dl2, r=r, eb=eb, ncol=ncol: e.activation(
                        out=ET[eb][:, 0:ncol], in_=psS[r][:, 0:ncol], func=AF.Exp, bias=abias[:, h, dl2:dl2 + 1], scale=1.0),
                        reads=[psS_b[r], misc_b], writes=[ET_b[eb]])
                    for sbk in range(n0, 3):
                        o0 = (sbk - n0) * 128
                        last = (kb == 3 * ti + sbk)
                        P.op("pe", lambda e, h=h, kb=kb, c=c, sbk=sbk, eb=eb, o0=o0, st_=(kb == 0 and sbk == 0), last=last: e.matmul(
                            psAcc[c][:, sbk, :], lhsT=ET[eb][:, o0:o0 + 128], rhs=vhist[:, kb, h, :], start=st_, stop=last,
                            skip_group_check=True), reads=[ET_b[eb], vh_b[kb // 3]], writes=[psAcc_b[c]])
                if n_hi < len(units):
                    return
                accS = [xs[:, 0, 387 * c_:387 * (c_ + 1)].rearrange("q (s e) -> q s e", s=3) for c_ in range(2)]
                P.op("dve", lambda e: e.tensor_copy(out=accS[0], in_=psAcc[0]), reads=[psAcc_b[0]], writes=[xs_b[0]])
                P.op("act", lambda e: e.copy(out=accS[1], in_=psAcc[1]), reads=[psAcc_b[1]], writes=[xs_b[0]])
                bc = lambda col: rz3[:, col, :].unsqueeze(2).broadcast_to([128, 3, 128])
                P.op("dve", lambda e: e.reciprocal(out=rz3[:, 0, :], in_=accS[0][:, :, 128]), reads=[xs_b[0]], writes=[rz_b])
                P.op("dve", lambda e: e.reciprocal(out=rz3[:, 1, :], in_=accS[1][:, :, 128]), reads=[xs_b[0]], writes=[rz_b])
                P.op("dve", lambda e: e.tensor_scalar(out=rz3[:, 1, :], in0=rz3[:, 1, :], scalar1=nlam[:, 0:1], scalar2=None, op0=ALU.mult),
                     reads=[rz_b, lam_b], writes=[rz_b])
                P.op("dve", lambda e: e.tensor_tensor(out=osb3[:], in0=accS[0][:, :, 0:128], in1=bc(0), op=ALU.mult), reads=[xs_b[0], rz_b], writes=[osb_b])
                P.op("dve", lambda e: e.tensor_tensor(out=lnv3, in0=accS[1][:, :, 0:128], in1=bc(1), op=ALU.mult), reads=[xs_b[0], rz_b], writes=[lnv_b])
                P.op("dve", lambda e: e.tensor_tensor(out=osb3[:], in0=osb3[:], in1=lnv3, op=ALU.add), reads=[osb_b, lnv_b], writes=[osb_b])
                P.op("dve", lambda e: e.tensor_tensor(out=lnv3, in0=osb3[:], in1=osb3[:], op=ALU.mult), reads=[osb_b], writes=[lnv_b])
                P.op("dve", lambda e: e.tensor_reduce(out=rz3[:, 2, :], in_=lnv3, axis=AX.X, op=ALU.add), reads=[lnv_b], writes=[rz_b])
                P.op("act", lambda e: e.activation(out=rz3[:, 2, :], in_=rz3[:, 2, :], func=AF.Sqrt, bias=C.epsc[:], scale=1.0 / 128), reads=[rz_b, C.b_eps], writes=[rz_b])
                P.op("dve", lambda e: e.reciprocal(out=rz3[:, 3, :], in_=rz3[:, 2, :]), reads=[rz_b], writes=[rz_b])
                P.op("dve", lambda e: e.tensor_tensor(out=onb3[:], in0=osb3[:], in1=bc(3), op=ALU.mult), reads=[osb_b, rz_b], writes=[onb_b])
                for sbk in range(3):
                    P.op("pe", lambda e, sbk=sbk: e.transpose(out=ps_t[:, sbk, :], in_=onb3[:, sbk, :], identity=C.ident[:]), reads=[onb_b, C.b_ident], writes=[ps_t_b])
                P.op("dve", lambda e, h=h: e.tensor_scalar(out=yaT[:, h, :], in0=ps_t[:, 0:3, :], scalar1=gsub[:, 0:1], scalar2=None, op0=ALU.mult),
                     reads=[ps_t_b, misc_b], writes=[yaT_b])
            att_items = []
            nu = len(units)
            cuts = [0, nu // 3, (2 * nu) // 3, nu]
            for h in range(NH):
                for part in range(3):
                    att_items.append(lambda h=h, part=part: emit_att(h, cuts[part], cuts[part + 1]))
            s5_X(0)
            for k in range(NCH + 2):
                if k < NCH:
                    s5_rot_in(k)
                    if k + 1 < NCH:
                        s5_X(k + 1)
                    s5_scan(k)
                if k < len(att_items):
                    att_items[k]()
                if 2 <= k <= NCH + 1:
                    s5_gelu(k - 2)
                    if (k - 2) % 4 == 3:
                        emit_glu((k - 2) // 4)
                if 1 <= k <= NCH:
                    s5_Y(k - 1)
            for k in range(NCH + 2, len(att_items)):
                att_items[k]()
            P.dma(lambda e, c0=c0: e.dma_start(out=A.ysT[:, c0:c0 + TB].rearrange("(kt p) t -> p kt t", p=128), in_=ysT[:]),
                  reads=[ysT_b], writes=[A.ysT_b])
            P.dma(lambda e, c0=c0: e.dma_start(out=A.yaT[:, c0:c0 + TB].rearrange("(kt p) t -> p kt t", p=128), in_=yaT[:]),
                  reads=[yaT_b], writes=[A.yaT_b])
        P.barrier()
```

```python
import contextlib
import math
import numpy as np
import concourse.bass as bass
import concourse.mybir as mybir
from concourse.bass_utils import run_bass_kernel_spmd

F32 = mybir.dt.float32
BF16 = mybir.dt.bfloat16
AF = mybir.ActivationFunctionType
ALU = mybir.AluOpType
AX = mybir.AxisListType

D = 1024
SEQ = 4096
NMETA = 16
LP = 4224
NBLK = LP // 128
EPS = 1e-6
DFF = 2816
NFT = DFF // 128


class Buf:
    __slots__ = ("name", "last_w", "readers")

    def __init__(self, name):
        self.name = name
        self.last_w = None
        self.readers = {}


class Prog:
    ENGS = ("pe", "dve", "act", "pool", "sp")

    def __init__(self, nc, stack, n_dma_sems=20):
        self.nc = nc
        self.sems = {}
        for e in self.ENGS:
            self.sems[e] = stack.enter_context(nc.semaphore("s_" + e))
        self.dma_keys = []
        for i in range(n_dma_sems):
            k = "dma%d" % i
            self.sems[k] = stack.enter_context(nc.semaphore("s_" + k))
            self.dma_keys.append(k)
        self.cnt = {k: 0 for k in self.sems}
        self.seen = {e: {} for e in self.ENGS}
        self.lists = {e: [] for e in self.ENGS}
        self.dma_rr_map = {}
        self.nbuf = 0

    def buf(self, name=None):
        self.nbuf += 1
        return Buf(name or ("b%d" % self.nbuf))

    def bufs(self, n, name="b"):
        return [self.buf("%s%d" % (name, i)) for i in range(n)]

    def _wait(self, eng, key, val):
        if val <= 0:
            return
        if self.seen[eng].get(key, 0) >= val:
            return
        self.seen[eng][key] = val
        self.lists[eng].append(("wait", key, val))

    def _deps(self, eng, reads, writes):
        for b in reads:
            if b.last_w is not None:
                self._wait(eng, *b.last_w)
        for b in writes:
            if b.last_w is not None and not (b.last_w[0] == eng == "pe"):
                self._wait(eng, *b.last_w)
            for k, v in b.readers.items():
                if not (k == eng == "pe"):
                    self._wait(eng, k, v)

    def op(self, eng, fn, reads=(), writes=()):
        self._deps(eng, reads, writes)
        self.cnt[eng] += 1
        idx = self.cnt[eng]
        self.lists[eng].append(("op", fn, eng, 1))
        for b in reads:
            b.readers[eng] = idx
        for b in writes:
            b.last_w = (eng, idx)
            b.readers = {}
        return idx

    def dma(self, fn, reads=(), writes=(), eng="sp"):
        half = len(self.dma_keys) // 2
        pool_keys = self.dma_keys[:half] if eng == "pool" else self.dma_keys[half:]
        rr = self.dma_rr_map.get(eng == "pool", 0)
        self.dma_rr_map[eng == "pool"] = rr + 1
        key = pool_keys[rr % len(pool_keys)]
        self._deps(eng, reads, writes)
        self._wait(eng, key, self.cnt[key])
        self.cnt[key] += 16
        val = self.cnt[key]
        self.lists[eng].append(("op", fn, key, 16))
        for b in reads:
            b.readers[key] = val
        for b in writes:
            b.last_w = (key, val)
            b.readers = {}
        return key, val

    def barrier(self):
        for e in self.ENGS:
            for k, v in self.cnt.items():
                if k != e:
                    self._wait(e, k, v)

    def finish(self, final_waits):
        for k, v in final_waits:
            self._wait("sp", k, v)
        nc = self.nc
        handles = dict(pe=nc.tensor, dve=nc.vector, act=nc.scalar, pool=nc.gpsimd, sp=nc.sync)
        sems = self.sems
        lists = self.lists

        def replay(name):
            eng = handles[name]
            for it in lists[name]:
                if it[0] == "wait":
                    eng.wait_ge(sems[it[1]], it[2])
                else:
                    ins = it[1](eng)
                    ins.then_inc(sems[it[2]], it[3])

        with nc.Block() as block:
            @block.sync
            def _(e):
                replay("sp")

            @block.tensor
            def _(e):
                replay("pe")

            @block.vector
            def _(e):
                replay("dve")

            @block.scalar
            def _(e):
                replay("act")

            @block.gpsimd
            def _(e):
                replay("pool")


TB = 384
GK = math.sqrt(2.0 / math.pi)
GC = 0.044715


class Ctx:
    pass


_PHASE_ID = [0]


def _common_consts(nc, P, st, C):
    _PHASE_ID[0] += 1
    tag = "_p%d" % _PHASE_ID[0]

    def sb(name, shape, dt):
        return st.enter_context(nc.sbuf_tensor(name + tag, shape, dt))
    C.sb = sb
    C.ident = sb("ident", [128, 128], BF16)
    C.b_ident = P.buf("ident")
    P.op("pool", lambda e: e.memset(C.ident[:], 1.0), writes=[C.b_ident])
    P.op("pool", lambda e: e.affine_select(out=C.ident[:], in_=C.ident[:], pattern=[[-1, 128]],
                                           compare_op=ALU.is_equal, fill=0.0, base=0, channel_multiplier=1),
         reads=[C.b_ident], writes=[C.b_ident])
    C.epsc = sb("epsc", [128, 1], F32)
    C.b_eps = P.buf("eps")
    P.op("pool", lambda e: e.memset(C.epsc[:], EPS), writes=[C.b_eps])


def load_weight_bf16(nc, P, dst, dst_buf, src2d, kt_n, ncols, col0=0, scale_col=None, extra_scale=None):
    for kt in range(kt_n):
        c = 0
        while c < ncols:
            w = min(2048, ncols - c)
            P.dma(lambda e, kt=kt, c=c, w=w: e.dma_start(
                out=dst[:, kt, c:c + w], in_=src2d[kt * 128:(kt + 1) * 128, col0 + c:col0 + c + w]),
                writes=[dst_buf], eng="pool")
            c += w


def norm_tile(nc, P, C, src_rows, tile_i, nblk, xs, xs_b, hnT, hnT_b, ps_t, ps_t_b, wk, blk0=None, col0=0):
    for j in range(nblk):
        r0 = ((tile_i * nblk if blk0 is None else blk0) + j) * 128
        P.dma(lambda e, j=j, r0=r0: e.dma_start(out=xs[:, j, :], in_=src_rows[r0:r0 + 128, :]), writes=[xs_b[j]])
    for j in range(nblk):
        P.op("act", lambda e, j=j: e.activation(out=wk.junk[:], in_=xs[:, j, :], func=AF.Square,
                                                accum_out=wk.ss[:, j:j + 1]),
             reads=[xs_b[j]], writes=[wk.junk_b, wk.ss_b[j]])
    P.op("act", lambda e: e.activation(out=wk.sd[:, 0:nblk], in_=wk.ss[:, 0:nblk], func=AF.Sqrt,
                                       bias=C.epsc[:], scale=1.0 / D),
         reads=list(wk.ss_b[:nblk]) + [C.b_eps], writes=[wk.sd_b])
    P.op("dve", lambda e: e.reciprocal(out=wk.rstd[:, 0:nblk], in_=wk.sd[:, 0:nblk]), reads=[wk.sd_b], writes=[wk.rstd_b])
    for j in range(nblk):
        P.op("dve", lambda e, j=j: e.tensor_scalar(out=wk.hn[:], in0=xs[:, j, :], scalar1=wk.rstd[:, j:j + 1],
                                                   scalar2=None, op0=ALU.mult),
             reads=[xs_b[j], wk.rstd_b], writes=[wk.hn_b])
        for kt in range(8):
            P.op("pe", lambda e, kt=kt: e.transpose(out=ps_t[:, kt, :], in_=wk.hn[:, kt * 128:(kt + 1) * 128],
                                                    identity=C.ident[:]),
                 reads=[wk.hn_b, C.b_ident], writes=[ps_t_b])
        P.op("act", lambda e, j=j: e.copy(out=hnT[:, :, col0 + j * 128:col0 + (j + 1) * 128], in_=ps_t[:, :, :]),
             reads=[ps_t_b], writes=[hnT_b])


def norm_block_stats(nc, P, C, src_rows, r0, j, xs, xs_b, wk, load=True):
    hn = wk.hn2[j % 2]; hn_b = wk.hn2_b[j % 2]
    if load:
        P.dma(lambda e: e.dma_start(out=xs[:, j, :], in_=src_rows[r0:r0 + 128, :]), writes=[xs_b[j]])
    P.op("act", lambda e: e.activation(out=hn[:], in_=xs[:, j, :], func=AF.Square, accum_out=wk.ss[:, j:j + 1]),
         reads=[xs_b[j]], writes=[hn_b, wk.ss_b[j]])
    P.op("act", lambda e: e.activation(out=wk.sd[:, j:j + 1], in_=wk.ss[:, j:j + 1], func=AF.Sqrt, bias=C.epsc[:], scale=1.0 / D),
         reads=[wk.ss_b[j], C.b_eps], writes=[wk.ss_b[j]])
    P.op("dve", lambda e: e.reciprocal(out=wk.rstd[:, j:j + 1], in_=wk.sd[:, j:j + 1]), reads=[wk.ss_b[j]], writes=[wk.ss_b[j]])
    P.op("dve", lambda e: e.tensor_scalar(out=hn[:], in0=xs[:, j, :], scalar1=wk.rstd[:, j:j + 1], scalar2=None, op0=ALU.mult),
         reads=[xs_b[j], wk.ss_b[j]], writes=[hn_b])


def norm_block_T(nc, P, C, j, hnT, hnT_b, ps_t, ps_t_b, wk):
    hn = wk.hn2[j % 2]; hn_b = wk.hn2_b[j % 2]
    for kt in range(8):
        P.op("pe", lambda e, kt=kt: e.transpose(out=ps_t[:, kt, :], in_=hn[:, kt * 128:(kt + 1) * 128], identity=C.ident[:]),
             reads=[hn_b, C.b_ident], writes=[ps_t_b])
    P.op("act", lambda e: e.copy(out=hnT[:, :, j * 128:(j + 1) * 128], in_=ps_t[:, :, :]), reads=[ps_t_b], writes=[hnT_b])


def alloc_norm_work(P, sb, tag, double_hn=False):
    wk = Ctx()
    wk.ss = sb("ss" + tag, [128, 4], F32); wk.ss_b = P.bufs(4)
    wk.sd = sb("sd" + tag, [128, 4], F32); wk.sd_b = P.buf()
    wk.rstd = sb("rstd" + tag, [128, 4], F32); wk.rstd_b = P.buf()
    wk.hn = sb("hn" + tag, [128, D], BF16); wk.hn_b = P.buf()
    wk.junk = wk.hn; wk.junk_b = wk.hn_b
    if double_hn:
        wk.hn2 = [wk.hn, sb("hnB" + tag, [128, D], BF16)]; wk.hn2_b = [wk.hn_b, P.buf()]
    else:
        wk.hn2 = [wk.hn, wk.hn]; wk.hn2_b = [wk.hn_b, wk.hn_b]
    return wk


def fold_gain(nc, P, C, w_sb, w_buf, g_dram_row, kt_n, ncols, tag):
    gcol = C.sb("gcol" + tag, [128, kt_n], F32)
    gb = P.buf()
    with nc.allow_non_contiguous_dma(reason="tiny gain vector"):
        P.dma(lambda e: e.dma_start(out=gcol[:], in_=g_dram_row.rearrange("(kt p) -> p kt", p=128), allow_slow_non_contiguous=True), writes=[gb])
    for kt in range(kt_n):
        eng = "dve"
        P.op(eng, lambda e, kt=kt: e.tensor_scalar(out=w_sb[:, kt, 0:ncols], in0=w_sb[:, kt, 0:ncols],
                                                   scalar1=gcol[:, kt:kt + 1], scalar2=None, op0=ALU.mult),
             reads=[gb, w_buf], writes=[w_buf])


def phase_ffn(nc, P, A, lp, n_out, wup_pre=None):
    ntile = lp // TB
    with contextlib.ExitStack() as st:
        C = Ctx()
        _common_consts(nc, P, st, C)
        sb = C.sb
        if wup_pre is None:
            wup = sb("wup", [128, 8, 2 * DFF], BF16); wup_b = P.buf("wup")
            load_weight_bf16(nc, P, wup, wup_b, A.w_up, 8, 2 * DFF)
        else:
            wup, wup_b = wup_pre
        wdn = sb("wdn", [128, NFT, D], BF16); wdn_b = P.buf("wdn")
        load_weight_bf16(nc, P, wdn, wdn_b, A.w_down, NFT, D)
        fold_gain(nc, P, C, wup, wup_b, A.norm2_g, 8, 2 * DFF, "c")
        for kt in range(8):
            P.op("dve", lambda e, kt=kt: e.tensor_scalar(out=wup[:, kt, DFF:2 * DFF], in0=wup[:, kt, DFF:2 * DFF], scalar1=0.5,
                                                         scalar2=None, op0=ALU.mult), reads=[wup_b], writes=[wup_b])
        cw = sb("cw", [128, 3, NFT], F32); cb = sb("cb", [128, NFT], F32); cwb = P.buf("cw")
        with nc.allow_non_contiguous_dma(reason="tiny conv params"):
            for k in range(3):
                P.dma(lambda e, k=k: e.dma_start(out=cw[:, k, :], in_=A.conv_w[k, :].rearrange("(f p) -> p f", p=128), allow_slow_non_contiguous=True), writes=[cwb])
            P.dma(lambda e: e.dma_start(out=cb[:], in_=A.conv_b.rearrange("(f p) -> p f", p=128), allow_slow_non_contiguous=True), writes=[cwb])
        dg = sb("dg", [128, 3, NFT, 128], BF16); dg_b = P.buf("dg")
        for k in range(3):
            for ft in range(NFT):
                eng = "dve" if (k * NFT + ft) % 2 == 0 else "pool"
                P.op(eng, lambda e, k=k, ft=ft: e.tensor_scalar(out=dg[:, k, ft, :], in0=C.ident[:], scalar1=cw[:, k, ft:ft + 1],
                                                                scalar2=None, op0=ALU.mult),
                     reads=[cwb, C.b_ident], writes=[dg_b])
        xs1 = sb("xs", [128, 3, D], F32); xs = [xs1, xs1]
        xs1_b = P.bufs(3, "xs_"); xs_b = [xs1_b, xs1_b]
        hnT = sb("hnT", [128, 8, TB], BF16); hnT_b = P.buf("hnT")
        wk = alloc_norm_work(P, sb, "c", double_hn=True)
        a_sb = sb("a_sb", [128, NFT, TB + 2], BF16); a_b = P.bufs(NFT, "a")
        P.op("pool", lambda e: e.memset(a_sb[:, :, 0:2], 0.0), writes=a_b)
        cbt1 = sb("cbt", [128, TB], F32); cbt = [cbt1, cbt1]; c1b = P.buf("cbt"); cbt_b = [c1b, c1b]
        sq1 = sb("sq", [128, TB], F32); sq = [sq1, sq1]; s1b = P.buf("sq"); sq_b = [s1b, s1b]
        inn = sq; inn_b = sq_b; th = sq; th_b = sq_b; gg = sq; gg_b = sq_b
        gb = sb("gb", [128, NFT, TB], BF16); gb_b = P.bufs(NFT, "gb")
        ps_t = st.enter_context(nc.psum_tensor("ps_tc", [128, 8, 128], BF16)); ps_t_b = P.buf()
        psA = [st.enter_context(nc.psum_tensor("psA%d" % i, [128, 512], F32)) for i in range(2)]; psA_b = P.bufs(2)
        psV = st.enter_context(nc.psum_tensor("psV", [128, 512], F32)); psV_b = P.buf()
        psB = [st.enter_context(nc.psum_tensor("psB%d" % i, [128, 512], F32)) for i in range(2)]; psB_b = P.bufs(2)
        psD = [st.enter_context(nc.psum_tensor("psD%d" % i, [128, 512], F32)) for i in range(2)]; psD_b = P.bufs(2)
        fin = []
        nd = 0
        for ti in range(ntile):
            s = ti % 2
            if ti == 0:
                for j in range(3):
                    norm_block_stats(nc, P, C, A.h1, j * 128, j, xs[s], xs_b[s], wk)
                    norm_block_T(nc, P, C, j, hnT, hnT_b, ps_t, ps_t_b, wk)
            def emit_a(ft):
                r = ft % 2
                for kt in range(8):
                    P.op("pe", lambda e, kt=kt, ft=ft, r=r: e.matmul(psA[r][:, 0:TB], lhsT=wup[:, kt, ft * 128:(ft + 1) * 128],
                                                                      rhs=hnT[:, kt, :], start=(kt == 0), stop=(kt == 7)),
                         reads=[wup_b, hnT_b], writes=[psA_b[r]])
                P.op("act", lambda e, ft=ft, r=r: e.copy(out=a_sb[:, ft, 2:TB + 2], in_=psA[r][:, 0:TB]),
                     reads=[psA_b[r]], writes=[a_b[ft]])
            emit_a(0)
            for ft in range(NFT):
                r = ft % 2
                if ft + 1 < NFT:
                    emit_a(ft + 1)
                for k in range(3):
                    P.op("pe", lambda e, k=k, ft=ft: e.matmul(psV[:, 0:TB], lhsT=dg[:, k, ft, :], rhs=a_sb[:, ft, k:k + TB],
                                                              start=(k == 0), stop=(k == 2)),
                         reads=[dg_b, a_b[ft]], writes=[psV_b])
                P.op("pool", lambda e, ft=ft: e.tensor_copy(out=a_sb[:, ft, 0:2], in_=a_sb[:, ft, TB:TB + 2]),
                     reads=[a_b[ft]], writes=[a_b[ft]])
                P.op("act", lambda e, ft=ft, r=r: e.activation(out=cbt[r][:], in_=psV[:, 0:TB], func=AF.Identity,
                                                              bias=cb[:, ft:ft + 1], scale=1.0),
                     reads=[psV_b, cwb], writes=[cbt_b[r]])
                P.op("pool", lambda e, r=r: e.tensor_tensor(out=sq[r][:], in0=cbt[r][:], in1=cbt[r][:], op=ALU.mult),
                     reads=[cbt_b[r]], writes=[sq_b[r]])
                P.op("dve", lambda e, r=r: e.scalar_tensor_tensor(out=inn[r][:], in0=sq[r][:], scalar=1.0 / GC, in1=cbt[r][:],
                                                                  op0=ALU.add, op1=ALU.mult),
                     reads=[sq_b[r], cbt_b[r]], writes=[inn_b[r]])
                P.op("act", lambda e, r=r: e.activation(out=th[r][:], in_=inn[r][:], func=AF.Tanh, scale=GK * GC),
                     reads=[inn_b[r]], writes=[th_b[r]])
                P.op("dve", lambda e, r=r: e.scalar_tensor_tensor(out=gg[r][:], in0=th[r][:], scalar=1.0, in1=cbt[r][:],
                                                                  op0=ALU.add, op1=ALU.mult),
                     reads=[th_b[r], cbt_b[r]], writes=[gg_b[r]])
                for kt in range(8):
                    P.op("pe", lambda e, kt=kt, ft=ft, r=r: e.matmul(psB[r][:, 0:TB], lhsT=wup[:, kt, DFF + ft * 128:DFF + (ft + 1) * 128],
                                                                      rhs=hnT[:, kt, :], start=(kt == 0), stop=(kt == 7)),
                         reads=[wup_b, hnT_b], writes=[psB_b[r]])
                P.op("dve", lambda e, ft=ft, r=r: e.tensor_tensor(out=gb[:, ft, :], in0=gg[r][:], in1=psB[r][:, 0:TB], op=ALU.mult),
                     reads=[gg_b[r], psB_b[r]], writes=[gb_b[ft]])
            pend_T = []
            for j in range(3):
                blk = ti * 3 + j
                o = nd % 2
                nd += 1
                for half in range(2):
                    for ft in range(NFT):
                        P.op("pe", lambda e, ft=ft, j=j, half=half: e.matmul(psD[half][:], lhsT=gb[:, ft, j * 128:(j + 1) * 128],
                                                                               rhs=wdn[:, ft, half * 512:(half + 1) * 512],
                                                                               start=(ft == 0), stop=(ft == NFT - 1)),
                             reads=[gb_b[ft], wdn_b], writes=[psD_b[half]])
                    P.op("dve", lambda e, j=j, half=half, s=s: e.tensor_tensor(out=xs[s][:, j, half * 512:(half + 1) * 512],
                                                                              in0=xs[s][:, j, half * 512:(half + 1) * 512],
                                                                              in1=psD[half][:], op=ALU.add),
                         reads=[xs_b[s][j], psD_b[half]], writes=[xs_b[s][j]])
                t0 = blk * 128
                lo = max(t0, NMETA); hi = min(t0 + 128, NMETA + n_out)
                if hi > lo:
                    fin.append(P.dma(lambda e, j=j, s=s, lo=lo, hi=hi, t0=t0: e.dma_start(out=A.out[lo - NMETA:hi - NMETA, :],
                                                                                            in_=xs[s][lo - t0:hi - t0, j, :]),
                                     reads=[xs_b[s][j]]))
                if ti + 1 < ntile:
                    norm_block_stats(nc, P, C, A.h1, ((ti + 1) * 3 + j) * 128, j, xs[s], xs_b[s], wk)
                    pend_T.append(j)
                    if len(pend_T) > 1:
                        norm_block_T(nc, P, C, pend_T.pop(0), hnT, hnT_b, ps_t, ps_t_b, wk)
            while pend_T:
                norm_block_T(nc, P, C, pend_T.pop(0), hnT, hnT_b, ps_t, ps_t_b, wk)
        P.barrier()
    return fin


def build_program(lp=LP, n_out=SEQ, mode="full"):
    nc = bass.Bass("TRN2", target_bir_lowering=False)
    A = Ctx()

    def din(name, shape):
        return nc.dram_tensor(name, shape, F32, kind="ExternalInput").ap()
    A.out = nc.dram_tensor("out", [n_out, D], F32, kind="ExternalOutput").ap()
    A.w_up = din("w_up", [D, 2 * DFF]); A.w_down = din("w_down", [DFF, D])
    A.norm2_g = din("norm2_g", [D]); A.conv_w = din("conv_w", [3, DFF]); A.conv_b = din("conv_b", [DFF])
    if mode == "ffn":
        A.h1 = din("h1", [lp, D])
    else:
        A.h1 = nc.dram_tensor("h1", [lp, D], F32, kind="Internal").ap()
        A.h0 = din("h0", [lp, D])
        A.norm1_g = din("norm1_g", [D]); A.w_in = din("w_in", [D, 4096])
        A.ssm_a_re = din("ssm_a_re", [32, 64]); A.ssm_a_im = din("ssm_a_im", [32, 64]); A.ssm_log_dt = din("ssm_log_dt", [32])
        A.ssm_b_re = din("ssm_b_re", [32, 64, 16]); A.ssm_b_im = din("ssm_b_im", [32, 64, 16])
        A.ssm_c_re = din("ssm_c_re", [32, 16, 64]); A.ssm_c_im = din("ssm_c_im", [32, 16, 64])
        A.ssm_d = din("ssm_d", [32, 16]); A.ssm_glu_w = din("ssm_glu_w", [512, 512]); A.ssm_glu_b = din("ssm_glu_b", [512])
        A.q_norm_g = din("q_norm_g", [64]); A.k_norm_g = din("k_norm_g", [64])
        for nm in ("lam_q1", "lam_k1", "lam_q2", "lam_k2"):
            setattr(A, nm, din(nm, [64]))
        A.subln_g = din("subln_g", [128])
        A.w_ssm_out = din("w_ssm_out", [512, D]); A.w_att_out = din("w_att_out", [512, D]); A.w_o = din("w_o", [D, D])
        if mode == "dbg":
            A.ysT = nc.dram_tensor("ysT", [512, lp], BF16, kind="ExternalOutput").ap()
            A.yaT = nc.dram_tensor("yaT", [512, lp], BF16, kind="ExternalOutput").ap()
        else:
            A.ysT = nc.dram_tensor("ysT", [512, lp], BF16, kind="Internal").ap()
            A.yaT = nc.dram_tensor("yaT", [512, lp], BF16, kind="Internal").ap()
    with contextlib.ExitStack() as st:
        P = Prog(nc, st)
        A.ysT_b = P.buf("ysT"); A.yaT_b = P.buf("yaT"); A.h1_b = P.buf("h1")
        wup_pre = None
        if mode != "ffn":
            phase_mixers(nc, P, A, lp)
            wup = st.enter_context(nc.sbuf_tensor("wup_shared", [128, 8, 2 * DFF], BF16)); wup_b = P.buf("wup")
            wup_pre = (wup, wup_b)
            phase_merge(nc, P, A, lp, prefetch=lambda: load_weight_bf16(nc, P, wup, wup_b, A.w_up, 8, 2 * DFF))
        fin = phase_ffn(nc, P, A, lp, n_out, wup_pre=wup_pre)
        if mode == "dbg":
            fin = fin + [A.ysT_b.last_w, A.yaT_b.last_w]
        P.finish(fin)
    return nc


_NC_CACHE = {}


def kernel(**inputs):
    x = np.asarray(inputs["x"], dtype=np.float32)
    meta = np.asarray(inputs["meta_tokens"], dtype=np.float32)
    bsz = x.shape[0]
    if "full" not in _NC_CACHE:
        _NC_CACHE["full"] = build_program(LP, SEQ, "full")
    nc = _NC_CACHE["full"]
    pad = np.zeros((LP - NMETA - SEQ, D), np.float32)
    shared = {}
    for k, v in inputs.items():
        if k in ("x", "meta_tokens"):
            continue
        shared[k] = np.ascontiguousarray(np.asarray(v, dtype=np.float32)[0])
    in_maps = []
    for b in range(bsz):
        m = dict(shared)
        m["h0"] = np.concatenate([meta, x[b], pad], axis=0)
        in_maps.append(m)
    res = run_bass_kernel_spmd(nc, in_maps, core_ids=list(range(bsz)))
    return np.stack([np.asarray(r["out"], dtype=np.float32) for r in res.results], axis=0)


def phase_merge(nc, P, A, lp, prefetch=None):
    ntile = lp // TB
    with contextlib.ExitStack() as st:
        C = Ctx()
        _common_consts(nc, P, st, C)
        sb = C.sb
        wg = sb("wg", [128, 8, 2048], BF16); wg_b = P.buf("wg")
        load_weight_bf16(nc, P, wg, wg_b, A.w_in, 8, 2048, col0=2048)
        fold_gain(nc, P, C, wg, wg_b, A.norm1_g, 8, 2048, "b")
        wso = sb("wso", [128, 4, D], BF16); wso_b = P.buf("wso")
        wao = sb("wao", [128, 4, D], BF16); wao_b = P.buf("wao")
        wo = sb("wo", [128, 8, D], BF16); wo_b = P.buf("wo")
        load_weight_bf16(nc, P, wso, wso_b, A.w_ssm_out, 4, D)
        for kt in range(4):
            P.op("dve", lambda e, kt=kt: e.tensor_scalar(out=wso[:, kt, :], in0=wso[:, kt, :], scalar1=0.25, scalar2=None, op0=ALU.mult),
                 reads=[wso_b], writes=[wso_b])
        load_weight_bf16(nc, P, wao, wao_b, A.w_att_out, 4, D)
        load_weight_bf16(nc, P, wo, wo_b, A.w_o, 8, D)
        if prefetch is not None:
            prefetch()
        xs = sb("xsb", [128, 3, D], F32); xs_b = P.bufs(3, "xsb")
        hnT = sb("hnTb", [128, 8, TB], BF16); hnT_b = P.buf("hnTb")
        wk = alloc_norm_work(P, sb, "b")
        gs = sb("gs", [128, 16, TB], BF16); gs_b = P.bufs(16, "gs")
        ys = sb("ys", [128, 4, TB], BF16); ys_b = P.buf("ys")
        ya = sb("ya", [128, 4, TB], BF16); ya_b = P.buf("ya")
        t1 = sb("t1", [128, TB], F32); t1_b = P.buf("t1")
        t2 = sb("t2", [128, TB], F32); t2_b = P.buf("t2")
        t1x = sb("t1x", [128, TB], F32); t1x_b = P.buf("t1x")
        t2x = sb("t2x", [128, TB], F32); t2x_b = P.buf("t2x")
        mx = sb("mx", [128, 8, TB], BF16); mx_b = P.bufs(8, "mx")
        ps_t = st.enter_context(nc.psum_tensor("ps_tb", [128, 8, 128], BF16)); ps_t_b = P.buf()
        psG = [st.enter_context(nc.psum_tensor("psG%d" % i, [128, 512], F32)) for i in range(2)]; psG_b = P.bufs(2)
        ps1 = st.enter_context(nc.psum_tensor("ps1", [128, 512], F32)); ps1_b = P.buf()
        ps2 = st.enter_context(nc.psum_tensor("ps2", [128, 512], F32)); ps2_b = P.buf()
        psO = [st.enter_context(nc.psum_tensor("psO%d" % i, [128, 512], F32)) for i in range(2)]; psO_b = P.bufs(2)
        for ti in range(ntile):
            c0 = ti * TB
            norm_tile(nc, P, C, A.h0, ti, 3, xs, xs_b, hnT, hnT_b, ps_t, ps_t_b, wk)
            P.dma(lambda e, c0=c0: e.dma_start(out=ys[:], in_=A.ysT[:, c0:c0 + TB].rearrange("(kt p) t -> p kt t", p=128)),
                  reads=[A.ysT_b], writes=[ys_b])
            P.dma(lambda e, c0=c0: e.dma_start(out=ya[:], in_=A.yaT[:, c0:c0 + TB].rearrange("(kt p) t -> p kt t", p=128)),
                  reads=[A.yaT_b], writes=[ya_b])
            for m in range(16):
                r = m % 2
                for kt in range(8):
                    P.op("pe", lambda e, kt=kt, m=m, r=r: e.matmul(psG[r][:, 0:TB], lhsT=wg[:, kt, m * 128:(m + 1) * 128],
                                                                   rhs=hnT[:, kt, :], start=(kt == 0), stop=(kt == 7)),
                         reads=[wg_b, hnT_b], writes=[psG_b[r]])
                P.op("act", lambda e, m=m, r=r: e.activation(out=gs[:, m, :], in_=psG[r][:, 0:TB], func=AF.Sigmoid),
                     reads=[psG_b[r]], writes=[gs_b[m]])
            for m in range(8):
                pa, pa_b = (ps1, ps1_b) if m % 2 == 0 else (psG[0], psG_b[0])
                pb, pb_b = (ps2, ps2_b) if m % 2 == 0 else (psG[1], psG_b[1])
                for kt in range(4):
                    P.op("pe", lambda e, kt=kt, m=m, pa=pa: e.matmul(pa[:, 0:TB], lhsT=wso[:, kt, m * 128:(m + 1) * 128], rhs=ys[:, kt, :],
                                                                     start=(kt == 0), stop=(kt == 3)),
                         reads=[wso_b, ys_b], writes=[pa_b])
                for kt in range(4):
                    P.op("pe", lambda e, kt=kt, m=m, pb=pb: e.matmul(pb[:, 0:TB], lhsT=wao[:, kt, m * 128:(m + 1) * 128], rhs=ya[:, kt, :],
                                                                     start=(kt == 0), stop=(kt == 3)),
                         reads=[wao_b, ya_b], writes=[pb_b])
                tA, tA_b = (t1, t1_b) if m % 2 == 0 else (t1x, t1x_b)
                tB, tB_b = (t2, t2_b) if m % 2 == 0 else (t2x, t2x_b)
                P.op("dve", lambda e, m=m, pa=pa, tA=tA: e.tensor_tensor(out=tA[:], in0=gs[:, m, :], in1=pa[:, 0:TB], op=ALU.mult),
                     reads=[gs_b[m], pa_b], writes=[tA_b])
                P.op("dve", lambda e, m=m, pb=pb, tB=tB: e.tensor_tensor(out=tB[:], in0=gs[:, 8 + m, :], in1=pb[:, 0:TB], op=ALU.mult),
                     reads=[gs_b[8 + m], pb_b], writes=[tB_b])
                P.op("pool", lambda e, m=m, tA=tA, tB=tB: e.tensor_tensor(out=mx[:, m, :], in0=tA[:], in1=tB[:], op=ALU.add),
                     reads=[tA_b, tB_b], writes=[mx_b[m]])
            for j in range(3):
                blk = ti * 3 + j
                for half in range(2):
                    for kt in range(8):
                        P.op("pe", lambda e, kt=kt, j=j, half=half: e.matmul(psO[half][:], lhsT=mx[:, kt, j * 128:(j + 1) * 128],
                                                                               rhs=wo[:, kt, half * 512:(half + 1) * 512],
                                                                               start=(kt == 0), stop=(kt == 7)),
                             reads=[mx_b[kt], wo_b], writes=[psO_b[half]])
                    P.op("dve", lambda e, j=j, half=half: e.tensor_tensor(out=xs[:, j, half * 512:(half + 1) * 512],
                                                                         in0=xs[:, j, half * 512:(half + 1) * 512],
                                                                         in1=psO[half][:], op=ALU.add),
                         reads=[xs_b[j], psO_b[half]], writes=[xs_b[j]])
                P.dma(lambda e, j=j, blk=blk: e.dma_start(out=A.h1[blk * 128:(blk + 1) * 128, :], in_=xs[:, j, :]),
                      reads=[xs_b[j]], writes=[A.h1_b])
        P.barrier()


NH = 4
SLOPES = [2.0 ** (-8.0 * (h + 1) / NH) for h in range(NH)]
LAM_INIT = 0.8 - 0.6 * math.exp(-0.3 * 0)
TS = 128


def s5_setup(nc, P, C, A, st, S):
    sb = C.sb
    tp = lambda n, sh, dt=F32: sb("s5_" + n, sh, dt)
    are = tp("are", [128, 16]); aim = tp("aim", [128, 16]); ldt = tp("ldt", [128, 16])
    b_in = P.buf("s5in")
    P.dma(lambda e: e.dma_start(out=are[:], in_=A.ssm_a_re.rearrange("(a s) p -> (s p) a", s=2), allow_slow_non_contiguous=True), writes=[b_in])
    P.dma(lambda e: e.dma_start(out=aim[:], in_=A.ssm_a_im.rearrange("(a s) p -> (s p) a", s=2), allow_slow_non_contiguous=True), writes=[b_in])
    l1 = tp("l1", [1, 32]); ones1 = tp("ones1", [1, 128]); b_l1 = P.buf()
    P.dma(lambda e: e.dma_start(out=l1[:], in_=A.ssm_log_dt.rearrange("(o g) -> o g", o=1)), writes=[b_l1])
    P.op("pool", lambda e: e.memset(ones1[:], 1.0), writes=[b_l1])
    psx = S.ps_misc; psx_b = S.ps_misc_b
    P.op("pe", lambda e: e.matmul(psx[:, 0:32], lhsT=ones1[:], rhs=l1[:], start=True, stop=True), reads=[b_l1], writes=[psx_b])
    lv = psx[:, 0:32].rearrange("q (a s) -> q a s", s=2)
    P.op("dve", lambda e: e.tensor_copy(out=ldt[0:64, :], in_=lv[0:64, :, 0]), reads=[psx_b], writes=[b_in])
    P.op("dve", lambda e: e.tensor_copy(out=ldt[64:128, :], in_=lv[64:128, :, 1]), reads=[psx_b], writes=[b_in])
    dt = tp("dt", [128, 16]); rho = tp("rho", [128, 16]); th = tp("th", [128, 16]); mag = tp("mag", [128, 16])
    cs = tp("cs", [128, 16]); sn = tp("sn", [128, 16]); tmpa = tp("tmpa", [128, 16]); tmpb = tp("tmpb", [128, 16])
    halfpi = tp("halfpi", [128, 1]); bq = P.buf("s5q")
    P.op("pool", lambda e: e.memset(halfpi[:], math.pi / 2), writes=[bq])
    P.op("act", lambda e: e.activation(out=dt[:], in_=ldt[:], func=AF.Exp), reads=[b_in], writes=[bq])
    P.op("dve", lambda e: e.tensor_tensor(out=rho[:], in0=are[:], in1=dt[:], op=ALU.mult), reads=[b_in, bq], writes=[bq])
    P.op("dve", lambda e: e.tensor_tensor(out=th[:], in0=aim[:], in1=dt[:], op=ALU.mult), reads=[b_in, bq], writes=[bq])
    P.op("act", lambda e: e.activation(out=mag[:], in_=rho[:], func=AF.Exp), reads=[bq], writes=[bq])
    P.op("act", lambda e: e.activation(out=sn[:], in_=th[:], func=AF.Sin, scale=1.0 / 16), reads=[bq], writes=[bq])
    P.op("act", lambda e: e.activation(out=cs[:], in_=th[:], func=AF.Sin, scale=1.0 / 16, bias=halfpi[:]), reads=[bq], writes=[bq])

    def cdouble(c, s_):
        P.op("dve", lambda e: e.tensor_tensor(out=tmpa[:], in0=c[:], in1=s_[:], op=ALU.mult), reads=[bq], writes=[bq])
        P.op("dve", lambda e: e.tensor_tensor(out=tmpb[:], in0=s_[:], in1=s_[:], op=ALU.mult), reads=[bq], writes=[bq])
        P.op("dve", lambda e: e.tensor_tensor(out=c[:], in0=c[:], in1=c[:], op=ALU.mult), reads=[bq], writes=[bq])
        P.op("dve", lambda e: e.tensor_tensor(out=c[:], in0=c[:], in1=tmpb[:], op=ALU.subtract), reads=[bq], writes=[bq])
        P.op("dve", lambda e: e.tensor_scalar(out=s_[:], in0=tmpa[:], scalar1=2.0, scalar2=None, op0=ALU.mult), reads=[bq], writes=[bq])
    for _ in range(4):
        cdouble(cs, sn)
    S.c1 = cs; S.s1 = sn; S.mag = mag; S.bq = bq
    lre = tp("lre", [128, 16]); lim = tp("lim", [128, 16]); den = tp("den", [128, 16])
    fre = tp("fre", [128, 16]); fim = tp("fim", [128, 16]); nfim = tp("nfim", [128, 16])
    P.op("dve", lambda e: e.tensor_tensor(out=lre[:], in0=mag[:], in1=cs[:], op=ALU.mult), reads=[bq], writes=[bq])
    P.op("dve", lambda e: e.tensor_tensor(out=lim[:], in0=mag[:], in1=sn[:], op=ALU.mult), reads=[bq], writes=[bq])
    P.op("dve", lambda e: e.tensor_tensor(out=den[:], in0=are[:], in1=are[:], op=ALU.mult), reads=[b_in, bq], writes=[bq])
    P.op("dve", lambda e: e.tensor_tensor(out=tmpa[:], in0=aim[:], in1=aim[:], op=ALU.mult), reads=[b_in, bq], writes=[bq])
    P.op("dve", lambda e: e.tensor_tensor(out=den[:], in0=den[:], in1=tmpa[:], op=ALU.add), reads=[bq], writes=[bq])
    P.op("dve", lambda e: e.reciprocal(out=den[:], in_=den[:]), reads=[bq], writes=[bq])
    P.op("dve", lambda e: e.tensor_scalar(out=tmpb[:], in0=lre[:], scalar1=-1.0, scalar2=None, op0=ALU.add), reads=[bq], writes=[bq])
    P.op("dve", lambda e: e.tensor_tensor(out=fre[:], in0=tmpb[:], in1=are[:], op=ALU.mult), reads=[bq, b_in], writes=[bq])
    P.op("dve", lambda e: e.tensor_tensor(out=tmpa[:], in0=lim[:], in1=aim[:], op=ALU.mult), reads=[bq, b_in], writes=[bq])
    P.op("dve", lambda e: e.tensor_tensor(out=fre[:], in0=fre[:], in1=tmpa[:], op=ALU.add), reads=[bq], writes=[bq])
    P.op("dve", lambda e: e.tensor_tensor(out=fre[:], in0=fre[:], in1=den[:], op=ALU.mult), reads=[bq], writes=[bq])
    P.op("dve", lambda e: e.tensor_tensor(out=fim[:], in0=lim[:], in1=are[:], op=ALU.mult), reads=[bq, b_in], writes=[bq])
    P.op("dve", lambda e: e.tensor_tensor(out=tmpa[:], in0=tmpb[:], in1=aim[:], op=ALU.mult), reads=[bq, b_in], writes=[bq])
    P.op("dve", lambda e: e.tensor_tensor(out=fim[:], in0=fim[:], in1=tmpa[:], op=ALU.subtract), reads=[bq], writes=[bq])
    P.op("dve", lambda e: e.tensor_tensor(out=fim[:], in0=fim[:], in1=den[:], op=ALU.mult), reads=[bq], writes=[bq])
    P.op("dve", lambda e: e.tensor_scalar(out=nfim[:], in0=fim[:], scalar1=-1.0, scalar2=None, op0=ALU.mult), reads=[bq], writes=[bq])
    al = getattr(S, "alias", {})
    bre = al["bre"] if "bre" in al else tp("bre", [128, 16, 16])[:]
    bim = al["bim"] if "bim" in al else tp("bim", [128, 16, 16])[:]
    bb = P.buf("s5b")
    P.dma(lambda e: e.dma_start(out=bre, in_=A.ssm_b_re.rearrange("(a s) p c -> (s p) a c", s=2)), writes=[bb])
    P.dma(lambda e: e.dma_start(out=bim, in_=A.ssm_b_im.rearrange("(a s) p c -> (s p) a c", s=2)), writes=[bb])
    wre = al["wre"] if "wre" in al else tp("wre", [128, 16, 128], BF16)[:]
    wim = al["wim"] if "wim" in al else tp("wim", [128, 16, 128], BF16)[:]
    bw = P.buf("s5w")
    P.op("pool", lambda e: e.memset(wre, 0.0), writes=[bw])
    P.op("pool", lambda e: e.memset(wim, 0.0), writes=[bw])
    tA, tB, BR, BI = [(al[nm] if nm in al else tp(nm, [128, 16, 16])[:]) for nm in ("tA", "tB", "BR", "BI")]
    bc3 = lambda t: t[:, :].unsqueeze(2).broadcast_to([128, 16, 16])
    P.op("dve", lambda e: e.tensor_tensor(out=tA, in0=bre, in1=bc3(fre), op=ALU.mult), reads=[bb, bq], writes=[bq])
    P.op("dve", lambda e: e.tensor_tensor(out=tB, in0=bim, in1=bc3(fim), op=ALU.mult), reads=[bb, bq], writes=[bq])
    P.op("dve", lambda e: e.tensor_tensor(out=BR, in0=tA, in1=tB, op=ALU.subtract), reads=[bq], writes=[bq])
    P.op("dve", lambda e: e.tensor_tensor(out=tA, in0=bim, in1=bc3(fre), op=ALU.mult), reads=[bb, bq], writes=[bq])
    P.op("dve", lambda e: e.tensor_tensor(out=tB, in0=bre, in1=bc3(fim), op=ALU.mult), reads=[bb, bq], writes=[bq])
    P.op("dve", lambda e: e.tensor_tensor(out=BI, in0=tA, in1=tB, op=ALU.add), reads=[bq], writes=[bq])
    for aq in range(4):
        for hf in range(2):
            c0 = 32 * aq + 16 * hf
            P.op("dve", lambda e, aq=aq, hf=hf, c0=c0: e.tensor_copy(out=wre[64 * hf:64 * hf + 64, aq::4, c0:c0 + 16], in_=BR[64 * hf:64 * hf + 64, aq::4, :]),
                 reads=[bq], writes=[bw])
            P.op("dve", lambda e, aq=aq, hf=hf, c0=c0: e.tensor_copy(out=wim[64 * hf:64 * hf + 64, aq::4, c0:c0 + 16], in_=BI[64 * hf:64 * hf + 64, aq::4, :]),
                 reads=[bq], writes=[bw])
    S.BBre = tp("BBre", [128, 16, 128], BF16); S.BBim = tp("BBim", [128, 16, 128], BF16); S.bBB = P.buf("BB")
    pst = S.ps_t; pst_b = S.ps_t_b
    for (src, dst) in ((wre, S.BBre), (wim, S.BBim)):
        for half in range(2):
            for a8 in range(8):
                a = half * 8 + a8
                P.op("pe", lambda e, a=a, a8=a8, src=src: e.transpose(out=pst[:, a8, :], in_=src[:, a, :], identity=C.ident[:]),
                     reads=[bw, C.b_ident], writes=[pst_b])
            P.op("dve", lambda e, half=half, dst=dst: e.tensor_copy(out=dst[:, half * 8:(half + 1) * 8, :], in_=pst[:, :, :]),
                 reads=[pst_b], writes=[S.bBB])
    cre = al["cre"] if "cre" in al else tp("cre", [128, 16, 16])[:]
    cim = al["cim"] if "cim" in al else tp("cim", [128, 16, 16])[:]
    bc = P.buf("s5c")
    for s_ in range(2):
        for a in range(16):
            P.dma(lambda e, s_=s_, a=a: e.dma_start(out=cre[64 * s_:64 * (s_ + 1), a, :], in_=A.ssm_c_re[2 * a + s_].rearrange("c p -> p c"),
                                                    allow_slow_non_contiguous=True), writes=[bc])
            P.dma(lambda e, s_=s_, a=a: e.dma_start(out=cim[64 * s_:64 * (s_ + 1), a, :], in_=A.ssm_c_im[2 * a + s_].rearrange("c p -> p c"),
                                                    allow_slow_non_contiguous=True), writes=[bc])
    S.CWre = tp("CWre", [128, 16, 32], BF16); S.CWim = tp("CWim", [128, 16, 32], BF16); S.bCW = P.buf("CW")
    P.op("pool", lambda e: e.memset(S.CWre[:], 0.0), writes=[S.bCW])
    P.op("pool", lambda e: e.memset(S.CWim[:], 0.0), writes=[S.bCW])
    P.op("dve", lambda e: e.tensor_copy(out=S.CWre[0:64, :, 0:16], in_=cre[0:64, :, :]), reads=[bc], writes=[S.bCW])
    P.op("dve", lambda e: e.tensor_copy(out=S.CWre[64:128, :, 16:32], in_=cre[64:128, :, :]), reads=[bc], writes=[S.bCW])
    P.op("dve", lambda e: e.tensor_scalar(out=S.CWim[0:64, :, 0:16], in0=cim[0:64, :, :], scalar1=-1.0, scalar2=None, op0=ALU.mult),
         reads=[bc], writes=[S.bCW])
    P.op("dve", lambda e: e.tensor_scalar(out=S.CWim[64:128, :, 16:32], in0=cim[64:128, :, :], scalar1=-1.0, scalar2=None, op0=ALU.mult),
         reads=[bc], writes=[S.bCW])
    S.dcol = tp("dcol", [128, 4]); S.b_d = P.buf("dcol")
    P.dma(lambda e: e.dma_start(out=S.dcol[:], in_=A.ssm_d.rearrange("(ct g) c -> (g c) ct", ct=4), allow_slow_non_contiguous=True), writes=[S.b_d])
    S.cosT = tp("cosT", [128, 16, TS]); S.sinT = tp("sinT", [128, 16, TS]); S.R = tp("R", [128, 16, TS]); S.bT = P.buf("tables")
    P.op("pool", lambda e: e.memset(S.cosT[:, :, 0:1], 1.0), writes=[S.bT])
    P.op("pool", lambda e: e.memset(S.sinT[:, :, 0:1], 0.0), writes=[S.bT])
    pc = tp("pc", [128, 16]); ps_ = tp("ps", [128, 16])
    P.op("dve", lambda e: e.tensor_copy(out=pc[:], in_=cs[:]), reads=[bq], writes=[bq])
    P.op("dve", lambda e: e.tensor_copy(out=ps_[:], in_=sn[:]), reads=[bq], writes=[bq])
    tt = al["tt"] if "tt" in al else tp("tt", [128, 16, TS // 2])[:]
    n = 1
    while n < TS:
        pcb = pc[:, :].unsqueeze(2).broadcast_to([128, 16, n]); psb = ps_[:, :].unsqueeze(2).broadcast_to([128, 16, n])
        sc = S.cosT[:, :, 0:n]; ss_ = S.sinT[:, :, 0:n]; dc = S.cosT[:, :, n:2 * n]; ds = S.sinT[:, :, n:2 * n]; t_ = tt[:, :, 0:n]
        P.op("dve", lambda e, dc=dc, sc=sc, pcb=pcb: e.tensor_tensor(out=dc, in0=sc, in1=pcb, op=ALU.mult), reads=[S.bT, bq], writes=[S.bT])
        P.op("dve", lambda e, t_=t_, ss_=ss_, psb=psb: e.tensor_tensor(out=t_, in0=ss_, in1=psb, op=ALU.mult), reads=[S.bT, bq], writes=[bq])
        P.op("dve", lambda e, dc=dc, t_=t_: e.tensor_tensor(out=dc, in0=dc, in1=t_, op=ALU.subtract), reads=[S.bT, bq], writes=[S.bT])
        P.op("dve", lambda e, ds=ds, ss_=ss_, pcb=pcb: e.tensor_tensor(out=ds, in0=ss_, in1=pcb, op=ALU.mult), reads=[S.bT, bq], writes=[S.bT])
        P.op("dve", lambda e, t_=t_, sc=sc, psb=psb: e.tensor_tensor(out=t_, in0=sc, in1=psb, op=ALU.mult), reads=[S.bT, bq], writes=[bq])
        P.op("dve", lambda e, ds=ds, t_=t_: e.tensor_tensor(out=ds, in0=ds, in1=t_, op=ALU.add), reads=[S.bT, bq], writes=[S.bT])
        cdouble(pc, ps_)
        n *= 2
    S.cN = pc; S.sN = ps_
    P.op("pool", lambda e: e.tensor_copy(out=S.R[:], in_=mag[:, :].unsqueeze(2).broadcast_to([128, 16, TS])), reads=[bq], writes=[S.bT])
    S.init_re = tp("init_re", [128, 16]); S.init_im = tp("init_im", [128, 16]); S.b_init = P.bufs(4, "init")
    P.op("pool", lambda e: e.memset(S.init_re[:], 0.0), writes=S.b_init)
    P.op("pool", lambda e: e.memset(S.init_im[:], 0.0), writes=S.b_init)


def phase_mixers(nc, P, A, lp):
    ntile = lp // TB
    nblk = lp // 128
    with contextlib.ExitStack() as st:
        C = Ctx()
        _common_consts(nc, P, st, C)
        sb = C.sb
        S = Ctx()
        ps_t = st.enter_context(nc.psum_tensor("ps_ta", [128, 8, 128], BF16)); ps_t_b = P.buf()
        psP = [st.enter_context(nc.psum_tensor("psP%d" % i, [128, 512], F32)) for i in range(2)]; psP_b = P.bufs(2)
        psQ = st.enter_context(nc.psum_tensor("psQ", [128, 512], F32)); psQ_b = P.buf("psQ"); psQr_b = [psQ_b, psQ_b, psQ_b]
        psS = [st.enter_context(nc.psum_tensor("psS%d" % i, [128, 512], F32)) for i in range(2)]; psS_b = P.bufs(2)
        psAccF = [st.enter_context(nc.psum_tensor("psAcc%d" % i, [128, 512], F32)) for i in range(2)]; psAcc_b = P.bufs(2)
        psAcc = [t[:, 0:387].rearrange("q (s e) -> q s e", s=3) for t in psAccF]
        S.ps_misc = psQ; S.ps_misc_b = psQ_b; S.ps_t = ps_t; S.ps_t_b = ps_t_b
        khist = sb("khist", [128, NH, lp], BF16); kh_b = P.bufs(ntile, "kh")
        vhist = sb("vhist", [128, nblk, NH, 129], BF16); vh_b = P.bufs(ntile, "vh")
        WS = []
        for k in range(2):
            W = Ctx()
            for nm in ("xtr", "xti", "tm1", "tm2", "wr", "wi"):
                setattr(W, nm, sb("%s%d" % (nm, k), [128, 4, TS], F32)); setattr(W, nm + "_b", P.buf())
            for nm in ("Sr", "Si"):
                setattr(W, nm, sb("%s%d" % (nm, k), [128, 4, TS], BF16)); setattr(W, nm + "_b", P.buf())
            for nm in ("yv", "yq"):
                setattr(W, nm, sb("%s%d" % (nm, k), [128, TS], F32)); setattr(W, nm + "_b", P.buf())
            for nm in ("c4", "c4b"):
                setattr(W, nm, sb("%s%d" % (nm, k), [128, 4], F32)); setattr(W, nm + "_b", P.buf())
            WS.append(W)
        xs = sb("xsa", [128, 1, D], F32); xs_b = P.bufs(1, "xsa")
        S.alias = {}
        S.alias["tt"] = xs[:, 0, :].rearrange("q (a k) -> q a k", a=16)
        if lp >= 2048:
            S.alias["wre"] = khist[:, 0, 0:2048].rearrange("q (a c) -> q a c", a=16)
            S.alias["wim"] = khist[:, 1, 0:2048].rearrange("q (a c) -> q a c", a=16)
        for nm, t in (("bre", WS[0].xtr), ("bim", WS[0].xti), ("cre", WS[0].tm1), ("cim", WS[0].tm2)):
            S.alias[nm] = t[:, 0:2, :].rearrange("q i (x c) -> q (i x) c", c=16)
        for nm, t in (("tA", WS[1].xtr), ("tB", WS[1].xti), ("BR", WS[1].tm1), ("BI", WS[1].tm2)):
            S.alias[nm] = t[:, 0:2, :].rearrange("q i (x c) -> q (i x) c", c=16)
        wa = sb("wa", [128, 8, 2048], BF16); wa_b = P.buf("wa")
        load_weight_bf16(nc, P, wa, wa_b, A.w_in, 8, 2048, col0=0)
        glu = sb("glu", [128, 4, 512], BF16); glu_b = P.buf("glu")
        load_weight_bf16(nc, P, glu, glu_b, A.ssm_glu_w, 4, 512)
        s5_setup(nc, P, C, A, st, S)
        P.barrier()
        P.op("pool", lambda e: e.memset(vhist[:, :, :, 128:129], 1.0), writes=vh_b)
        fold_gain(nc, P, C, wa, wa_b, A.norm1_g, 8, 2048, "a")
        for kt in range(4):
            P.op("pool", lambda e, kt=kt: e.tensor_scalar(out=glu[:, kt, :], in0=glu[:, kt, :], scalar1=0.5, scalar2=None, op0=ALU.mult),
                 reads=[glu_b], writes=[glu_b])
        glub = sb("glub", [128, 4], F32); misc_b = P.buf("misc")
        P.dma(lambda e: e.dma_start(out=glub[:], in_=A.ssm_glu_b.rearrange("(m p) -> p m", p=128), allow_slow_non_contiguous=True), writes=[misc_b])
        P.op("dve", lambda e: e.tensor_scalar(out=glub[:], in0=glub[:], scalar1=0.5, scalar2=None, op0=ALU.mult), reads=[misc_b], writes=[misc_b])
        gq = sb("gq", [128, 1], F32); gk = sb("gk", [128, 1], F32)
        for h2 in range(2):
            P.dma(lambda e, h2=h2: e.dma_start(out=gq[64 * h2:64 * h2 + 64, :], in_=A.q_norm_g.rearrange("(p o) -> p o", o=1)), writes=[misc_b])
            P.dma(lambda e, h2=h2: e.dma_start(out=gk[64 * h2:64 * h2 + 64, :], in_=A.k_norm_g.rearrange("(p o) -> p o", o=1)), writes=[misc_b])
        P.op("dve", lambda e: e.tensor_scalar(out=gq[:], in0=gq[:], scalar1=64 ** -0.5, scalar2=None, op0=ALU.mult), reads=[misc_b], writes=[misc_b])
        gsub = sb("gsub", [128, 1], F32)
        P.dma(lambda e: e.dma_start(out=gsub[:], in_=A.subln_g.rearrange("(p o) -> p o", o=1)), writes=[misc_b])
        P.op("dve", lambda e: e.tensor_scalar(out=gsub[:], in0=gsub[:], scalar1=1.0 - LAM_INIT, scalar2=None, op0=ALU.mult), reads=[misc_b], writes=[misc_b])
        lq = sb("lq", [1, 4, 64], F32); lpr = sb("lpr", [1, 2, 64], F32); lsum = sb("lsum", [1, 2], F32); lam1 = sb("lam1", [1, 2], F32)
        ones1 = sb("ones1a", [1, 128], F32); nlam = sb("nlam", [128, 1], F32); lam_b = P.buf("lam")
        for i, nm in enumerate(("lam_q1", "lam_k1", "lam_q2", "lam_k2")):
            P.dma(lambda e, i=i, nm=nm: e.dma_start(out=lq[:, i, :], in_=getattr(A, nm).rearrange("(o d) -> o d", o=1)), writes=[lam_b])
        P.op("pool", lambda e: e.memset(ones1[:], 1.0), writes=[lam_b])
        P.op("dve", lambda e: e.tensor_tensor(out=lpr[:, 0, :], in0=lq[:, 0, :], in1=lq[:, 1, :], op=ALU.mult), reads=[lam_b], writes=[lam_b])
        P.op("dve", lambda e: e.tensor_tensor(out=lpr[:, 1, :], in0=lq[:, 2, :], in1=lq[:, 3, :], op=ALU.mult), reads=[lam_b], writes=[lam_b])
        P.op("dve", lambda e: e.tensor_reduce(out=lsum[:], in_=lpr[:], axis=AX.X, op=ALU.add), reads=[lam_b], writes=[lam_b])
        P.op("act", lambda e: e.activation(out=lsum[:], in_=lsum[:], func=AF.Exp), reads=[lam_b], writes=[lam_b])
        P.op("dve", lambda e: e.tensor_tensor(out=lam1[:, 0:1], in0=lsum[:, 1:2], in1=lsum[:, 0:1], op=ALU.subtract), reads=[lam_b], writes=[lam_b])
        P.op("dve", lambda e: e.tensor_tensor(out=lam1[:, 1:2], in0=lsum[:, 1:2], in1=lsum[:, 0:1], op=ALU.subtract), reads=[lam_b], writes=[lam_b])
        P.op("dve", lambda e: e.tensor_scalar(out=lam1[:], in0=lam1[:], scalar1=-LAM_INIT, scalar2=None, op0=ALU.add), reads=[lam_b], writes=[lam_b])
        P.op("pe", lambda e: e.matmul(psQ[:, 0:2], lhsT=ones1[:], rhs=lam1[:], start=True, stop=True), reads=[lam_b], writes=[psQ_b])
        P.op("dve", lambda e: e.tensor_copy(out=nlam[:], in_=psQ[:, 0:1]), reads=[psQ_b], writes=[lam_b])
        bones = sb("bones", [128, 128], BF16)
        P.op("pool", lambda e: e.memset(bones[:], 0.0), writes=[misc_b])
        P.op("pool", lambda e: e.memset(bones[0:64, 0:64], 1.0), writes=[misc_b])
        P.op("pool", lambda e: e.memset(bones[64:128, 64:128], 1.0), writes=[misc_b])
        cmask = sb("cmask", [128, 128], BF16)
        P.op("pool", lambda e: e.memset(cmask[:], -30000.0), writes=[misc_b])
        P.op("pool", lambda e: e.affine_select(out=cmask[:], in_=cmask[:], pattern=[[-1, 128]], compare_op=ALU.is_gt, fill=0.0,
                                               base=0, channel_multiplier=1), reads=[misc_b], writes=[misc_b])
        kd = sb("kd", [128, nblk], F32)
        P.op("pool", lambda e: e.iota(kd[:], pattern=[[-128, nblk]], base=0, channel_multiplier=1, allow_small_or_imprecise_dtypes=True), writes=[misc_b])
        abias = sb("abias", [128, NH, nblk], F32)
        for h in range(NH):
            P.op("pool", lambda e, h=h: e.tensor_scalar(out=abias[:, h, :], in0=kd[:], scalar1=SLOPES[h], scalar2=None, op0=ALU.mult),
                 reads=[misc_b], writes=[misc_b])
        hnT = sb("hnTa", [128, 8, TB], BF16); hnT_b = P.buf("hnTa")
        wk = alloc_norm_work(P, sb, "a")
        uT = sb("uT", [128, 4, TB], BF16); uT_b = P.buf("uT")
        qz = [sb("qz%d" % c, [128, NH, TB], BF16) for c in range(2)]; qT_b = P.bufs(NH, "qT")
        P.op("pool", lambda e: e.memset(qz[0][:], 0.0), writes=qT_b)
        P.op("pool", lambda e: e.memset(qz[1][:], 0.0), writes=qT_b)
        sqb = sb("sqb", [128, TB], BF16); sqb_b = P.buf("sqb")
        lnv = sb("lnv", [128, TB], F32); lnv_b = P.buf("lnv")
        lnv3 = lnv[:, 0:384].rearrange("q (s e) -> q s e", s=3)
        ET = [sb("ET%d" % i, [128, TB], BF16) for i in range(3)]; ET_b = P.bufs(3, "ET")
        osb3 = sb("osb3", [128, 3, 128], F32); osb_b = P.buf("osb")
        onb3 = sb("onb3", [128, 3, 128], BF16); onb_b = P.buf("onb")
        rz3 = sb("rz3", [128, 4, 3], F32); rz_b = P.buf("rz")
        yaT = sb("yaTt", [128, NH, TB], BF16); yaT_b = P.buf("yaTt")
        ysT = sb("ysTt", [128, 4, TB], BF16); ysT_b = P.buf("ysTt")
        ygT = sb("ygT", [128, 4, TS], BF16); ygT_b = P.bufs(4, "ygT")
        sg = sb("sg", [128, TS], F32); sg_b = P.buf("sg")
        for ti in range(ntile):
            c0 = ti * TB
            for j in range(3):
                norm_tile(nc, P, C, A.h0, ti, 1, xs, xs_b, hnT, hnT_b, ps_t, ps_t_b, wk, blk0=ti * 3 + j, col0=j * 128)
            for m in range(12):
                r = m % 2
                for kt in range(8):
                    P.op("pe", lambda e, kt=kt, m=m, r=r: e.matmul(psP[r][:, 0:TB], lhsT=wa[:, kt, m * 128:(m + 1) * 128], rhs=hnT[:, kt, :],
                                                                   start=(kt == 0), stop=(kt == 7)), reads=[wa_b, hnT_b], writes=[psP_b[r]])
                if m < 4:
                    P.op("act", lambda e, m=m, r=r: e.copy(out=uT[:, m, :], in_=psP[r][:, 0:TB]), reads=[psP_b[r]], writes=[uT_b])
                    continue
                h = (m - 4) % 4
                isq = m < 8
                P.op("act", lambda e, r=r: e.activation(out=sqb[:], in_=psP[r][:, 0:TB], func=AF.Square), reads=[psP_b[r]], writes=[sqb_b])
                P.op("pe", lambda e: e.matmul(psQ[:, 0:TB], lhsT=bones[:], rhs=sqb[:], start=True, stop=True), reads=[misc_b, sqb_b], writes=[psQ_b])
                P.op("act", lambda e: e.activation(out=lnv[:], in_=psQ[:, 0:TB], func=AF.Ln, bias=C.epsc[:], scale=1.0 / 64),
                     reads=[psQ_b, C.b_eps], writes=[lnv_b])
                P.op("act", lambda e: e.activation(out=lnv[:], in_=lnv[:], func=AF.Exp, scale=-0.5), reads=[lnv_b], writes=[lnv_b])
                if isq:
                    for c in range(2):
                        P.op("dve", lambda e, h=h, r=r, c=c: e.scalar_tensor_tensor(out=qz[c][64 * c:64 * c + 64, h, :], in0=psP[r][64 * c:64 * c + 64, 0:TB],
                                                                                   scalar=gq[64 * c:64 * c + 64, 0:1], in1=lnv[64 * c:64 * c + 64, :],
                                                                                   op0=ALU.mult, op1=ALU.mult),
                             reads=[psP_b[r], lnv_b, misc_b], writes=[qT_b[h]])
                else:
                    P.op("dve", lambda e, h=h, r=r, c0=c0: e.scalar_tensor_tensor(out=khist[:, h, c0:c0 + TB], in0=psP[r][:, 0:TB], scalar=gk[:, 0:1],
                                                                                 in1=lnv[:], op0=ALU.mult, op1=ALU.mult),
                         reads=[psP_b[r], lnv_b, misc_b], writes=[kh_b[ti]])
            for j in range(3):
                r = j % 2
                for kt in range(8):
                    P.op("pe", lambda e, kt=kt, j=j, r=r: e.matmul(psP[r][:], lhsT=hnT[:, kt, j * 128:(j + 1) * 128], rhs=wa[:, kt, 1536:2048],
                                                                   start=(kt == 0), stop=(kt == 7)), reads=[wa_b, hnT_b], writes=[psP_b[r]])
                P.op("act", lambda e, j=j, r=r, ti=ti: e.copy(out=vhist[:, ti * 3 + j, :, 0:128], in_=psP[r][:].rearrange("q (h e) -> q h e", h=NH)),
                     reads=[psP_b[r]], writes=[vh_b[ti]])
            NCH = (TB // TS) * 4

            def s5_X(k):
                sub, ct = divmod(k, 4); cs_ = sub * TS
                for i in range(4):
                    a = 4 * ct + i
                    P.op("pe", lambda e, a=a, i=i, ct=ct, cs_=cs_: e.matmul(psP[0][:, i * TS:(i + 1) * TS], lhsT=S.BBre[:, a, :], rhs=uT[:, ct, cs_:cs_ + TS],
                                                                             start=True, stop=True), reads=[S.bBB, uT_b], writes=[psP_b[0]])
                    P.op("pe", lambda e, a=a, i=i, ct=ct, cs_=cs_: e.matmul(psP[1][:, i * TS:(i + 1) * TS], lhsT=S.BBim[:, a, :], rhs=uT[:, ct, cs_:cs_ + TS],
                                                                             start=True, stop=True), reads=[S.bBB, uT_b], writes=[psP_b[1]])

            def s5_rot_in(k):
                sub, ct = divmod(k, 4); W = WS[k % 2]
                Xr = psP[0][:].rearrange("q (i t) -> q i t", i=4); Xi = psP[1][:].rearrange("q (i t) -> q i t", i=4)
                cT = S.cosT[:, 4 * ct:4 * ct + 4, :]; sT = S.sinT[:, 4 * ct:4 * ct + 4, :]
                P.op("dve", lambda e: e.tensor_tensor(out=W.xtr[:], in0=Xr, in1=cT, op=ALU.mult), reads=[psP_b[0], S.bT], writes=[W.xtr_b])
                P.op("dve", lambda e: e.tensor_tensor(out=W.tm1[:], in0=Xi, in1=sT, op=ALU.mult), reads=[psP_b[1], S.bT], writes=[W.tm1_b])
                P.op("dve", lambda e: e.tensor_tensor(out=W.xti[:], in0=Xi, in1=cT, op=ALU.mult), reads=[psP_b[1], S.bT], writes=[W.xti_b])
                P.op("dve", lambda e: e.tensor_tensor(out=W.tm2[:], in0=Xr, in1=sT, op=ALU.mult), reads=[psP_b[0], S.bT], writes=[W.tm2_b])

            def s5_scan(k):
                sub, ct = divmod(k, 4); W = WS[k % 2]
                cT = S.cosT[:, 4 * ct:4 * ct + 4, :]; sT = S.sinT[:, 4 * ct:4 * ct + 4, :]
                P.op("dve", lambda e: e.tensor_tensor(out=W.xtr[:], in0=W.xtr[:], in1=W.tm1[:], op=ALU.add), reads=[W.xtr_b, W.tm1_b], writes=[W.xtr_b])
                P.op("dve", lambda e: e.tensor_tensor(out=W.xti[:], in0=W.xti[:], in1=W.tm2[:], op=ALU.subtract), reads=[W.xti_b, W.tm2_b], writes=[W.xti_b])
                for i in range(4):
                    a = 4 * ct + i
                    P.op("dve", lambda e, a=a, i=i: e.tensor_tensor_scan(out=W.wr[:, i, :], data0=S.R[:, a, :], data1=W.xtr[:, i, :],
                                                                         initial=S.init_re[:, a:a + 1], op0=ALU.mult, op1=ALU.add),
                         reads=[S.bT, W.xtr_b, S.b_init[ct]], writes=[W.wr_b])
                    P.op("dve", lambda e, a=a, i=i: e.tensor_tensor_scan(out=W.wi[:, i, :], data0=S.R[:, a, :], data1=W.xti[:, i, :],
                                                                         initial=S.init_im[:, a:a + 1], op0=ALU.mult, op1=ALU.add),
                         reads=[S.bT, W.xti_b, S.b_init[ct]], writes=[W.wi_b])
                P.op("pool", lambda e: e.tensor_tensor(out=W.tm1[:], in0=W.wi[:], in1=cT, op=ALU.mult), reads=[W.wi_b, S.bT], writes=[W.tm1_b])
                P.op("pool", lambda e: e.tensor_tensor(out=W.tm2[:], in0=W.wr[:], in1=sT, op=ALU.mult), reads=[W.wr_b, S.bT], writes=[W.tm2_b])
                P.op("dve", lambda e: e.tensor_tensor(out=W.xtr[:], in0=W.wr[:], in1=cT, op=ALU.mult), reads=[W.wr_b, S.bT], writes=[W.xtr_b])
                P.op("dve", lambda e: e.tensor_tensor(out=W.xti[:], in0=W.wi[:], in1=sT, op=ALU.mult), reads=[W.wi_b, S.bT], writes=[W.xti_b])
                a0 = 4 * ct
                wl_r = W.wr[:, :, TS - 1]; wl_i = W.wi[:, :, TS - 1]
                P.op("dve", lambda e: e.tensor_tensor(out=W.c4[:], in0=wl_r, in1=S.cN[:, a0:a0 + 4], op=ALU.mult), reads=[W.wr_b, S.bq], writes=[W.c4_b])
                P.op("dve", lambda e: e.tensor_tensor(out=W.c4b[:], in0=wl_i, in1=S.sN[:, a0:a0 + 4], op=ALU.mult), reads=[W.wi_b, S.bq], writes=[W.c4b_b])
                P.op("dve", lambda e: e.tensor_tensor(out=S.init_re[:, a0:a0 + 4], in0=W.c4[:], in1=W.c4b[:], op=ALU.subtract), reads=[W.c4_b, W.c4b_b], writes=[S.b_init[ct]])
                P.op("dve", lambda e: e.tensor_tensor(out=W.c4[:], in0=wl_r, in1=S.sN[:, a0:a0 + 4], op=ALU.mult), reads=[W.wr_b, S.bq], writes=[W.c4_b])
                P.op("dve", lambda e: e.tensor_tensor(out=W.c4b[:], in0=wl_i, in1=S.cN[:, a0:a0 + 4], op=ALU.mult), reads=[W.wi_b, S.bq], writes=[W.c4b_b])
                P.op("dve", lambda e: e.tensor_tensor(out=S.init_im[:, a0:a0 + 4], in0=W.c4[:], in1=W.c4b[:], op=ALU.add), reads=[W.c4_b, W.c4b_b], writes=[S.b_init[ct]])
                P.op("dve", lambda e: e.tensor_tensor(out=W.Sr[:], in0=W.xtr[:], in1=W.xti[:], op=ALU.subtract), reads=[W.xtr_b, W.xti_b], writes=[W.Sr_b])
                P.op("dve", lambda e: e.tensor_tensor(out=W.Si[:], in0=W.tm1[:], in1=W.tm2[:], op=ALU.add), reads=[W.tm1_b, W.tm2_b], writes=[W.Si_b])

            def s5_Y(k):
                sub, ct = divmod(k, 4); W = WS[k % 2]; q = k % 2
                for i in range(4):
                    a = 4 * ct + i
                    P.op("pe", lambda e, a=a, i=i: e.matmul(psQ[32 * i:32 * i + 32, 0:TS], lhsT=S.CWre[:, a, :], rhs=W.Sr[:, i, :], start=True, stop=False,
                                                            tile_position=(0, 32 * i), skip_group_check=True), reads=[S.bCW, W.Sr_b], writes=[psQr_b[q]])
                    P.op("pe", lambda e, a=a, i=i: e.matmul(psQ[32 * i:32 * i + 32, 0:TS], lhsT=S.CWim[:, a, :], rhs=W.Si[:, i, :], start=False, stop=True,
                                                            tile_position=(0, 32 * i), skip_group_check=True), reads=[S.bCW, W.Si_b], writes=[psQr_b[q]])

            def s5_gelu(k):
                sub, ct = divmod(k, 4); W = WS[k % 2]; q = k % 2; cs_ = sub * TS
                P.op("dve", lambda e: e.scalar_tensor_tensor(out=W.yv[:], in0=uT[:, ct, cs_:cs_ + TS], scalar=S.dcol[:, ct:ct + 1], in1=psQ[:, 0:TS],
                                                             op0=ALU.mult, op1=ALU.add), reads=[uT_b, S.b_d, psQr_b[q]], writes=[W.yv_b])
                P.op("dve", lambda e: e.tensor_tensor(out=W.yq[:], in0=W.yv[:], in1=W.yv[:], op=ALU.mult), reads=[W.yv_b], writes=[W.yq_b])
                P.op("dve", lambda e: e.scalar_tensor_tensor(out=W.yq[:], in0=W.yq[:], scalar=1.0 / GC, in1=W.yv[:], op0=ALU.add, op1=ALU.mult),
                     reads=[W.yq_b, W.yv_b], writes=[W.yq_b])
                P.op("act", lambda e: e.activation(out=W.yq[:], in_=W.yq[:], func=AF.Tanh, scale=GK * GC), reads=[W.yq_b], writes=[W.yq_b])
                P.op("dve", lambda e: e.scalar_tensor_tensor(out=ygT[:, ct, :], in0=W.yq[:], scalar=1.0, in1=W.yv[:], op0=ALU.add, op1=ALU.mult),
                     reads=[W.yq_b, W.yv_b], writes=[ygT_b[ct]])

            def emit_glu(sub, ti=ti):
                cs_ = sub * TS
                for m in range(4):
                    for kt in range(4):
                        P.op("pe", lambda e, kt=kt, m=m: e.matmul(psQ[:, 0:TS], lhsT=glu[:, kt, m * 128:(m + 1) * 128], rhs=ygT[:, kt, :],
                                                                  start=(kt == 0), stop=(kt == 3), skip_group_check=True), reads=[glu_b, ygT_b[kt]], writes=[psQr_b[2]])
                    P.op("act", lambda e, m=m: e.activation(out=sg[:], in_=psQ[:, 0:TS], func=AF.Tanh, bias=glub[:, m:m + 1], scale=0.5),
                         reads=[psQr_b[2], misc_b], writes=[sg_b])
                    P.op("dve", lambda e, m=m, cs_=cs_: e.scalar_tensor_tensor(out=ysT[:, m, cs_:cs_ + TS], in0=sg[:], scalar=1.0, in1=ygT[:, m, :],
                                                                              op0=ALU.add, op1=ALU.mult), reads=[ygT_b[m], sg_b], writes=[ysT_b])
            units = [(kb, c) for kb in range(3 * ti + 3) for c in range(2)]

            def emit_scores(h, n, ti=ti, units=units):
                kb, c = units[n]
                n0 = max(0, kb - 3 * ti)
                r = n % 2
                ncol = TB - n0 * 128
                diag = kb >= 3 * ti
                P.op("pe", lambda e, kb=kb, c=c, n0=n0, r=r, ncol=ncol, diag=diag: e.matmul(
                    psS[r][:, 0:ncol], lhsT=khist[:, h, kb * 128:(kb + 1) * 128],
                    rhs=qz[c][:, h, n0 * 128:TB], start=True, stop=(not diag)),
                    reads=[kh_b[kb // 3], qT_b[h]], writes=[psS_b[r]])
                if diag:
                    P.op("pe", lambda e, r=r: e.matmul(psS[r][:, 0:128], lhsT=C.ident[:], rhs=cmask[:], start=False, stop=True),
                         reads=[C.b_ident, misc_b], writes=[psS_b[r]])

            def emit_att(h, n_lo, n_hi, ti=ti, units=units):
                emit_scores(h, n_lo)
                for n in range(n_lo, n_hi):
                    if n + 1 < n_hi:
                        emit_scores(h, n + 1)
                    kb, c = units[n]
                    n0 = max(0, kb - 3 * ti)
                    r = n % 2; eb = n % 3
                    ncol = TB - n0 * 128
                    dl2 = 3 * ti + 2 - kb
                    P.op("act", lambda e, h=h, dl2=dl2, r=r, eb=eb, ncol=ncol: e.activation(
                        out=ET[eb][:, 0:ncol], in_=psS[r][:, 0:ncol], func=AF.Exp, bias=abias[:, h, dl2:dl2 + 1], scale=1.0),
                        reads=[psS_b[r], misc_b], writes=[ET_b[eb]])
                    for sbk in range(n0, 3):
                        o0 = (sbk - n0) * 128
                        last = (kb == 3 * ti + sbk)
                        P.op("pe", lambda e, h=h, kb=kb, c=c, sbk=sbk, eb=eb, o0=o0, st_=(kb == 0 and sbk == 0), last=last: e.matmul(
                            psAcc[c][:, sbk, :], lhsT=ET[eb][:, o0:o0 + 128], rhs=vhist[:, kb, h, :], start=st_, stop=last,
                            skip_group_check=True), reads=[ET_b[eb], vh_b[kb // 3]], writes=[psAcc_b[c]])
                if n_hi < len(units):
                    return
                accS = [xs[:, 0, 387 * c_:387 * (c_ + 1)].rearrange("q (s e) -> q s e", s=3) for c_ in range(2)]
                P.op("dve", lambda e: e.tensor_copy(out=accS[0], in_=psAcc[0]), reads=[psAcc_b[0]], writes=[xs_b[0]])
                P.op("act", lambda e: e.copy(out=accS[1], in_=psAcc[1]), reads=[psAcc_b[1]], writes=[xs_b[0]])
                bc = lambda col: rz3[:, col, :].unsqueeze(2).broadcast_to([128, 3, 128])
                P.op("dve", lambda e: e.reciprocal(out=rz3[:, 0, :], in_=accS[0][:, :, 128]), reads=[xs_b[0]], writes=[rz_b])
                P.op("dve", lambda e: e.reciprocal(out=rz3[:, 1, :], in_=accS[1][:, :, 128]), reads=[xs_b[0]], writes=[rz_b])
                P.op("dve", lambda e: e.tensor_scalar(out=rz3[:, 1, :], in0=rz3[:, 1, :], scalar1=nlam[:, 0:1], scalar2=None, op0=ALU.mult),
                     reads=[rz_b, lam_b], writes=[rz_b])
                P.op("dve", lambda e: e.tensor_tensor(out=osb3[:], in0=accS[0][:, :, 0:128], in1=bc(0), op=ALU.mult), reads=[xs_b[0], rz_b], writes=[osb_b])
                P.op("dve", lambda e: e.tensor_tensor(out=lnv3, in0=accS[1][:, :, 0:128], in1=bc(1), op=ALU.mult), reads=[xs_b[0], rz_b], writes=[lnv_b])
                P.op("dve", lambda e: e.tensor_tensor(out=osb3[:], in0=osb3[:], in1=lnv3, op=ALU.add), reads=[osb_b, lnv_b], writes=[osb_b])
                P.op("dve", lambda e: e.tensor_tensor(out=lnv3, in0=osb3[:], in1=osb3[:], op=ALU.mult), reads=[osb_b], writes=[lnv_b])
                P.op("dve", lambda e: e.tensor_reduce(out=rz3[:, 2, :], in_=lnv3, axis=AX.X, op=ALU.add), reads=[lnv_b], writes=[rz_b])
                P.op("act", lambda e: e.activation(out=rz3[:, 2, :], in_=rz3[:, 2, :], func=AF.Sqrt, bias=C.epsc[:], scale=1.0 / 128), reads=[rz_b, C.b_eps], writes=[rz_b])
                P.op("dve", lambda e: e.reciprocal(out=rz3[:, 3, :], in_=rz3[:, 2, :]), reads=[rz_b], writes=[rz_b])
                P.op("dve", lambda e: e.tensor_tensor(out=onb3[:], in0=osb3[:], in1=bc(3), op=ALU.mult), reads=[osb_b, rz_b], writes=[onb_b])
                for sbk in range(3):
                    P.op("pe", lambda e, sbk=sbk: e.transpose(out=ps_t[:, sbk, :], in_=onb3[:, sbk, :], identity=C.ident[:]), reads=[onb_b, C.b_ident], writes=[ps_t_b])
                P.op("dve", lambda e, h=h: e.tensor_scalar(out=yaT[:, h, :], in0=ps_t[:, 0:3, :], scalar1=gsub[:, 0:1], scalar2=None, op0=ALU.mult),
                     reads=[ps_t_b, misc_b], writes=[yaT_b])
            att_items = []
            nu = len(units)
            cuts = [0, nu // 3, (2 * nu) // 3, nu]
            for h in range(NH):
                for part in range(3):
                    att_items.append(lambda h=h, part=part: emit_att(h, cuts[part], cuts[part + 1]))
            s5_X(0)
            for k in range(NCH + 2):
                if k < NCH:
                    s5_rot_in(k)
                    if k + 1 < NCH:
                        s5_X(k + 1)
                    s5_scan(k)
                if k < len(att_items):
                    att_items[k]()
                if 2 <= k <= NCH + 1:
                    s5_gelu(k - 2)
                    if (k - 2) % 4 == 3:
                        emit_glu((k - 2) // 4)
                if 1 <= k <= NCH:
                    s5_Y(k - 1)
            for k in range(NCH + 2, len(att_items)):
                att_items[k]()
            P.dma(lambda e, c0=c0: e.dma_start(out=A.ysT[:, c0:c0 + TB].rearrange("(kt p) t -> p kt t", p=128), in_=ysT[:]),
                  reads=[ysT_b], writes=[A.ysT_b])
            P.dma(lambda e, c0=c0: e.dma_start(out=A.yaT[:, c0:c0 + TB].rearrange("(kt p) t -> p kt t", p=128), in_=yaT[:]),
                  reads=[yaT_b], writes=[A.yaT_b])
        P.barrier()
```
